# Optimizing a Trainium2 kernel written in Bass

```python
import math
import jax, jax.numpy as jnp
from jax import lax
import numpy as np


D_MODEL = 2048
BATCH = 4
SEQ = 4096
DEPTH = 1

MEM_LEN = 256
ATT_HEADS = 8
ATT_HEAD_DIM = 128
ATT_WIDTH = ATT_HEADS * ATT_HEAD_DIM
MOBA_BLOCK = 256
MOBA_TOPK = 3
MOBA_Q_CHUNK = 32
REL_BUCKETS = 32
REL_MAX_DIST = 128
GDN_HEADS = 8
GDN_HEAD_DIM = 128
GDN_WIDTH = GDN_HEADS * GDN_HEAD_DIM
GDN_CONV = 4
GDN_CHUNK = 64
XATT_HEADS = 4
XATT_HEAD_DIM = 128
XATT_WIDTH = XATT_HEADS * XATT_HEAD_DIM
D_FF = 5632
NORM_EPS = 1e-6
IN_SPLITS = (3 * ATT_WIDTH, 3 * GDN_WIDTH, GDN_WIDTH, GDN_HEADS, GDN_HEADS, D_MODEL, D_MODEL)
IN_WIDTH = sum(IN_SPLITS)

kernel_name = 'hybrid_moba_gdn_macaron_block'


def rms_norm(x, gain):
    xf = x.astype(jnp.float32)
    y = xf * lax.rsqrt(jnp.mean(xf * xf, axis=-1, keepdims=True) + NORM_EPS)
    return (y * gain.astype(jnp.float32)).astype(x.dtype)


def swiglu(x, w_gate, w_up, w_down):
    return (jax.nn.silu(x @ w_gate) * (x @ w_up)) @ w_down


def t5_bucket(rel):
    n = jnp.maximum(rel, 0)
    max_exact = REL_BUCKETS // 2
    nf = jnp.maximum(n, 1).astype(jnp.float32)
    large = max_exact + (jnp.log(nf / max_exact) / math.log(REL_MAX_DIST / max_exact)
                         * (REL_BUCKETS - max_exact)).astype(jnp.int32)
    large = jnp.minimum(large, REL_BUCKETS - 1)
    return jnp.where(n < max_exact, n, large)


def moba_attention(q, k, v, rel_bias):
    B, S, H, D = q.shape
    nb = -(-S // MOBA_BLOCK)
    s_pad = nb * MOBA_BLOCK
    k_sel_n = min(MOBA_TOPK, nb)
    n_sel = k_sel_n * MOBA_BLOCK
    q = (q * D ** -0.5).transpose(0, 2, 1, 3)
    pad = ((0, 0), (0, 0), (0, s_pad - S), (0, 0))
    kb = jnp.pad(k.transpose(0, 2, 1, 3), pad).reshape(B, H, nb, MOBA_BLOCK, D)
    vb = jnp.pad(v.transpose(0, 2, 1, 3), pad).reshape(B, H, nb, MOBA_BLOCK, D)
    k_mean = jnp.mean(kb, axis=3, dtype=jnp.float32)
    bias_t = rel_bias.astype(jnp.float32).T
    bi = jnp.arange(B)[:, None, None, None]
    hi = jnp.arange(H)[None, :, None, None]
    offs = jnp.arange(MOBA_BLOCK)

    def chunk(c):
        start = c * MOBA_Q_CHUNK
        q_c = lax.dynamic_slice_in_dim(q, start, MOBA_Q_CHUNK, axis=2)
        pos = start + jnp.arange(MOBA_Q_CHUNK)
        own = start // MOBA_BLOCK
        score = jnp.einsum('bhqd,bhnd->bhqn', q_c.astype(jnp.float32), k_mean)
        score = jnp.where(jnp.arange(nb) < own, score, -jnp.inf)
        _, sel = lax.top_k(score, k_sel_n)
        sel_ok = sel < own
        k_g = kb[bi, hi, sel]
        v_g = vb[bi, hi, sel]
        k_o = lax.dynamic_index_in_dim(kb, own, axis=2, keepdims=False)
        v_o = lax.dynamic_index_in_dim(vb, own, axis=2, keepdims=False)
        s_sel = jnp.einsum('bhqd,bhqnkd->bhqnk', q_c, k_g).astype(jnp.float32)
        s_own = jnp.einsum('bhqd,bhkd->bhqk', q_c, k_o).astype(jnp.float32)
        kpos_sel = sel[..., None] * MOBA_BLOCK + offs
        kpos_own = own * MOBA_BLOCK + offs
        b_sel = bias_t[hi[..., None], t5_bucket(pos[:, None, None] - kpos_sel)]
        b_own = bias_t[jnp.arange(H)[:, None, None], t5_bucket(pos[:, None] - kpos_own)[None]]
        s_sel = jnp.where(sel_ok[..., None], s_sel + b_sel, -jnp.inf)
        s_own = jnp.where(kpos_own <= pos[:, None], s_own + b_own, -jnp.inf)
        logits = jnp.concatenate([s_sel.reshape(B, H, MOBA_Q_CHUNK, n_sel), s_own], axis=-1)
        p = jax.nn.softmax(logits, axis=-1).astype(v.dtype)
        p_sel = p[..., :n_sel].reshape(B, H, MOBA_Q_CHUNK, k_sel_n, MOBA_BLOCK)
        return (jnp.einsum('bhqnk,bhqnkd->bhqd', p_sel, v_g)
                + jnp.einsum('bhqk,bhkd->bhqd', p[..., n_sel:], v_o))

    out = lax.map(chunk, jnp.arange(S // MOBA_Q_CHUNK))
    return out.transpose(1, 0, 3, 2, 4).reshape(B, S, H * D)


def chunk_gated_delta_rule(q, k, v, g, beta):
    B, S, H, Dk = q.shape
    Dv = v.shape[-1]
    C = GDN_CHUNK
    N = S // C
    to_chunks = lambda t: t.reshape(B, N, C, H, -1).transpose(0, 3, 1, 2, 4)
    q, k, v = to_chunks(q), to_chunks(k), to_chunks(v)
    g = g.reshape(B, N, C, H).transpose(0, 3, 1, 2)
    beta = beta.reshape(B, N, C, H).transpose(0, 3, 1, 2)
    G = jnp.cumsum(g, axis=-1)
    idx = jnp.arange(C)
    incl = idx[:, None] >= idx[None, :]
    strict = idx[:, None] > idx[None, :]
    decay = jnp.exp(jnp.where(incl, G[..., :, None] - G[..., None, :], -jnp.inf))
    kb = k * beta[..., None]
    m = jnp.where(strict, jnp.einsum('bhncd,bhnjd->bhncj', kb, k) * decay, 0.0)
    rhs = jnp.concatenate([v * beta[..., None], kb * jnp.exp(G)[..., None]], axis=-1)
    sol = lax.linalg.triangular_solve(m + jnp.eye(C, dtype=m.dtype), rhs,
                                      left_side=True, lower=True, unit_diagonal=True)
    u, w = sol[..., :Dv], sol[..., Dv:]
    attn = jnp.einsum('bhncd,bhnjd->bhncj', q, k) * decay
    q_dec = q * jnp.exp(G)[..., None]
    k_dec = k * jnp.exp(G[..., -1:] - G)[..., None]
    chunk_decay = jnp.exp(G[..., -1])
    xs = tuple(jnp.moveaxis(t, 2, 0) for t in (u, w, attn, q_dec, k_dec, chunk_decay))

    def step(state, inp):
        u_n, w_n, a_n, qd_n, kd_n, cd_n = inp
        v_new = u_n - jnp.einsum('bhck,bhkv->bhcv', w_n, state)
        o_n = (jnp.einsum('bhck,bhkv->bhcv', qd_n, state)
               + jnp.einsum('bhcj,bhjv->bhcv', a_n, v_new))
        state = state * cd_n[..., None, None] + jnp.einsum('bhck,bhcv->bhkv', kd_n, v_new)
        return state, o_n

    state0 = jnp.zeros((B, H, Dk, Dv), jnp.float32)
    _, o = lax.scan(step, state0, xs)
    return o.transpose(1, 0, 3, 2, 4).reshape(B, S, H, Dv)


def l2_normalize(t):
    return t * lax.rsqrt(jnp.sum(t * t, axis=-1, keepdims=True) + NORM_EPS)


def gated_deltanet(qkv_raw, z, b_logit, a_logit, conv_w, a_log, dt_bias, out_norm):
    dtype = qkv_raw.dtype
    B, S, Cn = qkv_raw.shape
    qkv = lax.conv_general_dilated(qkv_raw, conv_w[:, None, :].astype(dtype), (1,),
                                   [(GDN_CONV - 1, 0)], dimension_numbers=('NWC', 'WIO', 'NWC'),
                                   feature_group_count=Cn)
    qkv = jax.nn.silu(qkv).astype(jnp.float32).reshape(B, S, 3, GDN_HEADS, GDN_HEAD_DIM)
    q = l2_normalize(qkv[:, :, 0]) * GDN_HEAD_DIM ** -0.5
    k = l2_normalize(qkv[:, :, 1])
    v = qkv[:, :, 2]
    beta = jax.nn.sigmoid(b_logit.astype(jnp.float32))
    g = -jnp.exp(a_log.astype(jnp.float32)) * jax.nn.softplus(
        a_logit.astype(jnp.float32) + dt_bias.astype(jnp.float32))
    o = chunk_gated_delta_rule(q, k, v, g, beta)
    o = o * lax.rsqrt(jnp.mean(o * o, axis=-1, keepdims=True) + NORM_EPS)
    o = o * out_norm.astype(jnp.float32) * jax.nn.silu(
        z.astype(jnp.float32).reshape(B, S, GDN_HEADS, GDN_HEAD_DIM))
    return o.reshape(B, S, GDN_WIDTH).astype(dtype)


def memory_cross_attention(h_n, mem_n, wq, wkv, wo):
    B, S, _ = h_n.shape
    M = mem_n.shape[1]
    q = (h_n @ wq).reshape(B, S, XATT_HEADS, XATT_HEAD_DIM)
    kv = (mem_n @ wkv).reshape(B, M, 2, XATT_HEADS, XATT_HEAD_DIM)
    s = jnp.einsum('bshd,bmhd->bhsm', q, kv[:, :, 0]).astype(jnp.float32) * XATT_HEAD_DIM ** -0.5
    p = jax.nn.softmax(s, axis=-1).astype(kv.dtype)
    o = jnp.einsum('bhsm,bmhd->bshd', p, kv[:, :, 1]).reshape(B, S, XATT_WIDTH)
    return o @ wo


def setup_inputs(seed: int = 0) -> dict:
    key = jax.random.key(seed)
    ks = jax.random.split(key, 28)
    f32 = jnp.float32

    def dense(k, fan_in, fan_out):
        return jax.random.normal(k, (DEPTH, fan_in, fan_out), f32) * fan_in ** -0.5

    def gain(k, n):
        return 1.0 + 0.01 * jax.random.normal(k, (DEPTH, n), f32)

    dt = jnp.exp(jax.random.uniform(ks[10], (DEPTH, GDN_HEADS), f32,
                                    minval=math.log(1e-3), maxval=math.log(1e-1)))
    return {
        'x': jax.random.normal(ks[0], (BATCH, SEQ, D_MODEL), f32),
        'mem': jax.random.normal(ks[1], (BATCH, MEM_LEN, D_MODEL), f32),
        'ffn1_norm': gain(ks[2], D_MODEL),
        'ffn1_w_gate': dense(ks[3], D_MODEL, D_FF),
        'ffn1_w_up': dense(ks[4], D_MODEL, D_FF),
        'ffn1_w_down': dense(ks[5], D_FF, D_MODEL),
        'mix_norm': gain(ks[6], D_MODEL),
        'w_in': dense(ks[7], D_MODEL, IN_WIDTH),
        'gdn_conv': jax.random.normal(ks[8], (DEPTH, GDN_CONV, 3 * GDN_WIDTH), f32) * GDN_CONV ** -0.5,
        'gdn_a_log': jnp.log(jax.random.uniform(ks[9], (DEPTH, GDN_HEADS), f32, minval=1.0, maxval=16.0)),
        'gdn_dt_bias': dt + jnp.log(-jnp.expm1(-dt)),
        'gdn_out_norm': gain(ks[11], GDN_HEAD_DIM),
        'rel_bias': 0.5 * jax.random.normal(ks[12], (REL_BUCKETS, ATT_HEADS), f32),
        'w_branch_attn': dense(ks[13], ATT_WIDTH, D_MODEL),
        'w_branch_delta': dense(ks[14], GDN_WIDTH, D_MODEL),
        'w_out': dense(ks[15], D_MODEL, D_MODEL),
        'cross_norm': gain(ks[16], D_MODEL),
        'mem_norm': gain(ks[17], D_MODEL),
        'cross_wq': dense(ks[18], D_MODEL, XATT_WIDTH),
        'cross_wkv': dense(ks[19], D_MODEL, 2 * XATT_WIDTH),
        'cross_wo': dense(ks[20], XATT_WIDTH, D_MODEL),
        'ffn2_norm': gain(ks[21], D_MODEL),
        'ffn2_w_gate': dense(ks[22], D_MODEL, D_FF),
        'ffn2_w_up': dense(ks[23], D_MODEL, D_FF),
        'ffn2_w_down': dense(ks[24], D_FF, D_MODEL),
        'final_norm': 1.0 + 0.01 * jax.random.normal(ks[25], (D_MODEL,), f32),
    }


def reference(x, mem, ffn1_norm, ffn1_w_gate, ffn1_w_up, ffn1_w_down, mix_norm, w_in,
              gdn_conv, gdn_a_log, gdn_dt_bias, gdn_out_norm, rel_bias, w_branch_attn,
              w_branch_delta, w_out, cross_norm, mem_norm, cross_wq, cross_wkv, cross_wo,
              ffn2_norm, ffn2_w_gate, ffn2_w_up, ffn2_w_down, final_norm):
    B, S, _ = x.shape
    split_at = np.cumsum(IN_SPLITS)[:-1].tolist()
    h = x
    for l in range(DEPTH):
        h = h + 0.5 * swiglu(rms_norm(h, ffn1_norm[l]), ffn1_w_gate[l], ffn1_w_up[l], ffn1_w_down[l])
        u = rms_norm(h, mix_norm[l])
        att_qkv, gdn_qkv, gdn_z, gdn_b, gdn_a, gate_a, gate_b = jnp.split(u @ w_in[l], split_at, axis=-1)
        att_qkv = att_qkv.reshape(B, S, 3, ATT_HEADS, ATT_HEAD_DIM)
        y_att = moba_attention(att_qkv[:, :, 0], att_qkv[:, :, 1], att_qkv[:, :, 2], rel_bias) @ w_branch_attn[l]
        y_del = gated_deltanet(gdn_qkv, gdn_z, gdn_b, gdn_a, gdn_conv[l], gdn_a_log[l],
                               gdn_dt_bias[l], gdn_out_norm[l]) @ w_branch_delta[l]
        merged = jax.nn.sigmoid(gate_a) * y_att + jax.nn.sigmoid(gate_b) * y_del
        h = h + merged @ w_out[l]
        h = h + memory_cross_attention(rms_norm(h, cross_norm[l]), rms_norm(mem, mem_norm[l]),
                                       cross_wq[l], cross_wkv[l], cross_wo[l])
        h = h + 0.5 * swiglu(rms_norm(h, ffn2_norm[l]), ffn2_w_gate[l], ffn2_w_up[l], ffn2_w_down[l])
    return rms_norm(h, final_norm)
```

```python
import math
import numpy as np
import ml_dtypes
import concourse.bass as bass
import concourse.mybir as mybir
from concourse.bass_utils import run_bass_kernel_spmd

F32 = mybir.dt.float32
BF16 = mybir.dt.bfloat16
I32 = mybir.dt.int32
AF = mybir.ActivationFunctionType
ALU = mybir.AluOpType
AX = mybir.AxisListType

D = 2048
DC = 16
DFF = 5632
FC = 44
NTOK = 4096
NOWN = 2048
TT = 512
NTILE = NTOK // TT
EPS = 1e-6
BIG = 30000.0
NDS = 16


class Prog:
    ENG = ("pe", "dve", "act", "pool", "sp")

    def __init__(self):
        self.streams = {e: [] for e in self.ENG}
        self.cnt = {e: 0 for e in self.ENG}
        self.seen = {e: {} for e in self.ENG}
        self.state = {}
        self.dma_q = {}
        self.total = 0
        self.limit = None
        self.marks = []

    def _wait(self, eng, tok):
        s, v = tok
        if self.seen[eng].get(s, 0) >= v:
            return
        self.seen[eng][s] = v
        self.streams[eng].append(("wait", s, v))

    def _deps(self, reads, writes):
        deps = []
        for k in reads:
            st = self.state.get(k)
            if st is not None and st[0] is not None:
                deps.append(st[0])
        for k in writes:
            st = self.state.get(k)
            if st is not None:
                if st[0] is not None:
                    deps.append(st[0])
                deps.extend(st[1])
        return deps

    def _commit(self, tok, reads, writes):
        for k in reads:
            st = self.state.setdefault(k, [None, []])
            st[1] = [t for t in st[1] if t[0] != tok[0]] + [tok]
        for k in writes:
            self.state[k] = [tok, []]

    def _maxdeps(self, reads, writes):
        best = {}
        for s_, v in self._deps(reads, writes):
            if v > best.get(s_, 0):
                best[s_] = v
        return list(best.items())

    def op(self, eng, fn, reads=(), writes=()):
        self.total += 1
        if self.limit is not None and self.total > self.limit:
            return
        for tok in self._maxdeps(reads, writes):
            if tok[0] == eng and eng == "pe":
                continue
            self._wait(eng, tok)
        self.cnt[eng] += 1
        tok = (eng, self.cnt[eng])
        self.streams[eng].append(("op", fn, eng, 1))
        self._commit(tok, reads, writes)

    def dma(self, q, fn, reads=(), writes=()):
        self.total += 1
        if self.limit is not None and self.total > self.limit:
            return
        i = self.dma_q.get(q, 0)
        self.dma_q[q] = i + 1
        s = "dma_%s%d" % (q, i % NDS)
        v = 16 * (i // NDS + 1)
        if i >= NDS:
            self._wait(q, (s, v - 16))
        for tok in self._maxdeps(reads, writes):
            self._wait(q, tok)
        self.streams[q].append(("op", fn, s, 16))
        self._commit((s, v), reads, writes)

    def barrier(self):
        toks = [(e, self.cnt[e]) for e in self.ENG if self.cnt[e] > 0]
        for q, n in self.dma_q.items():
            for j in range(min(n, NDS)):
                last_i = ((n - 1 - j) // NDS) * NDS + j
                toks.append(("dma_%s%d" % (q, j), 16 * (last_i // NDS + 1)))
        for e in self.ENG:
            for t in toks:
                self._wait(e, t)
        self.state = {}

    def replay(self, nc, sems):
        engs = {}

        def run(name, eng):
            for it in self.streams[name]:
                if it[0] == "wait":
                    eng.wait_ge(sems[it[1]], it[2])
                else:
                    ins = it[1](eng)
                    ins.then_inc(sems[it[2]], it[3])

        with nc.Block() as block:
            @block.tensor
            def _(e):
                run("pe", e)

            @block.vector
            def _(e):
                run("dve", e)

            @block.scalar
            def _(e):
                run("act", e)

            @block.gpsimd
            def _(e):
                run("pool", e)

            @block.sync
            def _(e):
                run("sp", e)


def t5_thresholds():
    d = np.arange(0, 2048)
    nf = np.maximum(d, 1).astype(np.float32)
    large = 16 + (np.log(nf / np.float32(16)) / np.float32(math.log(128 / 16)) * np.float32(16)).astype(np.int32)
    large = np.minimum(large, 31)
    b = np.where(d < 16, d, large)
    return [int(np.argmax(b >= r)) for r in range(32)]


def moba_consts(s_role):
    dist = np.zeros((128, 4, 256), np.float32)
    for kc in range(4):
        dist[:, kc, :] = 256 + np.arange(256)[None, :] - 128 * kc - np.arange(128)[:, None]
    maskown = np.zeros((128, 8, 2, 16), np.float32)
    farmask = np.zeros((128, 8, 2, 16), np.float32)
    for qb in range(8):
        own = 8 + qb
        maskown[:, qb, :, own:] = -BIG
        if s_role == 0:
            maskown[:, qb, :, 0:8] = -BIG
        farmask[:, qb, :, 0:max(own - 1, 0)] = 1.0
    esel = np.zeros((16, 16, 128), np.float32)
    for n in range(16):
        esel[n, n, :] = 1.0
    return {"c_dist": dist.reshape(128, 1024), "c_maskown": maskown.reshape(128, 256),
            "c_farmask": farmask.reshape(128, 256), "c_esel": esel.reshape(16, 2048)}


def gdn_consts(inputs):
    i = np.arange(128)
    tri = (i[:, None] <= i[None, :]).astype(np.float32)
    masku = np.where(i[None, :] > i[:, None], -BIG, 0.0).astype(np.float32)
    strict = (i[:, None] > i[None, :]).astype(np.float32)
    cw = np.asarray(inputs["gdn_conv"], np.float32)[0]
    conv = np.ascontiguousarray(cw.T.reshape(24, 128, 4).transpose(1, 0, 2).reshape(128, 96))
    rep = lambda v, n: np.ascontiguousarray(np.broadcast_to(np.asarray(v, np.float32).reshape(1, n), (128, n)))
    return {"c_tri": tri, "c_masku": masku, "c_strict": strict, "c_onorm": rep(inputs["gdn_out_norm"][0], 128),
            "c_conv": conv, "c_alog": rep(inputs["gdn_a_log"][0], 8), "c_dtb": rep(inputs["gdn_dt_bias"][0], 8)}


IN_ATT_Q, IN_ATT_K, IN_ATT_V = 0, 1024, 2048
IN_GDN_Q, IN_GDN_K, IN_GDN_V = 3072, 4096, 5120
IN_Z, IN_BA, IN_GA, IN_GB = 6144, 7168, 7184, 9232


def build(dbg=False, tiles=None, phases="ABCDE", tiny_w=False, heads=None):
    heads = list(range(8)) if heads is None else heads
    tiles = list(range(NTILE)) if tiles is None else tiles
    from contextlib import ExitStack
    nc = bass.Bass("TRN2", target_bir_lowering=False)
    P = Prog()
    import os as _os
    if _os.environ.get("OPLIM"):
        P.limit = int(_os.environ["OPLIM"])
    es = ExitStack()

    def din(name, shape, dt=F32):
        import os
        if "noin" in os.environ.get("BIS", "") and name != "xin" and name not in os.environ.get("KEEP", "").split(","):
            class _D:
                def rearrange(self, *a, **k): return self
                def __getitem__(self, k): return self
            return _D()
        return nc.dram_tensor(name, list(shape), dt, kind="ExternalInput").ap()

    import os
    BIS0 = os.environ.get("BIS", "")

    def dscr(name, shape, dt):
        if "nodscr" in BIS0:
            return None
        return nc.dram_tensor(name, list(shape), dt).ap()

    def sb(name, shape, dt):
        if "nosb" in BIS0 and name != "xtok":
            return None
        return es.enter_context(nc.sbuf_tensor(name, list(shape), dt))

    xin = din("xin", [128, D] if tiny_w else [NTOK, D])
    if tiny_w:
        w1g, w1u, w1d, w_in = (din(n, [128, 128]) for n in ("ffn1_w_gate", "ffn1_w_up", "ffn1_w_down", "w_in"))
    else:
        w1g, w1u, w1d = din("ffn1_w_gate", [D, DFF]), din("ffn1_w_up", [D, DFF]), din("ffn1_w_down", [DFF, D])
        w_in = din("w_in", [D, 11280])
    g_ffn1, g_mix = din("g_ffn1", [128, DC]), din("g_mix", [128, DC])
    c_ident = din("c_ident", [128, 128])
    c_dist = din("c_dist", [128, 1024])
    c_maskown = din("c_maskown", [128, 256])
    c_farmask = din("c_farmask", [128, 256])
    c_relb = din("c_relb", [128, 256])
    c_esel = din("c_esel", [16, 2048])
    if tiny_w:
        wbra, wbrd, wout, wq, wkv, wo, w2g, w2u, w2d = (din(n, [128, 128]) for n in (
            "w_branch_attn", "w_branch_delta", "w_out", "cross_wq", "cross_wkv", "cross_wo", "ffn2_w_gate", "ffn2_w_up", "ffn2_w_down"))
    else:
        wbra, wbrd, wout = din("w_branch_attn", [1024, D]), din("w_branch_delta", [1024, D]), din("w_out", [D, D])
        wq, wkv, wo = din("cross_wq", [D, 512]), din("cross_wkv", [D, 1024]), din("cross_wo", [512, D])
        w2g, w2u, w2d = din("ffn2_w_gate", [D, DFF]), din("ffn2_w_up", [D, DFF]), din("ffn2_w_down", [DFF, D])
    g_cross, g_mem, g_ffn2, g_final = (din(n, [128, DC]) for n in ("g_cross", "g_mem", "g_ffn2", "g_final"))
    memb = din("memb", [256, D])
    c_tri, c_masku, c_strict, c_onorm = (din(n, [128, 128]) for n in ("c_tri", "c_masku", "c_strict", "c_onorm"))
    c_conv = din("c_conv", [128, 96])
    c_alog, c_dtb = din("c_alog", [128, 8]), din("c_dtb", [128, 8])
    del_d = dscr("del_d", [8, 128, NOWN], BF16)
    out = nc.dram_tensor("out", [NOWN, D], F32, kind="ExternalOutput").ap()
    att_d = dscr("att_d", [8, 128, NOWN], BF16)

    h1_d = dscr("h1_d", [128, DC, NOWN], F32)
    qT_d = dscr("qT_d", [8, 128, NOWN], BF16)
    kT_d = dscr("kT_d", [8, 128, NTOK], BF16)
    v_d = dscr("v_d", [NTOK, 1024], BF16)
    gq_d = dscr("gq_d", [8, 128, NTOK], BF16)
    gk_d = dscr("gk_d", [8, 128, NTOK], BF16)
    gv_d = dscr("gv_d", [8, 128, NTOK], BF16)
    z_d = dscr("z_d", [8, 128, NOWN], BF16)
    ba_d = dscr("ba_d", [NTOK, 16], F32)
    ga_d = dscr("ga_d", [16, 128, NOWN], BF16)
    gb_d = dscr("gb_d", [16, 128, NOWN], BF16)
    dbg_out = {}
    if dbg:
        dbg_out["d_h1"] = nc.dram_tensor("d_h1", [128, DC, NOWN], F32, kind="ExternalOutput").ap()
        dbg_out["d_kT"] = nc.dram_tensor("d_kT", [8, 128, NTOK], BF16, kind="ExternalOutput").ap()
        dbg_out["d_v"] = nc.dram_tensor("d_v", [NTOK, 1024], BF16, kind="ExternalOutput").ap()
        dbg_out["d_ba"] = nc.dram_tensor("d_ba", [NTOK, 16], F32, kind="ExternalOutput").ap()
        dbg_out["d_ga"] = nc.dram_tensor("d_ga", [16, 128, NOWN], BF16, kind="ExternalOutput").ap()

    ident = sb("ident", [128, 128], F32)
    ones_bf = sb("ones_bf", [128, 128], BF16)
    identb_t = sb("identb", [128, 128], BF16)
    identb = identb_t[:, :]
    gains = sb("gains", [128, 8, DC], F32)
    xtok = sb("xtok", [128, 2, D], F32)
    xT = sb("xT", [128, DC, TT], F32)
    uT = sb("uT", [128, DC, TT], BF16)
    aT = sb("aT", [128, FC, TT], BF16)
    wring = sb("wring", [128, 4, 4096], BF16)
    wba = sb("wba", [128, DC, 16], BF16)
    sq = sb("sq", [128, 2, TT], BF16)
    rstd = sb("rstd", [128, TT], F32)
    sg = sb("sg", [128, 2, TT], F32)
    ostage = sb("ostage", [128, 4, TT], BF16)
    bastage = sb("bastage", [128, 4, 16], F32)
    kms = sb("kms", [128, 8, 16], F32)
    misc = sb("misc", [128, 4096], F32)
    kvm = sb("kvm", [128, 2048], BF16)
    esel = sb("esel", [16, 2048], F32)
    import os
    nps = 4 if "ps4" in os.environ.get("BIS", "") else 8
    ps = [es.enter_context(nc.psum_tensor("ps%d" % i, [128, 512], F32)) for i in range(nps)]

    sems = {e: es.enter_context(nc.semaphore("s_" + e)) for e in Prog.ENG}
    for q in ("sp", "pool"):
        for j in range(NDS):
            sems["dma_%s%d" % (q, j)] = es.enter_context(nc.semaphore("s_dma_%s%d" % (q, j)))

    cp_rr = [0]

    def evac_copy(out_ap, in_ap, reads, writes):
        cp_rr[0] ^= 1
        if cp_rr[0]:
            P.op("dve", lambda e, o=out_ap, i=in_ap: e.tensor_copy(o, i), reads, writes)
        else:
            P.op("act", lambda e, o=out_ap, i=in_ap: e.copy(o, i), reads, writes)

    wr_i = [0]

    def wslot():
        s = wr_i[0] % 4
        wr_i[0] += 1
        return s

    def load_wcols(w2d, col0, ncols, nrc=DC):
        s = wslot()
        view = wring[:, s, 0:nrc * ncols].rearrange("p (c n) -> p c n", n=ncols)
        src = w2d.rearrange("(c p) n -> p c n", p=128)
        hs = max(nrc // 2, 1)
        for half in range(2):
            rows = range(half * hs, min(half * hs + hs, nrc))
            wkeys = tuple(sorted({("w", s, (r * ncols) // 2048) for r in rows}))
            P.dma("pool", lambda e, o=view[:, half * hs:half * hs + hs, :],
                  i=src[:, half * hs:half * hs + hs, col0:col0 + ncols]: e.dma_start(out=o, in_=i),
                  reads=(), writes=wkeys)
        return s, view

    def rms_to_uT(gidx, nt=TT, inplace=False):
        for c in range(DC):
            P.op("act", lambda e, o=sq[:, c % 2, 0:nt], i=xT[:, c, 0:nt]: e.activation(o, i, AF.Square),
                 reads=(("xT", c),), writes=(("sq", c % 2),))
            P.op("pe", lambda e, o=ps[6][:, 0:nt], r=sq[:, c % 2, 0:nt], st=(c == 0), sp=(c == DC - 1):
                 e.matmul(o, ones_bf[:, :], r, start=st, stop=sp),
                 reads=(("sq", c % 2), "ones"), writes=(("ps", 6),))
        P.op("dve", lambda e: e.tensor_scalar(rstd[:, 0:nt], ps[6][:, 0:nt], 1.0 / D, EPS, op0=ALU.mult, op1=ALU.add),
             reads=(("ps", 6),), writes=("rstd",))
        P.op("act", lambda e: e.activation(rstd[:, 0:nt], rstd[:, 0:nt], AF.Sqrt),
             reads=("rstd",), writes=("rstd",))
        P.op("dve", lambda e: e.reciprocal(rstd[:, 0:nt], rstd[:, 0:nt]),
             reads=("rstd",), writes=("rstd",))
        for c in range(DC):
            if inplace:
                P.op("dve", lambda e, o=xT[:, c, 0:nt], g=gains[:, gidx, c:c + 1]:
                     e.scalar_tensor_tensor(o, o, g, rstd[:, 0:nt], op0=ALU.mult, op1=ALU.mult),
                     reads=(("xT", c), "rstd", "gains"), writes=(("xT", c),))
                continue
            P.op("dve", lambda e, o=uT[:, c, 0:nt], i=xT[:, c, 0:nt], g=gains[:, gidx, c:c + 1]:
                 e.scalar_tensor_tensor(o, i, g, rstd[:, 0:nt], op0=ALU.mult, op1=ALU.mult),
                 reads=(("xT", c), "rstd", "gains"), writes=(("uT", c),))

    def ffn(wg, wu, wd):
        for fb in range(FC // 2):
            sgi, vg = load_wcols(wg, fb * 256, 256)
            sui, vu = load_wcols(wu, fb * 256, 256)
            for j in range(2):
                f = fb * 2 + j
                pg, pu = ps[f % 2], ps[2 + f % 2]
                for c in range(DC):
                    P.op("pe", lambda e, o=pg[:, :], l=vg[:, c, j * 128:(j + 1) * 128], r=uT[:, c, :],
                         st=(c == 0), sp=(c == DC - 1): e.matmul(o, l, r, start=st, stop=sp),
                         reads=(("w", sgi, c // 8), ("uT", c)), writes=(("ps", f % 2),))
                for c in range(DC):
                    P.op("pe", lambda e, o=pu[:, :], l=vu[:, c, j * 128:(j + 1) * 128], r=uT[:, c, :],
                         st=(c == 0), sp=(c == DC - 1): e.matmul(o, l, r, start=st, stop=sp),
                         reads=(("w", sui, c // 8), ("uT", c)), writes=(("ps", 2 + f % 2),))
                P.op("act", lambda e, o=sg[:, f % 2, :], i=pg[:, :]: e.activation(o, i, AF.Silu),
                     reads=(("ps", f % 2),), writes=(("sg", f % 2),))
                P.op("dve", lambda e, o=aT[:, f, :], a=sg[:, f % 2, :], b=pu[:, :]: e.tensor_tensor(o, b, a, op=ALU.mult),
                     reads=(("sg", f % 2), ("ps", 2 + f % 2)), writes=(("aT", f),))
        wdv = wd.rearrange("(fc p) d -> p fc d", p=128)
        for dg in range(4):
            for fb in range(FC // 4):
                s = wslot()
                view = wring[:, s, 0:2048].rearrange("p (f n) -> p f n", n=512)
                P.dma("pool", lambda e, o=view, i=wdv[:, fb * 4:fb * 4 + 4, dg * 512:(dg + 1) * 512]:
                      e.dma_start(out=o, in_=i), reads=(), writes=(("w", s, 0), ("w", s, 1)))
                for j in range(4):
                    f = fb * 4 + j
                    for q in range(4):
                        P.op("pe", lambda e, o=ps[4 + q][:, :], l=view[:, j, q * 128:(q + 1) * 128], r=aT[:, f, :],
                             st=(f == 0), sp=(f == FC - 1): e.matmul(o, l, r, start=st, stop=sp),
                             reads=(("w", s, 0), ("w", s, 1), ("aT", f)), writes=(("ps", 4 + q),))
            for q in range(4):
                c = dg * 4 + q
                P.op("dve", lambda e, o=xT[:, c, :], i=ps[4 + q][:, :]:
                     e.scalar_tensor_tensor(o, i, 0.5, o, op0=ALU.mult, op1=ALU.add),
                     reads=(("ps", 4 + q), ("xT", c)), writes=(("xT", c),))

    def load_xT(tok0, src=None, nsub=4):
        src = xin if src is None else src
        for s in range(nsub):
            P.dma("sp", lambda e, o=xtok[:, s % 2, :], i=src[tok0 + s * 128: tok0 + (s + 1) * 128, :]:
                  e.dma_start(out=o, in_=i), reads=(), writes=(("xtok", s % 2),))
            for cg in range(4):
                pb = ps[4 + cg % 2]
                for j in range(4):
                    c = cg * 4 + j
                    P.op("pe", lambda e, o=pb[:, j * 128:(j + 1) * 128], i=xtok[:, s % 2, c * 128:(c + 1) * 128]:
                         e.transpose(o, i, ident[:, :]),
                         reads=(("xtok", s % 2), "ident"), writes=(("ps", 4 + cg % 2),))
                evac_copy(xT[:, cg * 4:cg * 4 + 4, s * 128:(s + 1) * 128],
                          pb[:, :].rearrange("p (c t) -> p c t", t=128),
                          reads=(("ps", 4 + cg % 2),), writes=tuple(("xT", cg * 4 + j) for j in range(4)))

    import os
    BIS = os.environ.get("BIS", "")
    if "mini" in BIS:
        P.dma("pool", lambda e: e.dma_start(out=xtok[:, 0, :], in_=xin[0:128, :]), writes=("xtok",))
        P.dma("pool", lambda e: e.dma_start(out=out[0:128, :], in_=xtok[:, 0, :]), reads=("xtok",))
        P.barrier()
    if "nosetup" not in BIS:
      P.dma("sp", lambda e: e.dma_start(out=ident[:, :], in_=c_ident[:, :]), writes=("ident",))
      P.op("dve", lambda e: e.memset(ones_bf[:, :], 1.0), writes=("ones",))
      P.op("dve", lambda e: e.tensor_copy(identb, ident[:, :]), reads=("ident",), writes=("identb",))
      P.dma("sp", lambda e: e.dma_start(out=gains[:, 0, :], in_=g_ffn1[:, :]), writes=("gains",))
      P.dma("sp", lambda e: e.dma_start(out=gains[:, 1, :], in_=g_mix[:, :]), writes=("gains",))
      P.op("dve", lambda e: e.memset(kms[:, :, :], 0.0), writes=("kms",))
    if "wout" in BIS:
      P.dma("sp", lambda e: e.dma_start(out=out[0:128, :], in_=xtok[:, 0, :]), writes=())
      P.barrier()

    ost_i = [0]

    def inproj_fm(col0, dest, tok0, ntok_dest_off, func=None, kmean_h=None, t=None):
        s, view = load_wcols(w_in, col0, 256)
        for j in range(2):
            pi = ost_i[0] % 4
            ost_i[0] += 1
            pb = ps[pi]
            for c in range(DC):
                P.op("pe", lambda e, o=pb[:, :], l=view[:, c, j * 128:(j + 1) * 128], r=uT[:, c, :],
                     st=(c == 0), sp=(c == DC - 1): e.matmul(o, l, r, start=st, stop=sp),
                     reads=(("w", s, c // 8), ("uT", c)), writes=(("ps", pi),))
            if func is None:
                evac_copy(ostage[:, pi, :], pb[:, :], reads=(("ps", pi),), writes=(("ost", pi),))
            else:
                P.op("act", lambda e, o=ostage[:, pi, :], i=pb[:, :]: e.activation(o, i, func),
                     reads=(("ps", pi),), writes=(("ost", pi),))
            if kmean_h is not None:
                h = kmean_h + j
                P.op("dve", lambda e, o=kms[:, h, 2 * t:2 * t + 2], i=pb[:, :].rearrange("p (b k) -> p b k", k=256):
                     e.reduce_sum(o, i, axis=AX.X), reads=(("ps", pi),), writes=("kms",))
            dch = dest[0][dest[1] + j]
            P.dma("sp", lambda e, o=dch[:, ntok_dest_off:ntok_dest_off + TT], i=ostage[:, pi, :]:
                  e.dma_start(out=o, in_=i), reads=(("ost", pi),), writes=())

    def inproj_tm_v(tok0):
        for blk in range(4):
            s, view = load_wcols(w_in, IN_ATT_V + blk * 256, 256)
            for sub in range(4):
                pi = ost_i[0] % 4
                ost_i[0] += 1
                pb = ps[pi]
                for c in range(DC):
                    P.op("pe", lambda e, o=pb[:, 0:256], l=uT[:, c, sub * 128:(sub + 1) * 128], r=view[:, c, :],
                         st=(c == 0), sp=(c == DC - 1): e.matmul(o, l, r, start=st, stop=sp),
                         reads=(("w", s, c // 8), ("uT", c)), writes=(("ps", pi),))
                evac_copy(ostage[:, pi, 0:256], pb[:, 0:256], reads=(("ps", pi),), writes=(("ost", pi),))
                P.dma("sp", lambda e, o=v_d[tok0 + sub * 128: tok0 + (sub + 1) * 128, blk * 256:(blk + 1) * 256],
                      i=ostage[:, pi, 0:256]: e.dma_start(out=o, in_=i), reads=(("ost", pi),), writes=())

    def inproj_ba(tok0):
        for sub in range(4):
            pi = ost_i[0] % 4
            ost_i[0] += 1
            pb = ps[pi]
            for c in range(DC):
                P.op("pe", lambda e, o=pb[:, 0:16], l=uT[:, c, sub * 128:(sub + 1) * 128], r=wba[:, c, :],
                     st=(c == 0), sp=(c == DC - 1): e.matmul(o, l, r, start=st, stop=sp),
                     reads=("wba", ("uT", c)), writes=(("ps", pi),))
            P.op("dve", lambda e, o=bastage[:, sub, :], i=pb[:, 0:16]: e.tensor_copy(o, i),
                 reads=(("ps", pi),), writes=(("bast", sub),))
            P.dma("sp", lambda e, o=ba_d[tok0 + sub * 128: tok0 + (sub + 1) * 128, :], i=bastage[:, sub, :]:
                  e.dma_start(out=o, in_=i), reads=(("bast", sub),), writes=())

    if "A" in phases:
        P.dma("pool", lambda e: e.dma_start(out=wba[:, :, :],
              in_=w_in.rearrange("(c p) n -> p c n", p=128)[:, :, IN_BA:IN_BA + 16]), writes=("wba",))
        for t in tiles:
            tok0 = t * TT
            own = tok0 >= NTOK - NOWN
            otok = tok0 - (NTOK - NOWN)
            load_xT(tok0)
            rms_to_uT(0)
            ffn(w1g, w1u, w1d)
            rms_to_uT(1)
            if own:
                P.dma("sp", lambda e, o=h1_d[:, :, otok:otok + TT]: e.dma_start(out=o, in_=xT[:, :, :]),
                      reads=tuple(("xT", c) for c in range(DC)), writes=())
            for hb in range(4):
                if own:
                    inproj_fm(IN_ATT_Q + hb * 256, (qT_d, hb * 2), tok0, otok)
                inproj_fm(IN_ATT_K + hb * 256, (kT_d, hb * 2), tok0, tok0)
                inproj_fm(IN_GDN_Q + hb * 256, (gq_d, hb * 2), tok0, tok0)
                inproj_fm(IN_GDN_K + hb * 256, (gk_d, hb * 2), tok0, tok0)
                inproj_fm(IN_GDN_V + hb * 256, (gv_d, hb * 2), tok0, tok0)
                if own:
                    inproj_fm(IN_Z + hb * 256, (z_d, hb * 2), tok0, otok, func=AF.Silu)
            inproj_tm_v(tok0)
            inproj_ba(tok0)
            if own:
                for gbk in range(8):
                    inproj_fm(IN_GA + gbk * 256, (ga_d, gbk * 2), tok0, otok, func=AF.Sigmoid)
                    inproj_fm(IN_GB + gbk * 256, (gb_d, gbk * 2), tok0, otok, func=AF.Sigmoid)
        P.barrier()

    SCALE = 128.0 ** -0.5
    aTf = aT[:, :, :].rearrange("p c t -> p (c t)")
    xTf = xT[:, :, :].rearrange("p c t -> p (c t)")
    if "C" in phases:
        KT = aTf[:, 0:4096]
        Vt = aTf[:, 4096:8192].rearrange("p (c d) -> p c d", d=128)
        qT = aTf[:, 8192:10240]
        pT = aTf[:, 10240:12288].rearrange("p (r q) -> p r q", q=256)
        kmb = aTf[:, 12288:12304]
        attT = aTf[:, 12544:14592]
        biasT = xTf.rearrange("p (h k q) -> p h k q", h=8, k=4)
        mo = [0]

        def mtile(n):
            a = mo[0]
            mo[0] += n
            assert mo[0] <= 4096
            return misc[:, a:a + n]
        distt = mtile(1024).rearrange("p (k q) -> p k q", q=256)
        indt = mtile(512).rearrange("p (k q) -> p k q", q=256)
        tmpS = mtile(512).rearrange("p (k q) -> p k q", q=256)
        maskown = mtile(256).rearrange("p (b n) -> p b n", n=32)
        farmask = mtile(256).rearrange("p (b n) -> p b n", n=32)
        relb = mtile(256).rearrange("p (r h) -> p r h", h=8)
        dbc = mtile(256).rearrange("p (r h) -> p r h", h=8)
        sc = mtile(32)
        m8 = mtile(16).rearrange("p (t e) -> p t e", e=8)
        tsel = mtile(32)
        selb = mtile(32)
        selbT = mtile(256)
        rsum = mtile(256)
        ksum = mtile(16)
        P.dma("sp", lambda e: e.dma_start(out=distt, in_=c_dist.rearrange("p (k q) -> p k q", q=256)), writes=("dist",))
        P.dma("sp", lambda e: e.dma_start(out=maskown, in_=c_maskown.rearrange("p (b n) -> p b n", n=32)), writes=("maskown",))
        P.dma("sp", lambda e: e.dma_start(out=farmask, in_=c_farmask.rearrange("p (b n) -> p b n", n=32)), writes=("farmask",))
        P.dma("sp", lambda e: e.dma_start(out=relb, in_=c_relb.rearrange("p (r h) -> p r h", h=8)), writes=("relb",))
        P.dma("sp", lambda e: e.dma_start(out=esel[0:16, :], in_=c_esel[:, :]), writes=("esel",))
        P.op("dve", lambda e: e.tensor_copy(dbc[:, 0:1, :], relb[:, 0:1, :]), reads=("relb",), writes=("dbc",))
        P.op("dve", lambda e: e.tensor_sub(dbc[:, 1:32, :], relb[:, 1:32, :], relb[:, 0:31, :]), reads=("relb",), writes=("dbc",))
        thr = t5_thresholds()
        for kc in range(4):
            for h in range(8):
                P.op("dve", lambda e, o=biasT[:, h, kc, :], i=distt[:, kc, :]:
                     e.tensor_scalar(o, i, 0.0, -BIG, op0=ALU.is_lt, op1=ALU.mult),
                     reads=("dist",), writes=(("bias", h, kc),))
            for r in range(32):
                P.op("dve", lambda e, o=indt[:, r % 2, :], i=distt[:, kc, :], th=float(thr[r]):
                     e.tensor_single_scalar(o, i, th, op=ALU.is_ge), reads=("dist",), writes=(("ind", r % 2),))
                for h in range(8):
                    P.op("dve", lambda e, o=biasT[:, h, kc, :], i=indt[:, r % 2, :], sc_=dbc[:, r, h:h + 1]:
                         e.scalar_tensor_tensor(o, i, sc_, o, op0=ALU.mult, op1=ALU.add),
                         reads=(("ind", r % 2), "dbc", ("bias", h, kc)), writes=(("bias", h, kc),))
        for h in heads:
            P.dma("sp", lambda e, i=kT_d[h]: e.dma_start(out=KT, in_=i), writes=("KT",))
            for g4 in range(4):
                P.dma("sp", lambda e, o=Vt[:, g4 * 8:(g4 + 1) * 8, :],
                      i=v_d[g4 * 1024:(g4 + 1) * 1024, h * 128:(h + 1) * 128].rearrange("(c p) d -> p c d", p=128):
                      e.dma_start(out=o, in_=i), writes=(("Vt", g4),))
            P.dma("sp", lambda e, i=qT_d[h]: e.dma_start(out=qT, in_=i), writes=("qT",))
            P.op("dve", lambda e: e.reduce_sum(ksum, KT.rearrange("p (b k) -> p b k", k=256), axis=AX.X),
                 reads=("KT",), writes=("ksum",))
            P.op("act", lambda e: e.activation(kmb, ksum, AF.Copy, scale=1.0 / 256), reads=("ksum",), writes=("kmb",))
            for qb in range(8):
                own = 8 + qb
                for t in range(2):
                    P.op("pe", lambda e, o=ps[7][:, t * 16:(t + 1) * 16], l=qT[:, qb * 256 + t * 128: qb * 256 + (t + 1) * 128]:
                         e.matmul(o, l, kmb, start=True, stop=True), reads=("qT", "kmb"), writes=(("ps", 7),))
                P.op("dve", lambda e, m=maskown[:, qb, :]: e.tensor_tensor(sc, ps[7][:, 0:32], m, op=ALU.add),
                     reads=(("ps", 7), "maskown"), writes=("sc",))
                for t in range(2):
                    P.op("dve", lambda e, o=m8[:, t, :], i=sc[:, t * 16:(t + 1) * 16]: e.max(o, i), reads=("sc",), writes=(("m8", t),))
                for t in range(2):
                    P.op("dve", lambda e, o=tsel[:, t * 16:(t + 1) * 16], i=sc[:, t * 16:(t + 1) * 16], th=m8[:, t, 2:3]:
                         e.tensor_scalar(o, i, th, 1.0, op0=ALU.is_ge, op1=ALU.subtract),
                         reads=("sc", ("m8", t)), writes=("tsel",))
                P.op("dve", lambda e, m=maskown[:, qb, :]: e.scalar_tensor_tensor(selb, tsel, BIG, m, op0=ALU.mult, op1=ALU.add),
                     reads=("tsel", "maskown"), writes=("selb",))
                P.op("dve", lambda e, f=farmask[:, qb, :], t31=relb[:, 31, h:h + 1]:
                     e.scalar_tensor_tensor(selb, f, t31, selb, op0=ALU.mult, op1=ALU.add),
                     reads=("selb", "farmask", "relb"), writes=("selb",))
                for t in range(2):
                    P.op("pe", lambda e, o=ps[6][0:16, t * 128:(t + 1) * 128], i=selb[:, t * 16:(t + 1) * 16]:
                         e.transpose(o, i, ident[:, :]), reads=("selb", "ident"), writes=(("ps", 6),))
                P.op("act", lambda e: e.activation(selbT[0:16, :], ps[6][0:16, 0:256], AF.Copy, scale=1.0 / SCALE),
                     reads=(("ps", 6),), writes=("selbT",))
                nch = 2 * (own + 1)
                po, pr = 2 + 2 * (qb % 2), 3 + 2 * (qb % 2)
                for ci in range(nch):
                    n = ci // 2
                    pb = ps[ci % 2]
                    P.op("pe", lambda e, o=pb[:, 0:256], l=KT[:, ci * 128:(ci + 1) * 128], r=qT[:, qb * 256:(qb + 1) * 256],
                         sp_=(n == own): e.matmul(o, l, r, start=True, stop=sp_),
                         reads=("KT", "qT"), writes=(("ps", ci % 2),))
                    if n < own:
                        P.op("pe", lambda e, o=pb[:, 0:256], l=esel[0:16, n * 128:(n + 1) * 128]:
                             e.matmul(o, l, selbT[0:16, :], start=False, stop=True),
                             reads=("esel", "selbT"), writes=(("ps", ci % 2),))
                    sl = ci % 8
                    if n >= own - 1:
                        kc = (n - (own - 1)) * 2 + ci % 2
                        P.op("dve", lambda e, o=tmpS[:, ci % 2, :], i=pb[:, 0:256], b=biasT[:, h, kc, :]:
                             e.scalar_tensor_tensor(o, i, SCALE, b, op0=ALU.mult, op1=ALU.add),
                             reads=(("ps", ci % 2), ("bias", h, kc)), writes=(("tmpS", ci % 2),))
                        P.op("act", lambda e, o=pT[:, sl, :], i=tmpS[:, ci % 2, :]: e.activation(o, i, AF.Exp),
                             reads=(("tmpS", ci % 2),), writes=(("pT", sl),))
                    else:
                        P.op("act", lambda e, o=pT[:, sl, :], i=pb[:, 0:256]: e.activation(o, i, AF.Exp, scale=SCALE),
                             reads=(("ps", ci % 2),), writes=(("pT", sl),))
                    P.op("pe", lambda e, o=ps[po][:, 0:256], l=Vt[:, ci, :], r=pT[:, sl, :], st=(ci == 0), sp_=(ci == nch - 1):
                         e.matmul(o, l, r, start=st, stop=sp_), reads=(("Vt", ci // 8), ("pT", sl)), writes=(("ps", po),))
                    P.op("pe", lambda e, o=ps[pr][:, 0:256], r=pT[:, sl, :], st=(ci == 0), sp_=(ci == nch - 1):
                         e.matmul(o, ones_bf[:, :], r, start=st, stop=sp_), reads=("ones", ("pT", sl)), writes=(("ps", pr),))
                P.op("dve", lambda e, i=ps[pr][:, 0:256]: e.reciprocal(rsum, i), reads=(("ps", pr),), writes=("rsum",))
                P.op("dve", lambda e, o=attT[:, qb * 256:(qb + 1) * 256], i=ps[po][:, 0:256]:
                     e.tensor_tensor(o, i, rsum, op=ALU.mult), reads=(("ps", po), "rsum"), writes=("attT",))
            P.dma("sp", lambda e, o=att_d[h]: e.dma_start(out=o, in_=attT), reads=("attT",), writes=())
        P.barrier()

    if "D" in phases:
        uTf = uT[:, :, :].rearrange("p c t -> p (c t)")
        xtf = xtok[:, :, :].rearrange("p c t -> p (c t)")
        rawq, rawk, rawv = aTf[:, 0:4096], aTf[:, 4096:8192], aTf[:, 8192:12288]
        zT = aTf[:, 12288:14336]
        delT = aTf[:, 14336:16384]
        vsb = aTf[:, 16384:20480]
        qn, kn = xTf[:, 0:4096], xTf[:, 4096:8192]
        cacc = xtf[:, 0:4096]
        w32 = [xtf[:, i * 128:(i + 1) * 128] for i in range(32)]
        Et, Mt, R, gbt, osb, t1s, Stt = w32[0], w32[1], w32[2], w32[3], w32[4], w32[5], w32[6]
        Mp, Np = [w32[7], w32[8]], [w32[9], w32[10]]
        attnf = w32[11]
        b16 = [uTf[:, i * 128:(i + 1) * 128] for i in range(16)]
        attnT, kdec, kbeg, vbeta, Rb, wTn, vnew, Sb, onb = b16[0:9]
        mo = [0]

        def mt2(n):
            a = mo[0]
            mo[0] += n
            assert mo[0] <= 4096
            return misc[:, a:a + n]
        bat = mt2(512).rearrange("p (c n) -> p c n", n=16)
        beta, gt, Gt, Glt, eG, eGlG, cdt, skb = (mt2(256) for _ in range(8))
        tri, negtri, masku, strict, onorm, ones_f = (mt2(128) for _ in range(6))
        convw = mt2(96)
        alog, dtb, negA = mt2(8), mt2(8), mt2(8)
        ss, rs = mt2(1), mt2(1)
        v3 = lambda t: t.rearrange("p (c h) -> p c h", h=8)

        def slot(b, k):
            return ps[b][:, k * 128:(k + 1) * 128]

        def slotb(b, k):
            return ps[b][:, :].bitcast(BF16)[:, k * 256:k * 256 + 128]
        for dst, src, key in ((tri, c_tri, "tri"), (masku, c_masku, "masku"), (strict, c_strict, "strict"),
                              (onorm, c_onorm, "onorm"), (convw, c_conv, "convw"), (alog, c_alog, "alog"), (dtb, c_dtb, "dtb")):
            P.dma("sp", lambda e, o=dst, i=src: e.dma_start(out=o, in_=i[:, :]), writes=(key,))
        P.dma("sp", lambda e: e.dma_start(out=bat, in_=ba_d.rearrange("(c p) n -> p c n", p=128)), writes=("bat",))
        P.op("dve", lambda e: e.memset(ones_f, 1.0), writes=("ones_f",))
        P.op("dve", lambda e: e.tensor_scalar_mul(negtri, tri, -1.0), reads=("tri",), writes=("negtri",))
        P.op("act", lambda e: e.activation(v3(beta), bat[:, :, 0:8], AF.Sigmoid), reads=("bat",), writes=("beta",))
        for h in range(8):
            P.op("dve", lambda e, o=v3(gt)[:, :, h], i=bat[:, :, 8 + h], sc_=dtb[:, h:h + 1]: e.tensor_scalar_add(o, i, sc_),
                 reads=("bat", "dtb"), writes=("gt",))
        P.op("act", lambda e: e.activation(gt, gt, AF.Exp), reads=("gt",), writes=("gt",))
        P.op("dve", lambda e: e.tensor_scalar_add(gt, gt, 1.0), reads=("gt",), writes=("gt",))
        P.op("act", lambda e: e.activation(gt, gt, AF.Ln), reads=("gt",), writes=("gt",))
        P.op("act", lambda e: e.activation(negA, alog, AF.Exp), reads=("alog",), writes=("negA",))
        P.op("dve", lambda e: e.tensor_scalar_mul(negA, negA, -1.0), reads=("negA",), writes=("negA",))
        for h in range(8):
            P.op("dve", lambda e, o=v3(gt)[:, :, h], sc_=negA[:, h:h + 1]: e.tensor_scalar_mul(o, o, sc_),
                 reads=("gt", "negA"), writes=("gt",))
        P.op("pe", lambda e: e.matmul(ps[5][:, 0:256], tri, gt, start=True, stop=True), reads=("tri", "gt"), writes=(("ps", 5),))
        P.op("pe", lambda e: e.matmul(ps[5][:, 256:512], ones_f, gt, start=True, stop=True), reads=("ones_f", "gt"), writes=(("ps", 5),))
        P.op("dve", lambda e: e.tensor_copy(Gt, ps[5][:, 0:256]), reads=(("ps", 5),), writes=("Gt",))
        P.op("dve", lambda e: e.tensor_copy(Glt, ps[5][:, 256:512]), reads=(("ps", 5),), writes=("Glt",))
        P.op("act", lambda e: e.activation(eG, Gt, AF.Exp), reads=("Gt",), writes=("eG",))
        P.op("act", lambda e: e.activation(cdt, Glt, AF.Exp), reads=("Glt",), writes=("cdt",))
        P.op("dve", lambda e: e.tensor_sub(eGlG, Glt, Gt), reads=("Glt", "Gt"), writes=("eGlG",))
        P.op("act", lambda e: e.activation(eGlG, eGlG, AF.Exp), reads=("eGlG",), writes=("eGlG",))
        P.op("dve", lambda e: e.tensor_mul(skb, beta, eG), reads=("beta", "eG"), writes=("skb",))
        P.marks.append(("D-setup-end", P.total))
        P.barrier()
        SHARED = {"tri", "negtri", "masku", "strict", "onorm", "ones_f", "ones", "ident", "identb", "convw", "gt", "beta",
                  "eG", "eGlG", "cdt", "skb", "Gt", "Glt", "rstd", "cacc", "raw"}
        wrf = wring[:, :, :].rearrange("p s n -> p (s n)").bitcast(F32)
        rawb = uTf[:, 0:4096]
        LT = []
        for l_ in range(2):
            qn_l, kn_l = (xTf[:, 0:4096], xTf[:, 4096:8192]) if l_ == 0 else (wrf[:, 0:4096], wrf[:, 4096:8192])
            wt = [xtf[:, (l_ * 12 + i) * 128:(l_ * 12 + i + 1) * 128] for i in range(12)]
            bt = [uTf[:, 4096 + (l_ * 9 + i) * 128: 4096 + (l_ * 9 + i + 1) * 128] for i in range(9)]
            LT.append(dict(zT=aTf[:, 8192 + l_ * 2048: 8192 + (l_ + 1) * 2048], delT=aTf[:, 12288 + l_ * 2048: 12288 + (l_ + 1) * 2048],
                           vsb=aTf[:, l_ * 4096:(l_ + 1) * 4096], qn=qn_l, kn=kn_l, wt=wt, bt=bt, ss=mt2(1), rs=mt2(1)))

        def lane_ops(l):
            def lk(k):
                if k in SHARED or (isinstance(k, tuple) and k[0] in ("ps", "sq")):
                    return k
                return ("L", l, k)

            def op(eng, fn, reads=(), writes=()):
                P.op(eng, fn, tuple(lk(k) for k in reads), tuple(lk(k) for k in writes))

            def dma(q, fn, reads=(), writes=()):
                P.dma(q, fn, tuple(lk(k) for k in reads), tuple(lk(k) for k in writes))
            return op, dma

        def unpack(l):
            T = LT[l]
            wt, bt = T["wt"], T["bt"]
            return (T["zT"], T["delT"], T["vsb"], T["qn"], T["kn"], wt[0], wt[1], wt[2], wt[3], wt[4], wt[5], wt[6],
                    [wt[7], wt[8]], [wt[9], wt[10]], wt[11], bt[0], bt[1], bt[2], bt[3], bt[4], bt[5], bt[6], bt[7], bt[8], T["ss"], T["rs"])

        def prologue(h, l):
            op, dma = lane_ops(l)
            (zT, delT, vsb, qn, kn, Et, Mt, R, gbt, osb, t1s, Stt, Mp, Np, attnf,
             attnT, kdec, kbeg, vbeta, Rb, wTn, vnew, Sb, onb, ss, rs) = unpack(l)
            dma("sp", lambda e, i=z_d[h]: e.dma_start(out=zT, in_=i), writes=("zT",))
            for part, (rsrc, key) in enumerate(((gq_d, "raw"), (gk_d, "raw"), (gv_d, "raw"))):
                raw = rawb
                dma("sp", lambda e, i=rsrc[h]: e.dma_start(out=rawb, in_=i), writes=("raw",))
                cw = lambda j: convw[:, (part * 8 + h) * 4 + j:(part * 8 + h) * 4 + j + 1]
                op("dve", lambda e, r=raw, w=cw(3): e.tensor_scalar_mul(cacc, r, w), reads=(key, "convw"), writes=("cacc",))
                for sh in (1, 2, 3):
                    op("dve", lambda e, r=raw, w=cw(3 - sh), sh=sh:
                         e.scalar_tensor_tensor(cacc[:, sh:4096], r[:, 0:4096 - sh], w, cacc[:, sh:4096], op0=ALU.mult, op1=ALU.add),
                         reads=(key, "convw", "cacc"), writes=("cacc",))
                if part == 2:
                    op("act", lambda e: e.activation(vsb, cacc, AF.Silu), reads=("cacc",), writes=("vsb",))
                    continue
                dstn, dkey = (qn, "qn") if part == 0 else (kn, "kn")
                op("act", lambda e, o=dstn: e.activation(o, cacc, AF.Silu), reads=("cacc",), writes=(dkey,))
                for blk in range(8):
                    bs = slice(blk * 512, (blk + 1) * 512)
                    op("act", lambda e, i=dstn[:, bs], o=sq[:, blk % 2, :]: e.activation(o, i, AF.Square),
                         reads=(dkey,), writes=(("sq", blk % 2),))
                    pb = 6 + blk % 2
                    op("pe", lambda e, o=ps[pb][:, :], r=sq[:, blk % 2, :]: e.matmul(o, ones_bf[:, :], r, start=True, stop=True),
                         reads=(("sq", blk % 2), "ones"), writes=(("ps", pb),))
                    op("dve", lambda e, i=ps[pb][:, :]: e.tensor_scalar_add(rstd[:, :], i, EPS), reads=(("ps", pb),), writes=("rstd",))
                    op("act", lambda e: e.activation(rstd[:, :], rstd[:, :], AF.Sqrt), reads=("rstd",), writes=("rstd",))
                    op("dve", lambda e: e.reciprocal(rstd[:, :], rstd[:, :]), reads=("rstd",), writes=("rstd",))
                    op("dve", lambda e, o=dstn[:, bs], scl=(SCALE if part == 0 else 1.0):
                         e.scalar_tensor_tensor(o, o, scl, rstd[:, :], op0=ALU.mult, op1=ALU.mult),
                         reads=(dkey, "rstd"), writes=(dkey,))

        def chunk_body(h, c, l):
            op, dma = lane_ops(l)
            (zT, delT, vsb, qn, kn, Et, Mt, R, gbt, osb, t1s, Stt, Mp, Np, attnf,
             attnT, kdec, kbeg, vbeta, Rb, wTn, vnew, Sb, onb, ss, rs) = unpack(l)
            own = c >= 16
            col = c * 8 + h
            cs = slice(c * 128, (c + 1) * 128)
            colap = lambda t: t[:, col:col + 1]
            op("act", lambda e, g_=colap(gt): e.activation(gbt, ones_f, AF.Copy, scale=g_), reads=("gt", "ones_f"), writes=("gb",))
            op("pe", lambda e: e.matmul(slot(0, 0), tri, gbt, start=True, stop=False), reads=("tri", "gb"), writes=(("ps", 0),))
            op("pe", lambda e: e.matmul(slot(0, 0), gbt, negtri, start=False, stop=False), reads=("negtri", "gb"), writes=(("ps", 0),))
            op("pe", lambda e: e.matmul(slot(0, 0), ident[:, :], masku, start=False, stop=True), reads=("ident", "masku"), writes=(("ps", 0),))
            op("act", lambda e: e.activation(Et, slot(0, 0), AF.Exp), reads=(("ps", 0),), writes=("E",))
            if c in (0, 16): P.marks.append(("c%d-E" % c, P.total))
            op("pe", lambda e, k_=kn[:, cs]: e.matmul(slot(1, 0), k_, k_, start=True, stop=True), reads=("kn",), writes=(("ps", 1),))
            op("dve", lambda e, b_=colap(beta): e.scalar_tensor_tensor(Mt, slot(1, 0), b_, Et, op0=ALU.mult, op1=ALU.mult),
                 reads=(("ps", 1), "beta", "E"), writes=("Mt",))
            op("dve", lambda e: e.tensor_mul(Mp[0], Mt, strict), reads=("Mt", "strict"), writes=(("Mp", 0),))
            op("pe", lambda e: e.transpose(slot(1, 1), Mp[0], ident[:, :]), reads=(("Mp", 0), "ident"), writes=(("ps", 1),))
            op("act", lambda e: e.copy(Np[0], slot(1, 1)), reads=(("ps", 1),), writes=(("Np", 0),))
            op("dve", lambda e: e.scalar_tensor_tensor(R, Np[0], -1.0, ident[:, :], op0=ALU.mult, op1=ALU.add), reads=(("Np", 0), "ident"), writes=("R",))
            if c in (0, 16): P.marks.append(("c%d-preDoubling" % c, P.total))
            for it in range(6):
                cur, nxt = it % 2, 1 - it % 2
                op("pe", lambda e, o=slot(2, cur), l=Np[cur], r=Mp[cur]: e.matmul(o, l, r, start=True, stop=True),
                     reads=(("Np", cur), ("Mp", cur)), writes=(("ps", 2),))
                op("act", lambda e, o=Mp[nxt], i=slot(2, cur): e.copy(o, i), reads=(("ps", 2),), writes=(("Mp", nxt),))
                if it < 5:
                    op("pe", lambda e, o=slot(3, cur), l=Mp[cur], r=Np[cur]: e.matmul(o, l, r, start=True, stop=True),
                         reads=(("Np", cur), ("Mp", cur)), writes=(("ps", 3),))
                    op("dve", lambda e, o=Np[nxt], i=slot(3, cur): e.tensor_copy(o, i), reads=(("ps", 3),), writes=(("Np", nxt),))
                op("pe", lambda e, o=slot(4, cur), l=Mp[nxt]: e.matmul(o, l, R, start=True, stop=True),
                     reads=(("Mp", nxt), "R"), writes=(("ps", 4),))
                op("dve", lambda e, i=slot(4, cur): e.tensor_add(R, i, R), reads=(("ps", 4), "R"), writes=("R",))
            op("act", lambda e: e.copy(Rb, R), reads=("R",), writes=("Rb",))
            if own:
                op("pe", lambda e, q_=qn[:, cs], k_=kn[:, cs]: e.matmul(slot(5, 0), q_, k_, start=True, stop=True),
                     reads=("qn", "kn"), writes=(("ps", 5),))
                op("dve", lambda e: e.tensor_mul(attnf, slot(5, 0), Et), reads=(("ps", 5), "E"), writes=("attnf",))
                op("pe", lambda e: e.transpose(slot(5, 1), attnf, ident[:, :]), reads=("attnf", "ident"), writes=(("ps", 5),))
                op("act", lambda e: e.copy(attnT, slot(5, 1)), reads=(("ps", 5),), writes=("attnT",))
            if c in (0, 16): P.marks.append(("c%d-preKV" % c, P.total))
            op("pe", lambda e, k_=kn[:, cs]: e.transpose(slot(5, 2), k_, ident[:, :]), reads=("kn", "ident"), writes=(("ps", 5),))
            op("dve", lambda e, s_=colap(eGlG): e.tensor_scalar_mul(kdec, slot(5, 2), s_), reads=(("ps", 5), "eGlG"), writes=("kdec",))
            op("dve", lambda e, s_=colap(skb): e.tensor_scalar_mul(kbeg, slot(5, 2), s_), reads=(("ps", 5), "skb"), writes=("kbeg",))
            op("pe", lambda e, v_=vsb[:, cs]: e.transpose(slotb(5, 3), v_, identb), reads=("vsb", "identb"), writes=(("ps", 5),))
            op("dve", lambda e, s_=colap(beta): e.tensor_scalar_mul(vbeta, slotb(5, 3), s_), reads=(("ps", 5), "beta"), writes=("vbeta",))
            if c in (0, 16): P.marks.append(("c%d-preW" % c, P.total))
            op("pe", lambda e: e.matmul(slot(6, 0), kbeg, Rb, start=True, stop=True), reads=("kbeg", "Rb"), writes=(("ps", 6),))
            op("act", lambda e: e.activation(wTn, slot(6, 0), AF.Copy, scale=-1.0), reads=(("ps", 6),), writes=("wTn",))
            op("pe", lambda e: e.matmul(slot(6, 1), Rb, vbeta, start=True, stop=False), reads=("Rb", "vbeta"), writes=(("ps", 6),))
            op("pe", lambda e: e.matmul(slot(6, 1), wTn, Sb, start=False, stop=True), reads=("wTn", "Sb"), writes=(("ps", 6),))
            op("dve", lambda e: e.tensor_copy(vnew, slot(6, 1)), reads=(("ps", 6),), writes=("vnew",))
            if own:
                op("pe", lambda e, q_=qn[:, cs]: e.matmul(slot(6, 2), q_, Stt, start=True, stop=True), reads=("qn", "S"), writes=(("ps", 6),))
                op("pe", lambda e: e.matmul(slot(6, 3), attnT, vnew, start=True, stop=True), reads=("attnT", "vnew"), writes=(("ps", 6),))
                op("dve", lambda e, s_=colap(eG): e.tensor_scalar_mul(t1s, slot(6, 2), s_), reads=(("ps", 6), "eG"), writes=("t1",))
                op("dve", lambda e: e.tensor_add(osb, slot(6, 3), t1s), reads=("t1", ("ps", 6)), writes=("osb",))
            if c in (0, 16): P.marks.append(("c%d-preState" % c, P.total))
            op("pe", lambda e: e.matmul(slot(7, 0), kdec, vnew, start=True, stop=True), reads=("kdec", "vnew"), writes=(("ps", 7),))
            op("dve", lambda e, s_=colap(cdt): e.tensor_scalar_mul(Stt, Stt, s_), reads=("S", "cdt"), writes=("S",))
            op("dve", lambda e: e.tensor_add(Stt, slot(7, 0), Stt), reads=("S", ("ps", 7)), writes=("S",))
            op("act", lambda e: e.copy(Sb, Stt), reads=("S",), writes=("Sb",))
            if c in (0, 16): P.marks.append(("c%d-preOut" % c, P.total))
            if own:
                oc = slice((c - 16) * 128, (c - 15) * 128)
                op("dve", lambda e: e.memset(ss, 0.0), writes=("ss",))
                op("act", lambda e: e.activation(t1s, osb, AF.Square, accum_out=ss), reads=("osb", "ss"), writes=("t1", "ss"))
                op("dve", lambda e: e.tensor_scalar(rs, ss, 1.0 / 128, EPS, op0=ALU.mult, op1=ALU.add), reads=("ss",), writes=("rs",))
                op("act", lambda e: e.activation(rs, rs, AF.Sqrt), reads=("rs",), writes=("rs",))
                op("dve", lambda e: e.reciprocal(rs, rs), reads=("rs",), writes=("rs",))
                op("dve", lambda e: e.scalar_tensor_tensor(onb, osb, rs, onorm, op0=ALU.mult, op1=ALU.mult),
                     reads=("osb", "rs", "onorm"), writes=("onb",))
                op("pe", lambda e: e.transpose(slotb(7, 2), onb, identb), reads=("onb", "identb"), writes=(("ps", 7),))
                op("dve", lambda e, o=delT[:, oc], z_=zT[:, oc]: e.tensor_mul(o, slotb(7, 2), z_), reads=(("ps", 7), "zT"), writes=("delT",))

        for h0 in range(0, len(heads), 2):
            hp = heads[h0:h0 + 2]
            for l, h in enumerate(hp):
                prologue(h, l)
            P.barrier()
            for l, h in enumerate(hp):
                op, dma = lane_ops(l)
                T = LT[l]
                op("dve", lambda e, t=T["wt"][6]: e.memset(t, 0.0), writes=("S",))
                op("dve", lambda e, t=T["bt"][7]: e.memset(t, 0.0), writes=("Sb",))
            for c in range(32):
                for l, h in enumerate(hp):
                    chunk_body(h, c, l)
            for l, h in enumerate(hp):
                op, dma = lane_ops(l)
                dma("sp", lambda e, o=del_d[h], i=LT[l]["delT"]: e.dma_start(out=o, in_=i), reads=("delT",), writes=())
            P.barrier()

    if "E" in phases:
        KmT = kvm[:, 0:1024].rearrange("p (h m) -> p h m", m=256)
        Vm = kvm[:, 1024:2048].rearrange("p (s n) -> p s n", n=512)
        attA, delA = aT[:, 0:8, :], aT[:, 8:16, :]
        qxT, oxT = aT[:, 16:20, :], aT[:, 20:24, :]
        rr = misc[:, 0:512]
        for gi, gsrc in ((2, g_cross), (3, g_mem), (4, g_ffn2), (5, g_final)):
            P.dma("sp", lambda e, o=gains[:, gi, :], i=gsrc: e.dma_start(out=o, in_=i[:, :]), writes=("gains",))
        load_xT(0, src=memb, nsub=2)
        rms_to_uT(3, nt=256)
        for hb in range(2):
            s_, view = load_wcols(wkv, hb * 256, 256)
            for j in range(2):
                hd = hb * 2 + j
                for c in range(DC):
                    P.op("pe", lambda e, o=ps[j][:, 0:256], l=view[:, c, j * 128:(j + 1) * 128], r=uT[:, c, 0:256],
                         st=(c == 0), sp_=(c == DC - 1): e.matmul(o, l, r, start=st, stop=sp_),
                         reads=(("w", s_, c // 8), ("uT", c)), writes=(("ps", j),))
                evac_copy(KmT[:, hd, :], ps[j][:, 0:256], reads=(("ps", j),), writes=("kvm",))
        for blk in range(2):
            s_, view = load_wcols(wkv, 512 + blk * 256, 256)
            for sub in range(2):
                for c in range(DC):
                    P.op("pe", lambda e, o=ps[2 + sub][:, 0:256], l=uT[:, c, sub * 128:(sub + 1) * 128], r=view[:, c, :],
                         st=(c == 0), sp_=(c == DC - 1): e.matmul(o, l, r, start=st, stop=sp_),
                         reads=(("w", s_, c // 8), ("uT", c)), writes=(("ps", 2 + sub),))
                evac_copy(Vm[:, sub, blk * 256:(blk + 1) * 256], ps[2 + sub][:, 0:256], reads=(("ps", 2 + sub),), writes=("kvm",))
        for t in range(NOWN // TT):
            if t + (NTOK - NOWN) // TT not in tiles:
                continue
            ts_ = slice(t * TT, (t + 1) * TT)
            P.dma("sp", lambda e, i=att_d[:, :, ts_].rearrange("h p t -> p h t"): e.dma_start(out=attA, in_=i),
                  writes=tuple(("aT", f) for f in range(8)))
            P.dma("sp", lambda e, i=del_d[:, :, ts_].rearrange("h p t -> p h t"): e.dma_start(out=delA, in_=i),
                  writes=tuple(("aT", f) for f in range(8, 16)))
            P.dma("sp", lambda e, i=h1_d[:, :, ts_]: e.dma_start(out=xT[:, :, :], in_=i), writes=tuple(("xT", c) for c in range(DC)))
            for pair in range(8):
                sa, va = load_wcols(wbra, pair * 256, 256, nrc=8)
                sd, vd = load_wcols(wbrd, pair * 256, 256, nrc=8)
                for j in range(2):
                    c = pair * 2 + j
                    P.dma("sp", lambda e, o=ostage[:, 0 + j, :], i=ga_d[c][:, ts_]: e.dma_start(out=o, in_=i), writes=(("ost", 0 + j),))
                    P.dma("sp", lambda e, o=ostage[:, 2 + j, :], i=gb_d[c][:, ts_]: e.dma_start(out=o, in_=i), writes=(("ost", 2 + j),))
                    for hh in range(8):
                        P.op("pe", lambda e, o=ps[j][:, :], l=va[:, hh, j * 128:(j + 1) * 128], r=attA[:, hh, :],
                             st=(hh == 0), sp_=(hh == 7): e.matmul(o, l, r, start=st, stop=sp_),
                             reads=(("w", sa, 0), ("aT", hh)), writes=(("ps", j),))
                    for hh in range(8):
                        P.op("pe", lambda e, o=ps[2 + j][:, :], l=vd[:, hh, j * 128:(j + 1) * 128], r=delA[:, hh, :],
                             st=(hh == 0), sp_=(hh == 7): e.matmul(o, l, r, start=st, stop=sp_),
                             reads=(("w", sd, 0), ("aT", 8 + hh)), writes=(("ps", 2 + j),))
                    P.op("dve", lambda e, o=sg[:, 0, :], i=ps[j][:, :], g_=ostage[:, 0 + j, :]: e.tensor_tensor(o, i, g_, op=ALU.mult),
                         reads=(("ps", j), ("ost", 0 + j)), writes=(("sg", 0),))
                    P.op("dve", lambda e, o=sg[:, 1, :], i=ps[2 + j][:, :], g_=ostage[:, 2 + j, :]: e.tensor_tensor(o, i, g_, op=ALU.mult),
                         reads=(("ps", 2 + j), ("ost", 2 + j)), writes=(("sg", 1),))
                    P.op("pool", lambda e, o=uT[:, c, :]: e.tensor_tensor(o, sg[:, 0, :], sg[:, 1, :], op=ALU.add),
                         reads=(("sg", 0), ("sg", 1)), writes=(("uT", c),))

            def proj_add(w2d, nrc, rhs_of, rkeys):
                for pair in range(8):
                    s_, view = load_wcols(w2d, pair * 256, 256, nrc=nrc)
                    for j in range(2):
                        c = pair * 2 + j
                        pb = 4 + c % 2
                        for k in range(nrc):
                            P.op("pe", lambda e, o=ps[pb][:, :], l=view[:, k, j * 128:(j + 1) * 128], r=rhs_of(k),
                                 st=(k == 0), sp_=(k == nrc - 1): e.matmul(o, l, r, start=st, stop=sp_),
                                 reads=(("w", s_, (k * 256) // 2048), rkeys(k)), writes=(("ps", pb),))
                        P.op("dve", lambda e, o=xT[:, c, :], i=ps[pb][:, :]: e.tensor_add(o, i, o),
                             reads=(("ps", pb), ("xT", c)), writes=(("xT", c),))
            proj_add(wout, DC, lambda k: uT[:, k, :], lambda k: ("uT", k))
            rms_to_uT(2)
            for pair in range(2):
                s_, view = load_wcols(wq, pair * 256, 256)
                for j in range(2):
                    hd = pair * 2 + j
                    for c in range(DC):
                        P.op("pe", lambda e, o=ps[j][:, :], l=view[:, c, j * 128:(j + 1) * 128], r=uT[:, c, :],
                             st=(c == 0), sp_=(c == DC - 1): e.matmul(o, l, r, start=st, stop=sp_),
                             reads=(("w", s_, c // 8), ("uT", c)), writes=(("ps", j),))
                    evac_copy(qxT[:, hd, :], ps[j][:, :], reads=(("ps", j),), writes=(("aT", 16 + hd),))
            for hd in range(4):
                for mc in range(2):
                    P.op("pe", lambda e, o=ps[mc][:, :], l=KmT[:, hd, mc * 128:(mc + 1) * 128], r=qxT[:, hd, :]:
                         e.matmul(o, l, r, start=True, stop=True), reads=("kvm", ("aT", 16 + hd)), writes=(("ps", mc),))
                    P.op("act", lambda e, o=aT[:, 24 + mc, :], i=ps[mc][:, :]: e.activation(o, i, AF.Exp, scale=SCALE),
                         reads=(("ps", mc),), writes=(("aT", 24 + mc),))
                    P.op("pe", lambda e, l=Vm[:, mc, hd * 128:(hd + 1) * 128], r=aT[:, 24 + mc, :], st=(mc == 0), sp_=(mc == 1):
                         e.matmul(ps[2][:, :], l, r, start=st, stop=sp_), reads=("kvm", ("aT", 24 + mc)), writes=(("ps", 2),))
                    P.op("pe", lambda e, r=aT[:, 24 + mc, :], st=(mc == 0), sp_=(mc == 1):
                         e.matmul(ps[3][:, :], ones_bf[:, :], r, start=st, stop=sp_), reads=("ones", ("aT", 24 + mc)), writes=(("ps", 3),))
                P.op("dve", lambda e: e.reciprocal(rr, ps[3][:, :]), reads=(("ps", 3),), writes=("rr",))
                P.op("dve", lambda e, o=oxT[:, hd, :]: e.tensor_tensor(o, ps[2][:, :], rr, op=ALU.mult),
                     reads=(("ps", 2), "rr"), writes=(("aT", 20 + hd),))
            proj_add(wo, 4, lambda k: oxT[:, k, :], lambda k: ("aT", 20 + k))
            rms_to_uT(4)
            ffn(w2g, w2u, w2d)
            rms_to_uT(5, inplace=True)
            for sub in range(4):
                for cg in range(4):
                    pb = ps[4 + cg % 2]
                    for j in range(4):
                        c = cg * 4 + j
                        P.op("pe", lambda e, o=pb[:, j * 128:(j + 1) * 128], i=xT[:, c, sub * 128:(sub + 1) * 128]:
                             e.transpose(o, i, ident[:, :]), reads=(("xT", c), "ident"), writes=(("ps", 4 + cg % 2),))
                    evac_copy(xtok[:, sub % 2, cg * 512:(cg + 1) * 512], pb[:, :], reads=(("ps", 4 + cg % 2),), writes=(("xtok", sub % 2),))
                P.dma("sp", lambda e, o=out[t * TT + sub * 128: t * TT + (sub + 1) * 128, :], i=xtok[:, sub % 2, :]:
                      e.dma_start(out=o, in_=i), reads=(("xtok", sub % 2),), writes=())
        P.barrier()

    if dbg:
        for nm, src in (("d_h1", h1_d), ("d_kT", kT_d), ("d_v", v_d), ("d_ba", ba_d), ("d_ga", ga_d)):
            P.dma("sp", lambda e, o=dbg_out[nm], i=src: e.dma_start(out=o, in_=i), writes=())
        P.barrier()

    if _os.environ.get("OPMARKS"):
        print("MARKS", P.marks, "total", P.total)
    P.replay(nc, sems)
    es.close()
    return nc


def _bf(a):
    return np.asarray(a, dtype=np.float32).astype(ml_dtypes.bfloat16)


def _gain_layout(g):
    return np.ascontiguousarray(np.asarray(g, np.float32).reshape(DC, 128).T)


def make_in_maps(inputs):
    x = np.asarray(inputs["x"], np.float32)
    shared = {
        "ffn1_w_gate": np.asarray(inputs["ffn1_w_gate"], np.float32)[0],
        "ffn1_w_up": np.asarray(inputs["ffn1_w_up"], np.float32)[0],
        "ffn1_w_down": np.asarray(inputs["ffn1_w_down"], np.float32)[0],
        "w_in": np.asarray(inputs["w_in"], np.float32)[0],
        "g_ffn1": _gain_layout(inputs["ffn1_norm"][0]),
        "g_mix": _gain_layout(inputs["mix_norm"][0]),
        "g_cross": _gain_layout(inputs["cross_norm"][0]),
        "g_mem": _gain_layout(inputs["mem_norm"][0]),
        "g_ffn2": _gain_layout(inputs["ffn2_norm"][0]),
        "g_final": _gain_layout(inputs["final_norm"]),
        "w_branch_attn": np.asarray(inputs["w_branch_attn"], np.float32)[0],
        "w_branch_delta": np.asarray(inputs["w_branch_delta"], np.float32)[0],
        "w_out": np.asarray(inputs["w_out"], np.float32)[0],
        "cross_wq": np.asarray(inputs["cross_wq"], np.float32)[0],
        "cross_wkv": np.asarray(inputs["cross_wkv"], np.float32)[0],
        "cross_wo": np.asarray(inputs["cross_wo"], np.float32)[0],
        "ffn2_w_gate": np.asarray(inputs["ffn2_w_gate"], np.float32)[0],
        "ffn2_w_up": np.asarray(inputs["ffn2_w_up"], np.float32)[0],
        "ffn2_w_down": np.asarray(inputs["ffn2_w_down"], np.float32)[0],
        "c_ident": np.eye(128, dtype=np.float32),
        "c_relb": np.ascontiguousarray(np.broadcast_to(np.asarray(inputs["rel_bias"], np.float32).reshape(1, 256), (128, 256))),
    }
    shared.update(gdn_consts(inputs))
    maps = []
    for c in range(8):
        b, s = c // 2, c % 2
        xin = np.zeros((NTOK, D), np.float32)
        if s == 1:
            xin[:] = x[b]
        else:
            xin[NOWN:] = x[b, :NOWN]
        m = dict(shared)
        m["xin"] = xin
        m["memb"] = np.ascontiguousarray(np.asarray(inputs["mem"], np.float32)[b])
        m.update(moba_consts(s))
        maps.append(m)
    return maps


def kernel(**inputs):
    nc = build()
    maps = make_in_maps(inputs)
    res = run_bass_kernel_spmd(nc, maps, core_ids=list(range(8)))
    outp = np.zeros((4, 4096, D), np.float32)
    for c in range(8):
        b, s = c // 2, c % 2
        outp[b, s * NOWN:(s + 1) * NOWN] = res.results[c]["out"]
    return outp
```

```python
import math
import numpy as np
import ml_dtypes
import concourse.bass as bass
import concourse.mybir as mybir
from concourse.bass_utils import run_bass_kernel_spmd

F32 = mybir.dt.float32
BF16 = mybir.dt.bfloat16
I32 = mybir.dt.int32
AF = mybir.ActivationFunctionType
ALU = mybir.AluOpType
AX = mybir.AxisListType

D = 2048
DC = 16
DFF = 5632
FC = 44
NTOK = 4096
NOWN = 2048
TT = 512
NTILE = NTOK // TT
EPS = 1e-6
BIG = 30000.0
NDS = 16


class Prog:
    ENG = ("pe", "dve", "act", "pool", "sp")

    def __init__(self):
        self.streams = {e: [] for e in self.ENG}
        self.cnt = {e: 0 for e in self.ENG}
        self.seen = {e: {} for e in self.ENG}
        self.state = {}
        self.dma_q = {}
        self.total = 0
        self.limit = None
        self.marks = []

    def _wait(self, eng, tok):
        s, v = tok
        if self.seen[eng].get(s, 0) >= v:
            return
        self.seen[eng][s] = v
        self.streams[eng].append(("wait", s, v))

    def _deps(self, reads, writes):
        deps = []
        for k in reads:
            st = self.state.get(k)
            if st is not None and st[0] is not None:
                deps.append(st[0])
        for k in writes:
            st = self.state.get(k)
            if st is not None:
                if st[0] is not None:
                    deps.append(st[0])
                deps.extend(st[1])
        return deps

    def _commit(self, tok, reads, writes):
        for k in reads:
            st = self.state.setdefault(k, [None, []])
            st[1] = [t for t in st[1] if t[0] != tok[0]] + [tok]
        for k in writes:
            self.state[k] = [tok, []]

    def _maxdeps(self, reads, writes):
        best = {}
        for s_, v in self._deps(reads, writes):
            if v > best.get(s_, 0):
                best[s_] = v
        return list(best.items())

    def op(self, eng, fn, reads=(), writes=()):
        self.total += 1
        if self.limit is not None and self.total > self.limit:
            return
        for tok in self._maxdeps(reads, writes):
            if tok[0] == eng and eng == "pe":
                continue
            self._wait(eng, tok)
        self.cnt[eng] += 1
        tok = (eng, self.cnt[eng])
        self.streams[eng].append(("op", fn, eng, 1))
        self._commit(tok, reads, writes)

    def dma(self, q, fn, reads=(), writes=()):
        self.total += 1
        if self.limit is not None and self.total > self.limit:
            return
        i = self.dma_q.get(q, 0)
        self.dma_q[q] = i + 1
        s = "dma_%s%d" % (q, i % NDS)
        v = 16 * (i // NDS + 1)
        if i >= NDS:
            self._wait(q, (s, v - 16))
        for tok in self._maxdeps(reads, writes):
            self._wait(q, tok)
        self.streams[q].append(("op", fn, s, 16))
        self._commit((s, v), reads, writes)

    def barrier(self):
        toks = [(e, self.cnt[e]) for e in self.ENG if self.cnt[e] > 0]
        for q, n in self.dma_q.items():
            for j in range(min(n, NDS)):
                last_i = ((n - 1 - j) // NDS) * NDS + j
                toks.append(("dma_%s%d" % (q, j), 16 * (last_i // NDS + 1)))
        for e in self.ENG:
            for t in toks:
                self._wait(e, t)
        self.state = {}

    def replay(self, nc, sems):
        engs = {}

        def run(name, eng):
            for it in self.streams[name]:
                if it[0] == "wait":
                    eng.wait_ge(sems[it[1]], it[2])
                else:
                    ins = it[1](eng)
                    ins.then_inc(sems[it[2]], it[3])

        with nc.Block() as block:
            @block.tensor
            def _(e):
                run("pe", e)

            @block.vector
            def _(e):
                run("dve", e)

            @block.scalar
            def _(e):
                run("act", e)

            @block.gpsimd
            def _(e):
                run("pool", e)

            @block.sync
            def _(e):
                run("sp", e)


def t5_thresholds():
    d = np.arange(0, 2048)
    nf = np.maximum(d, 1).astype(np.float32)
    large = 16 + (np.log(nf / np.float32(16)) / np.float32(math.log(128 / 16)) * np.float32(16)).astype(np.int32)
    large = np.minimum(large, 31)
    b = np.where(d < 16, d, large)
    return [int(np.argmax(b >= r)) for r in range(32)]


def moba_consts(s_role):
    dist = np.zeros((128, 4, 256), np.float32)
    for kc in range(4):
        dist[:, kc, :] = 256 + np.arange(256)[None, :] - 128 * kc - np.arange(128)[:, None]
    maskown = np.zeros((128, 8, 2, 16), np.float32)
    farmask = np.zeros((128, 8, 2, 16), np.float32)
    for qb in range(8):
        own = 8 + qb
        maskown[:, qb, :, own:] = -BIG
        if s_role == 0:
            maskown[:, qb, :, 0:8] = -BIG
        farmask[:, qb, :, 0:max(own - 1, 0)] = 1.0
    esel = np.zeros((16, 16, 128), np.float32)
    for n in range(16):
        esel[n, n, :] = 1.0
    return {"c_dist": dist.reshape(128, 1024), "c_maskown": maskown.reshape(128, 256),
            "c_farmask": farmask.reshape(128, 256), "c_esel": esel.reshape(16, 2048)}


def gdn_consts(inputs):
    i = np.arange(128)
    tri = (i[:, None] <= i[None, :]).astype(np.float32)
    masku = np.where(i[None, :] > i[:, None], -BIG, 0.0).astype(np.float32)
    strict = (i[:, None] > i[None, :]).astype(np.float32)
    cw = np.asarray(inputs["gdn_conv"], np.float32)[0]
    conv = np.ascontiguousarray(cw.T.reshape(24, 128, 4).transpose(1, 0, 2).reshape(128, 96))
    rep = lambda v, n: np.ascontiguousarray(np.broadcast_to(np.asarray(v, np.float32).reshape(1, n), (128, n)))
    return {"c_tri": tri, "c_masku": masku, "c_strict": strict, "c_onorm": rep(inputs["gdn_out_norm"][0], 128),
            "c_conv": conv, "c_alog": rep(inputs["gdn_a_log"][0], 8), "c_dtb": rep(inputs["gdn_dt_bias"][0], 8)}


IN_ATT_Q, IN_ATT_K, IN_ATT_V = 0, 1024, 2048
IN_GDN_Q, IN_GDN_K, IN_GDN_V = 3072, 4096, 5120
IN_Z, IN_BA, IN_GA, IN_GB = 6144, 7168, 7184, 9232


def build(dbg=False, tiles=None, phases="ABCDE", tiny_w=False, heads=None):
    heads = list(range(8)) if heads is None else heads
    tiles = list(range(NTILE)) if tiles is None else tiles
    from contextlib import ExitStack
    nc = bass.Bass("TRN2", target_bir_lowering=False)
    P = Prog()
    import os as _os
    if _os.environ.get("OPLIM"):
        P.limit = int(_os.environ["OPLIM"])
    es = ExitStack()

    def din(name, shape, dt=F32):
        import os
        if "noin" in os.environ.get("BIS", "") and name != "xin" and name not in os.environ.get("KEEP", "").split(","):
            class _D:
                def rearrange(self, *a, **k): return self
                def __getitem__(self, k): return self
            return _D()
        return nc.dram_tensor(name, list(shape), dt, kind="ExternalInput").ap()

    import os
    BIS0 = os.environ.get("BIS", "")

    def dscr(name, shape, dt):
        if "nodscr" in BIS0:
            return None
        return nc.dram_tensor(name, list(shape), dt).ap()

    def sb(name, shape, dt):
        if "nosb" in BIS0 and name != "xtok":
            return None
        return es.enter_context(nc.sbuf_tensor(name, list(shape), dt))

    xin = din("xin", [128, D] if tiny_w else [NTOK, D])
    if tiny_w:
        w1g, w1u, w1d, w_in = (din(n, [128, 128]) for n in ("ffn1_w_gate", "ffn1_w_up", "ffn1_w_down", "w_in"))
    else:
        w1g, w1u, w1d = din("ffn1_w_gate", [D, DFF]), din("ffn1_w_up", [D, DFF]), din("ffn1_w_down", [DFF, D])
        w_in = din("w_in", [D, 11280])
    g_ffn1, g_mix = din("g_ffn1", [128, DC]), din("g_mix", [128, DC])
    c_ident = din("c_ident", [128, 128])
    c_dist = din("c_dist", [128, 1024])
    c_maskown = din("c_maskown", [128, 256])
    c_farmask = din("c_farmask", [128, 256])
    c_relb = din("c_relb", [128, 256])
    c_esel = din("c_esel", [16, 2048])
    if tiny_w:
        wbra, wbrd, wout, wq, wkv, wo, w2g, w2u, w2d = (din(n, [128, 128]) for n in (
            "w_branch_attn", "w_branch_delta", "w_out", "cross_wq", "cross_wkv", "cross_wo", "ffn2_w_gate", "ffn2_w_up", "ffn2_w_down"))
    else:
        wbra, wbrd, wout = din("w_branch_attn", [1024, D]), din("w_branch_delta", [1024, D]), din("w_out", [D, D])
        wq, wkv, wo = din("cross_wq", [D, 512]), din("cross_wkv", [D, 1024]), din("cross_wo", [512, D])
        w2g, w2u, w2d = din("ffn2_w_gate", [D, DFF]), din("ffn2_w_up", [D, DFF]), din("ffn2_w_down", [DFF, D])
    g_cross, g_mem, g_ffn2, g_final = (din(n, [128, DC]) for n in ("g_cross", "g_mem", "g_ffn2", "g_final"))
    memb = din("memb", [256, D])
    c_tri, c_masku, c_strict, c_onorm = (din(n, [128, 128]) for n in ("c_tri", "c_masku", "c_strict", "c_onorm"))
    c_conv = din("c_conv", [128, 96])
    c_alog, c_dtb = din("c_alog", [128, 8]), din("c_dtb", [128, 8])
    del_d = dscr("del_d", [8, 128, NOWN], BF16)
    out = nc.dram_tensor("out", [NOWN, D], F32, kind="ExternalOutput").ap()
    att_d = dscr("att_d", [8, 128, NOWN], BF16)

    h1_d = dscr("h1_d", [128, DC, NOWN], F32)
    qT_d = dscr("qT_d", [8, 128, NOWN], BF16)
    kT_d = dscr("kT_d", [8, 128, NTOK], BF16)
    v_d = dscr("v_d", [NTOK, 1024], BF16)
    gq_d = dscr("gq_d", [8, 128, NTOK], BF16)
    gk_d = dscr("gk_d", [8, 128, NTOK], BF16)
    gv_d = dscr("gv_d", [8, 128, NTOK], BF16)
    z_d = dscr("z_d", [8, 128, NOWN], BF16)
    ba_d = dscr("ba_d", [NTOK, 16], F32)
    ga_d = dscr("ga_d", [16, 128, NOWN], BF16)
    gb_d = dscr("gb_d", [16, 128, NOWN], BF16)
    dbg_out = {}
    if dbg:
        dbg_out["d_h1"] = nc.dram_tensor("d_h1", [128, DC, NOWN], F32, kind="ExternalOutput").ap()
        dbg_out["d_kT"] = nc.dram_tensor("d_kT", [8, 128, NTOK], BF16, kind="ExternalOutput").ap()
        dbg_out["d_v"] = nc.dram_tensor("d_v", [NTOK, 1024], BF16, kind="ExternalOutput").ap()
        dbg_out["d_ba"] = nc.dram_tensor("d_ba", [NTOK, 16], F32, kind="ExternalOutput").ap()
        dbg_out["d_ga"] = nc.dram_tensor("d_ga", [16, 128, NOWN], BF16, kind="ExternalOutput").ap()

    ident = sb("ident", [128, 128], F32)
    ones_bf = sb("ones_bf", [128, 128], BF16)
    identb_t = sb("identb", [128, 128], BF16)
    identb = identb_t[:, :]
    gains = sb("gains", [128, 8, DC], F32)
    xtok = sb("xtok", [128, 2, D], F32)
    xT = sb("xT", [128, DC, TT], F32)
    uT = sb("uT", [128, DC, TT], BF16)
    aT = sb("aT", [128, FC, TT], BF16)
    wring = sb("wring", [128, 4, 4096], BF16)
    wba = sb("wba", [128, DC, 16], BF16)
    sq = sb("sq", [128, 2, TT], BF16)
    rstd = sb("rstd", [128, TT], F32)
    sg = sb("sg", [128, 2, TT], F32)
    ostage = sb("ostage", [128, 4, TT], BF16)
    bastage = sb("bastage", [128, 4, 16], F32)
    kms = sb("kms", [128, 8, 16], F32)
    misc = sb("misc", [128, 4096], F32)
    kvm = sb("kvm", [128, 2048], BF16)
    esel = sb("esel", [16, 2048], F32)
    import os
    nps = 4 if "ps4" in os.environ.get("BIS", "") else 8
    ps = [es.enter_context(nc.psum_tensor("ps%d" % i, [128, 512], F32)) for i in range(nps)]

    sems = {e: es.enter_context(nc.semaphore("s_" + e)) for e in Prog.ENG}
    for q in ("sp", "pool"):
        for j in range(NDS):
            sems["dma_%s%d" % (q, j)] = es.enter_context(nc.semaphore("s_dma_%s%d" % (q, j)))

    cp_rr = [0]

    def evac_copy(out_ap, in_ap, reads, writes):
        cp_rr[0] ^= 1
        if cp_rr[0]:
            P.op("dve", lambda e, o=out_ap, i=in_ap: e.tensor_copy(o, i), reads, writes)
        else:
            P.op("act", lambda e, o=out_ap, i=in_ap: e.copy(o, i), reads, writes)

    wr_i = [0]

    def wslot():
        s = wr_i[0] % 4
        wr_i[0] += 1
        return s

    def load_wcols(w2d, col0, ncols, nrc=DC):
        s = wslot()
        view = wring[:, s, 0:nrc * ncols].rearrange("p (c n) -> p c n", n=ncols)
        src = w2d.rearrange("(c p) n -> p c n", p=128)
        hs = max(nrc // 2, 1)
        for half in range(2):
            rows = range(half * hs, min(half * hs + hs, nrc))
            wkeys = tuple(sorted({("w", s, (r * ncols) // 2048) for r in rows}))
            P.dma("pool", lambda e, o=view[:, half * hs:half * hs + hs, :],
                  i=src[:, half * hs:half * hs + hs, col0:col0 + ncols]: e.dma_start(out=o, in_=i),
                  reads=(), writes=wkeys)
        return s, view

    def rms_to_uT(gidx, nt=TT, inplace=False):
        for c in range(DC):
            P.op("act", lambda e, o=sq[:, c % 2, 0:nt], i=xT[:, c, 0:nt]: e.activation(o, i, AF.Square),
                 reads=(("xT", c),), writes=(("sq", c % 2),))
            P.op("pe", lambda e, o=ps[6][:, 0:nt], r=sq[:, c % 2, 0:nt], st=(c == 0), sp=(c == DC - 1):
                 e.matmul(o, ones_bf[:, :], r, start=st, stop=sp),
                 reads=(("sq", c % 2), "ones"), writes=(("ps", 6),))
        P.op("dve", lambda e: e.tensor_scalar(rstd[:, 0:nt], ps[6][:, 0:nt], 1.0 / D, EPS, op0=ALU.mult, op1=ALU.add),
             reads=(("ps", 6),), writes=("rstd",))
        P.op("act", lambda e: e.activation(rstd[:, 0:nt], rstd[:, 0:nt], AF.Sqrt),
             reads=("rstd",), writes=("rstd",))
        P.op("dve", lambda e: e.reciprocal(rstd[:, 0:nt], rstd[:, 0:nt]),
             reads=("rstd",), writes=("rstd",))
        for c in range(DC):
            if inplace:
                P.op("dve", lambda e, o=xT[:, c, 0:nt], g=gains[:, gidx, c:c + 1]:
                     e.scalar_tensor_tensor(o, o, g, rstd[:, 0:nt], op0=ALU.mult, op1=ALU.mult),
                     reads=(("xT", c), "rstd", "gains"), writes=(("xT", c),))
                continue
            P.op("dve", lambda e, o=uT[:, c, 0:nt], i=xT[:, c, 0:nt], g=gains[:, gidx, c:c + 1]:
                 e.scalar_tensor_tensor(o, i, g, rstd[:, 0:nt], op0=ALU.mult, op1=ALU.mult),
                 reads=(("xT", c), "rstd", "gains"), writes=(("uT", c),))

    def ffn(wg, wu, wd):
        for fb in range(FC // 2):
            sgi, vg = load_wcols(wg, fb * 256, 256)
            sui, vu = load_wcols(wu, fb * 256, 256)
            for j in range(2):
                f = fb * 2 + j
                pg, pu = ps[f % 2], ps[2 + f % 2]
                for c in range(DC):
                    P.op("pe", lambda e, o=pg[:, :], l=vg[:, c, j * 128:(j + 1) * 128], r=uT[:, c, :],
                         st=(c == 0), sp=(c == DC - 1): e.matmul(o, l, r, start=st, stop=sp),
                         reads=(("w", sgi, c // 8), ("uT", c)), writes=(("ps", f % 2),))
                for c in range(DC):
                    P.op("pe", lambda e, o=pu[:, :], l=vu[:, c, j * 128:(j + 1) * 128], r=uT[:, c, :],
                         st=(c == 0), sp=(c == DC - 1): e.matmul(o, l, r, start=st, stop=sp),
                         reads=(("w", sui, c // 8), ("uT", c)), writes=(("ps", 2 + f % 2),))
                P.op("act", lambda e, o=sg[:, f % 2, :], i=pg[:, :]: e.activation(o, i, AF.Silu),
                     reads=(("ps", f % 2),), writes=(("sg", f % 2),))
                P.op("dve", lambda e, o=aT[:, f, :], a=sg[:, f % 2, :], b=pu[:, :]: e.tensor_tensor(o, b, a, op=ALU.mult),
                     reads=(("sg", f % 2), ("ps", 2 + f % 2)), writes=(("aT", f),))
        wdv = wd.rearrange("(fc p) d -> p fc d", p=128)
        for dg in range(4):
            for fb in range(FC // 4):
                s = wslot()
                view = wring[:, s, 0:2048].rearrange("p (f n) -> p f n", n=512)
                P.dma("pool", lambda e, o=view, i=wdv[:, fb * 4:fb * 4 + 4, dg * 512:(dg + 1) * 512]:
                      e.dma_start(out=o, in_=i), reads=(), writes=(("w", s, 0), ("w", s, 1)))
                for j in range(4):
                    f = fb * 4 + j
                    for q in range(4):
                        P.op("pe", lambda e, o=ps[4 + q][:, :], l=view[:, j, q * 128:(q + 1) * 128], r=aT[:, f, :],
                             st=(f == 0), sp=(f == FC - 1): e.matmul(o, l, r, start=st, stop=sp),
                             reads=(("w", s, 0), ("w", s, 1), ("aT", f)), writes=(("ps", 4 + q),))
            for q in range(4):
                c = dg * 4 + q
                P.op("dve", lambda e, o=xT[:, c, :], i=ps[4 + q][:, :]:
                     e.scalar_tensor_tensor(o, i, 0.5, o, op0=ALU.mult, op1=ALU.add),
                     reads=(("ps", 4 + q), ("xT", c)), writes=(("xT", c),))

    def load_xT(tok0, src=None, nsub=4):
        src = xin if src is None else src
        for s in range(nsub):
            P.dma("sp", lambda e, o=xtok[:, s % 2, :], i=src[tok0 + s * 128: tok0 + (s + 1) * 128, :]:
                  e.dma_start(out=o, in_=i), reads=(), writes=(("xtok", s % 2),))
            for cg in range(4):
                pb = ps[4 + cg % 2]
                for j in range(4):
                    c = cg * 4 + j
                    P.op("pe", lambda e, o=pb[:, j * 128:(j + 1) * 128], i=xtok[:, s % 2, c * 128:(c + 1) * 128]:
                         e.transpose(o, i, ident[:, :]),
                         reads=(("xtok", s % 2), "ident"), writes=(("ps", 4 + cg % 2),))
                evac_copy(xT[:, cg * 4:cg * 4 + 4, s * 128:(s + 1) * 128],
                          pb[:, :].rearrange("p (c t) -> p c t", t=128),
                          reads=(("ps", 4 + cg % 2),), writes=tuple(("xT", cg * 4 + j) for j in range(4)))

    import os
    BIS = os.environ.get("BIS", "")
    if "mini" in BIS:
        P.dma("pool", lambda e: e.dma_start(out=xtok[:, 0, :], in_=xin[0:128, :]), writes=("xtok",))
        P.dma("pool", lambda e: e.dma_start(out=out[0:128, :], in_=xtok[:, 0, :]), reads=("xtok",))
        P.barrier()
    if "nosetup" not in BIS:
      P.dma("sp", lambda e: e.dma_start(out=ident[:, :], in_=c_ident[:, :]), writes=("ident",))
      P.op("dve", lambda e: e.memset(ones_bf[:, :], 1.0), writes=("ones",))
      P.op("dve", lambda e: e.tensor_copy(identb, ident[:, :]), reads=("ident",), writes=("identb",))
      P.dma("sp", lambda e: e.dma_start(out=gains[:, 0, :], in_=g_ffn1[:, :]), writes=("gains",))
      P.dma("sp", lambda e: e.dma_start(out=gains[:, 1, :], in_=g_mix[:, :]), writes=("gains",))
      P.op("dve", lambda e: e.memset(kms[:, :, :], 0.0), writes=("kms",))
    if "wout" in BIS:
      P.dma("sp", lambda e: e.dma_start(out=out[0:128, :], in_=xtok[:, 0, :]), writes=())
      P.barrier()

    ost_i = [0]

    def inproj_fm(col0, dest, tok0, ntok_dest_off, func=None, kmean_h=None, t=None):
        s, view = load_wcols(w_in, col0, 256)
        for j in range(2):
            pi = ost_i[0] % 4
            ost_i[0] += 1
            pb = ps[pi]
            for c in range(DC):
                P.op("pe", lambda e, o=pb[:, :], l=view[:, c, j * 128:(j + 1) * 128], r=uT[:, c, :],
                     st=(c == 0), sp=(c == DC - 1): e.matmul(o, l, r, start=st, stop=sp),
                     reads=(("w", s, c // 8), ("uT", c)), writes=(("ps", pi),))
            if func is None:
                evac_copy(ostage[:, pi, :], pb[:, :], reads=(("ps", pi),), writes=(("ost", pi),))
            else:
                P.op("act", lambda e, o=ostage[:, pi, :], i=pb[:, :]: e.activation(o, i, func),
                     reads=(("ps", pi),), writes=(("ost", pi),))
            if kmean_h is not None:
                h = kmean_h + j
                P.op("dve", lambda e, o=kms[:, h, 2 * t:2 * t + 2], i=pb[:, :].rearrange("p (b k) -> p b k", k=256):
                     e.reduce_sum(o, i, axis=AX.X), reads=(("ps", pi),), writes=("kms",))
            dch = dest[0][dest[1] + j]
            P.dma("sp", lambda e, o=dch[:, ntok_dest_off:ntok_dest_off + TT], i=ostage[:, pi, :]:
                  e.dma_start(out=o, in_=i), reads=(("ost", pi),), writes=())

    def inproj_tm_v(tok0):
        for blk in range(4):
            s, view = load_wcols(w_in, IN_ATT_V + blk * 256, 256)
            for sub in range(4):
                pi = ost_i[0] % 4
                ost_i[0] += 1
                pb = ps[pi]
                for c in range(DC):
                    P.op("pe", lambda e, o=pb[:, 0:256], l=uT[:, c, sub * 128:(sub + 1) * 128], r=view[:, c, :],
                         st=(c == 0), sp=(c == DC - 1): e.matmul(o, l, r, start=st, stop=sp),
                         reads=(("w", s, c // 8), ("uT", c)), writes=(("ps", pi),))
                evac_copy(ostage[:, pi, 0:256], pb[:, 0:256], reads=(("ps", pi),), writes=(("ost", pi),))
                P.dma("sp", lambda e, o=v_d[tok0 + sub * 128: tok0 + (sub + 1) * 128, blk * 256:(blk + 1) * 256],
                      i=ostage[:, pi, 0:256]: e.dma_start(out=o, in_=i), reads=(("ost", pi),), writes=())

    def inproj_ba(tok0):
        for sub in range(4):
            pi = ost_i[0] % 4
            ost_i[0] += 1
            pb = ps[pi]
            for c in range(DC):
                P.op("pe", lambda e, o=pb[:, 0:16], l=uT[:, c, sub * 128:(sub + 1) * 128], r=wba[:, c, :],
                     st=(c == 0), sp=(c == DC - 1): e.matmul(o, l, r, start=st, stop=sp),
                     reads=("wba", ("uT", c)), writes=(("ps", pi),))
            P.op("dve", lambda e, o=bastage[:, sub, :], i=pb[:, 0:16]: e.tensor_copy(o, i),
                 reads=(("ps", pi),), writes=(("bast", sub),))
            P.dma("sp", lambda e, o=ba_d[tok0 + sub * 128: tok0 + (sub + 1) * 128, :], i=bastage[:, sub, :]:
                  e.dma_start(out=o, in_=i), reads=(("bast", sub),), writes=())

    if "A" in phases:
        P.dma("pool", lambda e: e.dma_start(out=wba[:, :, :],
              in_=w_in.rearrange("(c p) n -> p c n", p=128)[:, :, IN_BA:IN_BA + 16]), writes=("wba",))
        for t in tiles:
            tok0 = t * TT
            own = tok0 >= NTOK - NOWN
            otok = tok0 - (NTOK - NOWN)
            load_xT(tok0)
            rms_to_uT(0)
            ffn(w1g, w1u, w1d)
            rms_to_uT(1)
            if own:
                P.dma("sp", lambda e, o=h1_d[:, :, otok:otok + TT]: e.dma_start(out=o, in_=xT[:, :, :]),
                      reads=tuple(("xT", c) for c in range(DC)), writes=())
            for hb in range(4):
                if own:
                    inproj_fm(IN_ATT_Q + hb * 256, (qT_d, hb * 2), tok0, otok)
                inproj_fm(IN_ATT_K + hb * 256, (kT_d, hb * 2), tok0, tok0)
                inproj_fm(IN_GDN_Q + hb * 256, (gq_d, hb * 2), tok0, tok0)
                inproj_fm(IN_GDN_K + hb * 256, (gk_d, hb * 2), tok0, tok0)
                inproj_fm(IN_GDN_V + hb * 256, (gv_d, hb * 2), tok0, tok0)
                if own:
                    inproj_fm(IN_Z + hb * 256, (z_d, hb * 2), tok0, otok, func=AF.Silu)
            inproj_tm_v(tok0)
            inproj_ba(tok0)
            if own:
                for gbk in range(8):
                    inproj_fm(IN_GA + gbk * 256, (ga_d, gbk * 2), tok0, otok, func=AF.Sigmoid)
                    inproj_fm(IN_GB + gbk * 256, (gb_d, gbk * 2), tok0, otok, func=AF.Sigmoid)
        P.barrier()

    SCALE = 128.0 ** -0.5
    aTf = aT[:, :, :].rearrange("p c t -> p (c t)")
    xTf = xT[:, :, :].rearrange("p c t -> p (c t)")
    if "C" in phases:
        KT = aTf[:, 0:4096]
        Vt = aTf[:, 4096:8192].rearrange("p (c d) -> p c d", d=128)
        qT = aTf[:, 8192:10240]
        pT = aTf[:, 10240:12288].rearrange("p (r q) -> p r q", q=256)
        kmb = aTf[:, 12288:12304]
        attT = aTf[:, 12544:14592]
        biasT = xTf.rearrange("p (h k q) -> p h k q", h=8, k=4)
        mo = [0]

        def mtile(n):
            a = mo[0]
            mo[0] += n
            assert mo[0] <= 4096
            return misc[:, a:a + n]
        distt = mtile(1024).rearrange("p (k q) -> p k q", q=256)
        indt = mtile(512).rearrange("p (k q) -> p k q", q=256)
        tmpS = mtile(512).rearrange("p (k q) -> p k q", q=256)
        maskown = mtile(256).rearrange("p (b n) -> p b n", n=32)
        farmask = mtile(256).rearrange("p (b n) -> p b n", n=32)
        relb = mtile(256).rearrange("p (r h) -> p r h", h=8)
        dbc = mtile(256).rearrange("p (r h) -> p r h", h=8)
        sc = mtile(32)
        m8 = mtile(16).rearrange("p (t e) -> p t e", e=8)
        tsel = mtile(32)
        selb = mtile(32)
        selbT = mtile(256)
        rsum = mtile(256)
        ksum = mtile(16)
        P.dma("sp", lambda e: e.dma_start(out=distt, in_=c_dist.rearrange("p (k q) -> p k q", q=256)), writes=("dist",))
        P.dma("sp", lambda e: e.dma_start(out=maskown, in_=c_maskown.rearrange("p (b n) -> p b n", n=32)), writes=("maskown",))
        P.dma("sp", lambda e: e.dma_start(out=farmask, in_=c_farmask.rearrange("p (b n) -> p b n", n=32)), writes=("farmask",))
        P.dma("sp", lambda e: e.dma_start(out=relb, in_=c_relb.rearrange("p (r h) -> p r h", h=8)), writes=("relb",))
        P.dma("sp", lambda e: e.dma_start(out=esel[0:16, :], in_=c_esel[:, :]), writes=("esel",))
        P.op("dve", lambda e: e.tensor_copy(dbc[:, 0:1, :], relb[:, 0:1, :]), reads=("relb",), writes=("dbc",))
        P.op("dve", lambda e: e.tensor_sub(dbc[:, 1:32, :], relb[:, 1:32, :], relb[:, 0:31, :]), reads=("relb",), writes=("dbc",))
        thr = t5_thresholds()
        for kc in range(4):
            for h in range(8):
                P.op("dve", lambda e, o=biasT[:, h, kc, :], i=distt[:, kc, :]:
                     e.tensor_scalar(o, i, 0.0, -BIG, op0=ALU.is_lt, op1=ALU.mult),
                     reads=("dist",), writes=(("bias", h, kc),))
            for r in range(32):
                P.op("dve", lambda e, o=indt[:, r % 2, :], i=distt[:, kc, :], th=float(thr[r]):
                     e.tensor_single_scalar(o, i, th, op=ALU.is_ge), reads=("dist",), writes=(("ind", r % 2),))
                for h in range(8):
                    P.op("dve", lambda e, o=biasT[:, h, kc, :], i=indt[:, r % 2, :], sc_=dbc[:, r, h:h + 1]:
                         e.scalar_tensor_tensor(o, i, sc_, o, op0=ALU.mult, op1=ALU.add),
                         reads=(("ind", r % 2), "dbc", ("bias", h, kc)), writes=(("bias", h, kc),))
        for h in heads:
            P.dma("sp", lambda e, i=kT_d[h]: e.dma_start(out=KT, in_=i), writes=("KT",))
            for g4 in range(4):
                P.dma("sp", lambda e, o=Vt[:, g4 * 8:(g4 + 1) * 8, :],
                      i=v_d[g4 * 1024:(g4 + 1) * 1024, h * 128:(h + 1) * 128].rearrange("(c p) d -> p c d", p=128):
                      e.dma_start(out=o, in_=i), writes=(("Vt", g4),))
            P.dma("sp", lambda e, i=qT_d[h]: e.dma_start(out=qT, in_=i), writes=("qT",))
            P.op("dve", lambda e: e.reduce_sum(ksum, KT.rearrange("p (b k) -> p b k", k=256), axis=AX.X),
                 reads=("KT",), writes=("ksum",))
            P.op("act", lambda e: e.activation(kmb, ksum, AF.Copy, scale=1.0 / 256), reads=("ksum",), writes=("kmb",))
            for qb in range(8):
                own = 8 + qb
                for t in range(2):
                    P.op("pe", lambda e, o=ps[7][:, t * 16:(t + 1) * 16], l=qT[:, qb * 256 + t * 128: qb * 256 + (t + 1) * 128]:
                         e.matmul(o, l, kmb, start=True, stop=True), reads=("qT", "kmb"), writes=(("ps", 7),))
                P.op("dve", lambda e, m=maskown[:, qb, :]: e.tensor_tensor(sc, ps[7][:, 0:32], m, op=ALU.add),
                     reads=(("ps", 7), "maskown"), writes=("sc",))
                for t in range(2):
                    P.op("dve", lambda e, o=m8[:, t, :], i=sc[:, t * 16:(t + 1) * 16]: e.max(o, i), reads=("sc",), writes=(("m8", t),))
                for t in range(2):
                    P.op("dve", lambda e, o=tsel[:, t * 16:(t + 1) * 16], i=sc[:, t * 16:(t + 1) * 16], th=m8[:, t, 2:3]:
                         e.tensor_scalar(o, i, th, 1.0, op0=ALU.is_ge, op1=ALU.subtract),
                         reads=("sc", ("m8", t)), writes=("tsel",))
                P.op("dve", lambda e, m=maskown[:, qb, :]: e.scalar_tensor_tensor(selb, tsel, BIG, m, op0=ALU.mult, op1=ALU.add),
                     reads=("tsel", "maskown"), writes=("selb",))
                P.op("dve", lambda e, f=farmask[:, qb, :], t31=relb[:, 31, h:h + 1]:
                     e.scalar_tensor_tensor(selb, f, t31, selb, op0=ALU.mult, op1=ALU.add),
                     reads=("selb", "farmask", "relb"), writes=("selb",))
                for t in range(2):
                    P.op("pe", lambda e, o=ps[6][0:16, t * 128:(t + 1) * 128], i=selb[:, t * 16:(t + 1) * 16]:
                         e.transpose(o, i, ident[:, :]), reads=("selb", "ident"), writes=(("ps", 6),))
                P.op("act", lambda e: e.activation(selbT[0:16, :], ps[6][0:16, 0:256], AF.Copy, scale=1.0 / SCALE),
                     reads=(("ps", 6),), writes=("selbT",))
                nch = 2 * (own + 1)
                po, pr = 2 + 2 * (qb % 2), 3 + 2 * (qb % 2)
                def qk_(ci):
                    n = ci // 2
                    pb = ps[ci % 2]
                    P.op("pe", lambda e, o=pb[:, 0:256], l=KT[:, ci * 128:(ci + 1) * 128], r=qT[:, qb * 256:(qb + 1) * 256],
                         sp_=(n == own): e.matmul(o, l, r, start=True, stop=sp_),
                         reads=("KT", "qT"), writes=(("ps", ci % 2),))
                    if n < own:
                        P.op("pe", lambda e, o=pb[:, 0:256], l=esel[0:16, n * 128:(n + 1) * 128]:
                             e.matmul(o, l, selbT[0:16, :], start=False, stop=True),
                             reads=("esel", "selbT"), writes=(("ps", ci % 2),))
                def ex_(ci):
                    n = ci // 2
                    pb = ps[ci % 2]
                    sl = ci % 8
                    if n >= own - 1:
                        kc = (n - (own - 1)) * 2 + ci % 2
                        P.op("dve", lambda e, o=tmpS[:, ci % 2, :], i=pb[:, 0:256], b=biasT[:, h, kc, :]:
                             e.scalar_tensor_tensor(o, i, SCALE, b, op0=ALU.mult, op1=ALU.add),
                             reads=(("ps", ci % 2), ("bias", h, kc)), writes=(("tmpS", ci % 2),))
                        P.op("act", lambda e, o=pT[:, sl, :], i=tmpS[:, ci % 2, :]: e.activation(o, i, AF.Exp),
                             reads=(("tmpS", ci % 2),), writes=(("pT", sl),))
                    else:
                        P.op("act", lambda e, o=pT[:, sl, :], i=pb[:, 0:256]: e.activation(o, i, AF.Exp, scale=SCALE),
                             reads=(("ps", ci % 2),), writes=(("pT", sl),))
                def pv_(ci):
                    sl = ci % 8
                    P.op("pe", lambda e, o=ps[po][:, 0:256], l=Vt[:, ci, :], r=pT[:, sl, :], st=(ci == 0), sp_=(ci == nch - 1):
                         e.matmul(o, l, r, start=st, stop=sp_), reads=(("Vt", ci // 8), ("pT", sl)), writes=(("ps", po),))
                    P.op("pe", lambda e, o=ps[pr][:, 0:256], r=pT[:, sl, :], st=(ci == 0), sp_=(ci == nch - 1):
                         e.matmul(o, ones_bf[:, :], r, start=st, stop=sp_), reads=("ones", ("pT", sl)), writes=(("ps", pr),))
                qk_(0)
                for ci in range(nch):
                    if ci + 1 < nch:
                        qk_(ci + 1)
                    ex_(ci)
                    pv_(ci)
                P.op("dve", lambda e, i=ps[pr][:, 0:256]: e.reciprocal(rsum, i), reads=(("ps", pr),), writes=("rsum",))
                P.op("dve", lambda e, o=attT[:, qb * 256:(qb + 1) * 256], i=ps[po][:, 0:256]:
                     e.tensor_tensor(o, i, rsum, op=ALU.mult), reads=(("ps", po), "rsum"), writes=("attT",))
            P.dma("sp", lambda e, o=att_d[h]: e.dma_start(out=o, in_=attT), reads=("attT",), writes=())
        P.barrier()

    if "D" in phases:
        uTf = uT[:, :, :].rearrange("p c t -> p (c t)")
        xtf = xtok[:, :, :].rearrange("p c t -> p (c t)")
        rawq, rawk, rawv = aTf[:, 0:4096], aTf[:, 4096:8192], aTf[:, 8192:12288]
        zT = aTf[:, 12288:14336]
        delT = aTf[:, 14336:16384]
        vsb = aTf[:, 16384:20480]
        qn, kn = xTf[:, 0:4096], xTf[:, 4096:8192]
        cacc = xtf[:, 0:4096]
        w32 = [xtf[:, i * 128:(i + 1) * 128] for i in range(32)]
        Et, Mt, R, gbt, osb, t1s, Stt = w32[0], w32[1], w32[2], w32[3], w32[4], w32[5], w32[6]
        Mp, Np = [w32[7], w32[8]], [w32[9], w32[10]]
        attnf = w32[11]
        b16 = [uTf[:, i * 128:(i + 1) * 128] for i in range(16)]
        attnT, kdec, kbeg, vbeta, Rb, wTn, vnew, Sb, onb = b16[0:9]
        mo = [0]

        def mt2(n):
            a = mo[0]
            mo[0] += n
            assert mo[0] <= 4096
            return misc[:, a:a + n]
        bat = mt2(512).rearrange("p (c n) -> p c n", n=16)
        beta, gt, Gt, Glt, eG, eGlG, cdt, skb = (mt2(256) for _ in range(8))
        tri, negtri, masku, strict, onorm, ones_f = (mt2(128) for _ in range(6))
        convw = mt2(96)
        alog, dtb, negA = mt2(8), mt2(8), mt2(8)
        ss, rs = mt2(1), mt2(1)
        v3 = lambda t: t.rearrange("p (c h) -> p c h", h=8)

        def slot(b, k):
            return ps[b][:, k * 128:(k + 1) * 128]

        def slotb(b, k):
            return ps[b][:, :].bitcast(BF16)[:, k * 256:k * 256 + 128]
        for dst, src, key in ((tri, c_tri, "tri"), (masku, c_masku, "masku"), (strict, c_strict, "strict"),
                              (onorm, c_onorm, "onorm"), (convw, c_conv, "convw"), (alog, c_alog, "alog"), (dtb, c_dtb, "dtb")):
            P.dma("sp", lambda e, o=dst, i=src: e.dma_start(out=o, in_=i[:, :]), writes=(key,))
        P.dma("sp", lambda e: e.dma_start(out=bat, in_=ba_d.rearrange("(c p) n -> p c n", p=128)), writes=("bat",))
        P.op("dve", lambda e: e.memset(ones_f, 1.0), writes=("ones_f",))
        P.op("dve", lambda e: e.tensor_scalar_mul(negtri, tri, -1.0), reads=("tri",), writes=("negtri",))
        P.op("act", lambda e: e.activation(v3(beta), bat[:, :, 0:8], AF.Sigmoid), reads=("bat",), writes=("beta",))
        for h in range(8):
            P.op("dve", lambda e, o=v3(gt)[:, :, h], i=bat[:, :, 8 + h], sc_=dtb[:, h:h + 1]: e.tensor_scalar_add(o, i, sc_),
                 reads=("bat", "dtb"), writes=("gt",))
        P.op("act", lambda e: e.activation(gt, gt, AF.Exp), reads=("gt",), writes=("gt",))
        P.op("dve", lambda e: e.tensor_scalar_add(gt, gt, 1.0), reads=("gt",), writes=("gt",))
        P.op("act", lambda e: e.activation(gt, gt, AF.Ln), reads=("gt",), writes=("gt",))
        P.op("act", lambda e: e.activation(negA, alog, AF.Exp), reads=("alog",), writes=("negA",))
        P.op("dve", lambda e: e.tensor_scalar_mul(negA, negA, -1.0), reads=("negA",), writes=("negA",))
        for h in range(8):
            P.op("dve", lambda e, o=v3(gt)[:, :, h], sc_=negA[:, h:h + 1]: e.tensor_scalar_mul(o, o, sc_),
                 reads=("gt", "negA"), writes=("gt",))
        P.op("pe", lambda e: e.matmul(ps[5][:, 0:256], tri, gt, start=True, stop=True), reads=("tri", "gt"), writes=(("ps", 5),))
        P.op("pe", lambda e: e.matmul(ps[5][:, 256:512], ones_f, gt, start=True, stop=True), reads=("ones_f", "gt"), writes=(("ps", 5),))
        P.op("dve", lambda e: e.tensor_copy(Gt, ps[5][:, 0:256]), reads=(("ps", 5),), writes=("Gt",))
        P.op("dve", lambda e: e.tensor_copy(Glt, ps[5][:, 256:512]), reads=(("ps", 5),), writes=("Glt",))
        P.op("act", lambda e: e.activation(eG, Gt, AF.Exp), reads=("Gt",), writes=("eG",))
        P.op("act", lambda e: e.activation(cdt, Glt, AF.Exp), reads=("Glt",), writes=("cdt",))
        P.op("dve", lambda e: e.tensor_sub(eGlG, Glt, Gt), reads=("Glt", "Gt"), writes=("eGlG",))
        P.op("act", lambda e: e.activation(eGlG, eGlG, AF.Exp), reads=("eGlG",), writes=("eGlG",))
        P.op("dve", lambda e: e.tensor_mul(skb, beta, eG), reads=("beta", "eG"), writes=("skb",))
        P.marks.append(("D-setup-end", P.total))
        P.barrier()
        SHARED = {"tri", "negtri", "masku", "strict", "onorm", "ones_f", "ones", "ident", "identb", "convw", "gt", "beta",
                  "eG", "eGlG", "cdt", "skb", "Gt", "Glt", "rstd", "cacc", "raw"}
        wrf = wring[:, :, :].rearrange("p s n -> p (s n)").bitcast(F32)
        rawb = uTf[:, 0:4096]
        LT = []
        for l_ in range(2):
            qn_l, kn_l = (xTf[:, 0:4096], xTf[:, 4096:8192]) if l_ == 0 else (wrf[:, 0:4096], wrf[:, 4096:8192])
            wt = [xtf[:, (l_ * 12 + i) * 128:(l_ * 12 + i + 1) * 128] for i in range(12)]
            bt = [uTf[:, 4096 + (l_ * 9 + i) * 128: 4096 + (l_ * 9 + i + 1) * 128] for i in range(9)]
            LT.append(dict(zT=aTf[:, 8192 + l_ * 2048: 8192 + (l_ + 1) * 2048], delT=aTf[:, 12288 + l_ * 2048: 12288 + (l_ + 1) * 2048],
                           vsb=aTf[:, l_ * 4096:(l_ + 1) * 4096], qn=qn_l, kn=kn_l, wt=wt, bt=bt, ss=mt2(1), rs=mt2(1)))

        def lane_ops(l):
            def lk(k):
                if k in SHARED or (isinstance(k, tuple) and k[0] in ("ps", "sq")):
                    return k
                return ("L", l, k)

            def op(eng, fn, reads=(), writes=()):
                P.op(eng, fn, tuple(lk(k) for k in reads), tuple(lk(k) for k in writes))

            def dma(q, fn, reads=(), writes=()):
                P.dma(q, fn, tuple(lk(k) for k in reads), tuple(lk(k) for k in writes))
            return op, dma

        def unpack(l):
            T = LT[l]
            wt, bt = T["wt"], T["bt"]
            return (T["zT"], T["delT"], T["vsb"], T["qn"], T["kn"], wt[0], wt[1], wt[2], wt[3], wt[4], wt[5], wt[6],
                    [wt[7], wt[8]], [wt[9], wt[10]], wt[11], bt[0], bt[1], bt[2], bt[3], bt[4], bt[5], bt[6], bt[7], bt[8], T["ss"], T["rs"])

        def prologue(h, l):
            op, dma = lane_ops(l)
            (zT, delT, vsb, qn, kn, Et, Mt, R, gbt, osb, t1s, Stt, Mp, Np, attnf,
             attnT, kdec, kbeg, vbeta, Rb, wTn, vnew, Sb, onb, ss, rs) = unpack(l)
            dma("sp", lambda e, i=z_d[h]: e.dma_start(out=zT, in_=i), writes=("zT",))
            for part, (rsrc, key) in enumerate(((gq_d, "raw"), (gk_d, "raw"), (gv_d, "raw"))):
                raw = rawb
                dma("sp", lambda e, i=rsrc[h]: e.dma_start(out=rawb, in_=i), writes=("raw",))
                cw = lambda j: convw[:, (part * 8 + h) * 4 + j:(part * 8 + h) * 4 + j + 1]
                op("dve", lambda e, r=raw, w=cw(3): e.tensor_scalar_mul(cacc, r, w), reads=(key, "convw"), writes=("cacc",))
                for sh in (1, 2, 3):
                    op("dve", lambda e, r=raw, w=cw(3 - sh), sh=sh:
                         e.scalar_tensor_tensor(cacc[:, sh:4096], r[:, 0:4096 - sh], w, cacc[:, sh:4096], op0=ALU.mult, op1=ALU.add),
                         reads=(key, "convw", "cacc"), writes=("cacc",))
                if part == 2:
                    op("act", lambda e: e.activation(vsb, cacc, AF.Silu), reads=("cacc",), writes=("vsb",))
                    continue
                dstn, dkey = (qn, "qn") if part == 0 else (kn, "kn")
                op("act", lambda e, o=dstn: e.activation(o, cacc, AF.Silu), reads=("cacc",), writes=(dkey,))
                for blk in range(8):
                    bs = slice(blk * 512, (blk + 1) * 512)
                    op("act", lambda e, i=dstn[:, bs], o=sq[:, blk % 2, :]: e.activation(o, i, AF.Square),
                         reads=(dkey,), writes=(("sq", blk % 2),))
                    pb = 6 + blk % 2
                    op("pe", lambda e, o=ps[pb][:, :], r=sq[:, blk % 2, :]: e.matmul(o, ones_bf[:, :], r, start=True, stop=True),
                         reads=(("sq", blk % 2), "ones"), writes=(("ps", pb),))
                    op("dve", lambda e, i=ps[pb][:, :]: e.tensor_scalar_add(rstd[:, :], i, EPS), reads=(("ps", pb),), writes=("rstd",))
                    op("act", lambda e: e.activation(rstd[:, :], rstd[:, :], AF.Sqrt), reads=("rstd",), writes=("rstd",))
                    op("dve", lambda e: e.reciprocal(rstd[:, :], rstd[:, :]), reads=("rstd",), writes=("rstd",))
                    op("dve", lambda e, o=dstn[:, bs], scl=(SCALE if part == 0 else 1.0):
                         e.scalar_tensor_tensor(o, o, scl, rstd[:, :], op0=ALU.mult, op1=ALU.mult),
                         reads=(dkey, "rstd"), writes=(dkey,))

        def chunk_body(h, c, l):
            op, dma = lane_ops(l)
            (zT, delT, vsb, qn, kn, Et, Mt, R, gbt, osb, t1s, Stt, Mp, Np, attnf,
             attnT, kdec, kbeg, vbeta, Rb, wTn, vnew, Sb, onb, ss, rs) = unpack(l)
            own = c >= 16
            col = c * 8 + h
            cs = slice(c * 128, (c + 1) * 128)
            colap = lambda t: t[:, col:col + 1]
            op("act", lambda e, g_=colap(gt): e.activation(gbt, ones_f, AF.Copy, scale=g_), reads=("gt", "ones_f"), writes=("gb",))
            yield
            op("pe", lambda e: e.matmul(slot(0, 0), tri, gbt, start=True, stop=False), reads=("tri", "gb"), writes=(("ps", 0),))
            op("pe", lambda e: e.matmul(slot(0, 0), gbt, negtri, start=False, stop=False), reads=("negtri", "gb"), writes=(("ps", 0),))
            op("pe", lambda e: e.matmul(slot(0, 0), ident[:, :], masku, start=False, stop=True), reads=("ident", "masku"), writes=(("ps", 0),))
            op("act", lambda e: e.activation(Et, slot(0, 0), AF.Exp), reads=(("ps", 0),), writes=("E",))
            if c in (0, 16): P.marks.append(("c%d-E" % c, P.total))
            yield
            op("pe", lambda e, k_=kn[:, cs]: e.matmul(slot(1, 0), k_, k_, start=True, stop=True), reads=("kn",), writes=(("ps", 1),))
            op("dve", lambda e, b_=colap(beta): e.scalar_tensor_tensor(Mt, slot(1, 0), b_, Et, op0=ALU.mult, op1=ALU.mult),
                 reads=(("ps", 1), "beta", "E"), writes=("Mt",))
            op("dve", lambda e: e.tensor_mul(Mp[0], Mt, strict), reads=("Mt", "strict"), writes=(("Mp", 0),))
            yield
            op("pe", lambda e: e.transpose(slot(1, 1), Mp[0], ident[:, :]), reads=(("Mp", 0), "ident"), writes=(("ps", 1),))
            op("act", lambda e: e.copy(Np[0], slot(1, 1)), reads=(("ps", 1),), writes=(("Np", 0),))
            op("dve", lambda e: e.scalar_tensor_tensor(R, Np[0], -1.0, ident[:, :], op0=ALU.mult, op1=ALU.add), reads=(("Np", 0), "ident"), writes=("R",))
            if c in (0, 16): P.marks.append(("c%d-preDoubling" % c, P.total))
            for it in range(6):
                cur, nxt = it % 2, 1 - it % 2
                yield
                op("pe", lambda e, o=slot(2, cur), l=Np[cur], r=Mp[cur]: e.matmul(o, l, r, start=True, stop=True),
                     reads=(("Np", cur), ("Mp", cur)), writes=(("ps", 2),))
                op("act", lambda e, o=Mp[nxt], i=slot(2, cur): e.copy(o, i), reads=(("ps", 2),), writes=(("Mp", nxt),))
                if it < 5:
                    yield
                    op("pe", lambda e, o=slot(3, cur), l=Mp[cur], r=Np[cur]: e.matmul(o, l, r, start=True, stop=True),
                         reads=(("Np", cur), ("Mp", cur)), writes=(("ps", 3),))
                    op("dve", lambda e, o=Np[nxt], i=slot(3, cur): e.tensor_copy(o, i), reads=(("ps", 3),), writes=(("Np", nxt),))
                yield
                op("pe", lambda e, o=slot(4, cur), l=Mp[nxt]: e.matmul(o, l, R, start=True, stop=True),
                     reads=(("Mp", nxt), "R"), writes=(("ps", 4),))
                op("dve", lambda e, i=slot(4, cur): e.tensor_add(R, i, R), reads=(("ps", 4), "R"), writes=("R",))
            op("act", lambda e: e.copy(Rb, R), reads=("R",), writes=("Rb",))
            if own:
                yield
                op("pe", lambda e, q_=qn[:, cs], k_=kn[:, cs]: e.matmul(slot(5, 0), q_, k_, start=True, stop=True),
                     reads=("qn", "kn"), writes=(("ps", 5),))
                op("dve", lambda e: e.tensor_mul(attnf, slot(5, 0), Et), reads=(("ps", 5), "E"), writes=("attnf",))
                yield
                op("pe", lambda e: e.transpose(slot(5, 1), attnf, ident[:, :]), reads=("attnf", "ident"), writes=(("ps", 5),))
                op("act", lambda e: e.copy(attnT, slot(5, 1)), reads=(("ps", 5),), writes=("attnT",))
            if c in (0, 16): P.marks.append(("c%d-preKV" % c, P.total))
            yield
            op("pe", lambda e, k_=kn[:, cs]: e.transpose(slot(5, 2), k_, ident[:, :]), reads=("kn", "ident"), writes=(("ps", 5),))
            op("dve", lambda e, s_=colap(eGlG): e.tensor_scalar_mul(kdec, slot(5, 2), s_), reads=(("ps", 5), "eGlG"), writes=("kdec",))
            op("dve", lambda e, s_=colap(skb): e.tensor_scalar_mul(kbeg, slot(5, 2), s_), reads=(("ps", 5), "skb"), writes=("kbeg",))
            yield
            op("pe", lambda e, v_=vsb[:, cs]: e.transpose(slotb(5, 3), v_, identb), reads=("vsb", "identb"), writes=(("ps", 5),))
            op("dve", lambda e, s_=colap(beta): e.tensor_scalar_mul(vbeta, slotb(5, 3), s_), reads=(("ps", 5), "beta"), writes=("vbeta",))
            if c in (0, 16): P.marks.append(("c%d-preW" % c, P.total))
            yield
            op("pe", lambda e: e.matmul(slot(6, 0), kbeg, Rb, start=True, stop=True), reads=("kbeg", "Rb"), writes=(("ps", 6),))
            op("act", lambda e: e.activation(wTn, slot(6, 0), AF.Copy, scale=-1.0), reads=(("ps", 6),), writes=("wTn",))
            yield
            op("pe", lambda e: e.matmul(slot(6, 1), Rb, vbeta, start=True, stop=False), reads=("Rb", "vbeta"), writes=(("ps", 6),))
            op("pe", lambda e: e.matmul(slot(6, 1), wTn, Sb, start=False, stop=True), reads=("wTn", "Sb"), writes=(("ps", 6),))
            op("dve", lambda e: e.tensor_copy(vnew, slot(6, 1)), reads=(("ps", 6),), writes=("vnew",))
            if own:
                yield
                op("pe", lambda e, q_=qn[:, cs]: e.matmul(slot(6, 2), q_, Stt, start=True, stop=True), reads=("qn", "S"), writes=(("ps", 6),))
                op("pe", lambda e: e.matmul(slot(6, 3), attnT, vnew, start=True, stop=True), reads=("attnT", "vnew"), writes=(("ps", 6),))
                op("dve", lambda e, s_=colap(eG): e.tensor_scalar_mul(t1s, slot(6, 2), s_), reads=(("ps", 6), "eG"), writes=("t1",))
                op("dve", lambda e: e.tensor_add(osb, slot(6, 3), t1s), reads=("t1", ("ps", 6)), writes=("osb",))
            if c in (0, 16): P.marks.append(("c%d-preState" % c, P.total))
            yield
            op("pe", lambda e: e.matmul(slot(7, 0), kdec, vnew, start=True, stop=True), reads=("kdec", "vnew"), writes=(("ps", 7),))
            op("dve", lambda e, s_=colap(cdt): e.tensor_scalar_mul(Stt, Stt, s_), reads=("S", "cdt"), writes=("S",))
            op("dve", lambda e: e.tensor_add(Stt, slot(7, 0), Stt), reads=("S", ("ps", 7)), writes=("S",))
            op("act", lambda e: e.copy(Sb, Stt), reads=("S",), writes=("Sb",))
            if c in (0, 16): P.marks.append(("c%d-preOut" % c, P.total))
            if own:
                oc = slice((c - 16) * 128, (c - 15) * 128)
                op("dve", lambda e: e.memset(ss, 0.0), writes=("ss",))
                op("act", lambda e: e.activation(t1s, osb, AF.Square, accum_out=ss), reads=("osb", "ss"), writes=("t1", "ss"))
                op("dve", lambda e: e.tensor_scalar(rs, ss, 1.0 / 128, EPS, op0=ALU.mult, op1=ALU.add), reads=("ss",), writes=("rs",))
                op("act", lambda e: e.activation(rs, rs, AF.Sqrt), reads=("rs",), writes=("rs",))
                op("dve", lambda e: e.reciprocal(rs, rs), reads=("rs",), writes=("rs",))
                op("dve", lambda e: e.scalar_tensor_tensor(onb, osb, rs, onorm, op0=ALU.mult, op1=ALU.mult),
                     reads=("osb", "rs", "onorm"), writes=("onb",))
                yield
                op("pe", lambda e: e.transpose(slotb(7, 2), onb, identb), reads=("onb", "identb"), writes=(("ps", 7),))
                op("dve", lambda e, o=delT[:, oc], z_=zT[:, oc]: e.tensor_mul(o, slotb(7, 2), z_), reads=(("ps", 7), "zT"), writes=("delT",))

        for h0 in range(0, len(heads), 2):
            hp = heads[h0:h0 + 2]
            for l, h in enumerate(hp):
                prologue(h, l)
            P.barrier()
            for l, h in enumerate(hp):
                op, dma = lane_ops(l)
                T = LT[l]
                op("dve", lambda e, t=T["wt"][6]: e.memset(t, 0.0), writes=("S",))
                op("dve", lambda e, t=T["bt"][7]: e.memset(t, 0.0), writes=("Sb",))
            for c in range(32):
                gens = [chunk_body(h, c, l) for l, h in enumerate(hp)]
                while gens:
                    for g_ in list(gens):
                        try:
                            next(g_)
                        except StopIteration:
                            gens.remove(g_)
            for l, h in enumerate(hp):
                op, dma = lane_ops(l)
                dma("sp", lambda e, o=del_d[h], i=LT[l]["delT"]: e.dma_start(out=o, in_=i), reads=("delT",), writes=())
            P.barrier()

    if "E" in phases:
        KmT = kvm[:, 0:1024].rearrange("p (h m) -> p h m", m=256)
        Vm = kvm[:, 1024:2048].rearrange("p (s n) -> p s n", n=512)
        attA, delA = aT[:, 0:8, :], aT[:, 8:16, :]
        qxT, oxT = aT[:, 16:20, :], aT[:, 20:24, :]
        rr = misc[:, 0:512]
        for gi, gsrc in ((2, g_cross), (3, g_mem), (4, g_ffn2), (5, g_final)):
            P.dma("sp", lambda e, o=gains[:, gi, :], i=gsrc: e.dma_start(out=o, in_=i[:, :]), writes=("gains",))
        load_xT(0, src=memb, nsub=2)
        rms_to_uT(3, nt=256)
        for hb in range(2):
            s_, view = load_wcols(wkv, hb * 256, 256)
            for j in range(2):
                hd = hb * 2 + j
                for c in range(DC):
                    P.op("pe", lambda e, o=ps[j][:, 0:256], l=view[:, c, j * 128:(j + 1) * 128], r=uT[:, c, 0:256],
                         st=(c == 0), sp_=(c == DC - 1): e.matmul(o, l, r, start=st, stop=sp_),
                         reads=(("w", s_, c // 8), ("uT", c)), writes=(("ps", j),))
                evac_copy(KmT[:, hd, :], ps[j][:, 0:256], reads=(("ps", j),), writes=("kvm",))
        for blk in range(2):
            s_, view = load_wcols(wkv, 512 + blk * 256, 256)
            for sub in range(2):
                for c in range(DC):
                    P.op("pe", lambda e, o=ps[2 + sub][:, 0:256], l=uT[:, c, sub * 128:(sub + 1) * 128], r=view[:, c, :],
                         st=(c == 0), sp_=(c == DC - 1): e.matmul(o, l, r, start=st, stop=sp_),
                         reads=(("w", s_, c // 8), ("uT", c)), writes=(("ps", 2 + sub),))
                evac_copy(Vm[:, sub, blk * 256:(blk + 1) * 256], ps[2 + sub][:, 0:256], reads=(("ps", 2 + sub),), writes=("kvm",))
        for t in range(NOWN // TT):
            if t + (NTOK - NOWN) // TT not in tiles:
                continue
            ts_ = slice(t * TT, (t + 1) * TT)
            P.dma("sp", lambda e, i=att_d[:, :, ts_].rearrange("h p t -> p h t"): e.dma_start(out=attA, in_=i),
                  writes=tuple(("aT", f) for f in range(8)))
            P.dma("sp", lambda e, i=del_d[:, :, ts_].rearrange("h p t -> p h t"): e.dma_start(out=delA, in_=i),
                  writes=tuple(("aT", f) for f in range(8, 16)))
            P.dma("sp", lambda e, i=h1_d[:, :, ts_]: e.dma_start(out=xT[:, :, :], in_=i), writes=tuple(("xT", c) for c in range(DC)))
            for pair in range(8):
                sa, va = load_wcols(wbra, pair * 256, 256, nrc=8)
                sd, vd = load_wcols(wbrd, pair * 256, 256, nrc=8)
                for j in range(2):
                    c = pair * 2 + j
                    P.dma("sp", lambda e, o=ostage[:, 0 + j, :], i=ga_d[c][:, ts_]: e.dma_start(out=o, in_=i), writes=(("ost", 0 + j),))
                    P.dma("sp", lambda e, o=ostage[:, 2 + j, :], i=gb_d[c][:, ts_]: e.dma_start(out=o, in_=i), writes=(("ost", 2 + j),))
                    for hh in range(8):
                        P.op("pe", lambda e, o=ps[j][:, :], l=va[:, hh, j * 128:(j + 1) * 128], r=attA[:, hh, :],
                             st=(hh == 0), sp_=(hh == 7): e.matmul(o, l, r, start=st, stop=sp_),
                             reads=(("w", sa, 0), ("aT", hh)), writes=(("ps", j),))
                    for hh in range(8):
                        P.op("pe", lambda e, o=ps[2 + j][:, :], l=vd[:, hh, j * 128:(j + 1) * 128], r=delA[:, hh, :],
                             st=(hh == 0), sp_=(hh == 7): e.matmul(o, l, r, start=st, stop=sp_),
                             reads=(("w", sd, 0), ("aT", 8 + hh)), writes=(("ps", 2 + j),))
                    P.op("dve", lambda e, o=sg[:, 0, :], i=ps[j][:, :], g_=ostage[:, 0 + j, :]: e.tensor_tensor(o, i, g_, op=ALU.mult),
                         reads=(("ps", j), ("ost", 0 + j)), writes=(("sg", 0),))
                    P.op("dve", lambda e, o=sg[:, 1, :], i=ps[2 + j][:, :], g_=ostage[:, 2 + j, :]: e.tensor_tensor(o, i, g_, op=ALU.mult),
                         reads=(("ps", 2 + j), ("ost", 2 + j)), writes=(("sg", 1),))
                    P.op("pool", lambda e, o=uT[:, c, :]: e.tensor_tensor(o, sg[:, 0, :], sg[:, 1, :], op=ALU.add),
                         reads=(("sg", 0), ("sg", 1)), writes=(("uT", c),))

            def proj_add(w2d, nrc, rhs_of, rkeys):
                for pair in range(8):
                    s_, view = load_wcols(w2d, pair * 256, 256, nrc=nrc)
                    for j in range(2):
                        c = pair * 2 + j
                        pb = 4 + c % 2
                        for k in range(nrc):
                            P.op("pe", lambda e, o=ps[pb][:, :], l=view[:, k, j * 128:(j + 1) * 128], r=rhs_of(k),
                                 st=(k == 0), sp_=(k == nrc - 1): e.matmul(o, l, r, start=st, stop=sp_),
                                 reads=(("w", s_, (k * 256) // 2048), rkeys(k)), writes=(("ps", pb),))
                        P.op("dve", lambda e, o=xT[:, c, :], i=ps[pb][:, :]: e.tensor_add(o, i, o),
                             reads=(("ps", pb), ("xT", c)), writes=(("xT", c),))
            proj_add(wout, DC, lambda k: uT[:, k, :], lambda k: ("uT", k))
            rms_to_uT(2)
            for pair in range(2):
                s_, view = load_wcols(wq, pair * 256, 256)
                for j in range(2):
                    hd = pair * 2 + j
                    for c in range(DC):
                        P.op("pe", lambda e, o=ps[j][:, :], l=view[:, c, j * 128:(j + 1) * 128], r=uT[:, c, :],
                             st=(c == 0), sp_=(c == DC - 1): e.matmul(o, l, r, start=st, stop=sp_),
                             reads=(("w", s_, c // 8), ("uT", c)), writes=(("ps", j),))
                    evac_copy(qxT[:, hd, :], ps[j][:, :], reads=(("ps", j),), writes=(("aT", 16 + hd),))
            for hd in range(4):
                for mc in range(2):
                    P.op("pe", lambda e, o=ps[mc][:, :], l=KmT[:, hd, mc * 128:(mc + 1) * 128], r=qxT[:, hd, :]:
                         e.matmul(o, l, r, start=True, stop=True), reads=("kvm", ("aT", 16 + hd)), writes=(("ps", mc),))
                    P.op("act", lambda e, o=aT[:, 24 + mc, :], i=ps[mc][:, :]: e.activation(o, i, AF.Exp, scale=SCALE),
                         reads=(("ps", mc),), writes=(("aT", 24 + mc),))
                    P.op("pe", lambda e, l=Vm[:, mc, hd * 128:(hd + 1) * 128], r=aT[:, 24 + mc, :], st=(mc == 0), sp_=(mc == 1):
                         e.matmul(ps[2][:, :], l, r, start=st, stop=sp_), reads=("kvm", ("aT", 24 + mc)), writes=(("ps", 2),))
                    P.op("pe", lambda e, r=aT[:, 24 + mc, :], st=(mc == 0), sp_=(mc == 1):
                         e.matmul(ps[3][:, :], ones_bf[:, :], r, start=st, stop=sp_), reads=("ones", ("aT", 24 + mc)), writes=(("ps", 3),))
                P.op("dve", lambda e: e.reciprocal(rr, ps[3][:, :]), reads=(("ps", 3),), writes=("rr",))
                P.op("dve", lambda e, o=oxT[:, hd, :]: e.tensor_tensor(o, ps[2][:, :], rr, op=ALU.mult),
                     reads=(("ps", 2), "rr"), writes=(("aT", 20 + hd),))
            proj_add(wo, 4, lambda k: oxT[:, k, :], lambda k: ("aT", 20 + k))
            rms_to_uT(4)
            ffn(w2g, w2u, w2d)
            rms_to_uT(5, inplace=True)
            for sub in range(4):
                for cg in range(4):
                    pb = ps[4 + cg % 2]
                    for j in range(4):
                        c = cg * 4 + j
                        P.op("pe", lambda e, o=pb[:, j * 128:(j + 1) * 128], i=xT[:, c, sub * 128:(sub + 1) * 128]:
                             e.transpose(o, i, ident[:, :]), reads=(("xT", c), "ident"), writes=(("ps", 4 + cg % 2),))
                    evac_copy(xtok[:, sub % 2, cg * 512:(cg + 1) * 512], pb[:, :], reads=(("ps", 4 + cg % 2),), writes=(("xtok", sub % 2),))
                P.dma("sp", lambda e, o=out[t * TT + sub * 128: t * TT + (sub + 1) * 128, :], i=xtok[:, sub % 2, :]:
                      e.dma_start(out=o, in_=i), reads=(("xtok", sub % 2),), writes=())
        P.barrier()

    if dbg:
        for nm, src in (("d_h1", h1_d), ("d_kT", kT_d), ("d_v", v_d), ("d_ba", ba_d), ("d_ga", ga_d)):
            P.dma("sp", lambda e, o=dbg_out[nm], i=src: e.dma_start(out=o, in_=i), writes=())
        P.barrier()

    if _os.environ.get("OPMARKS"):
        print("MARKS", P.marks, "total", P.total)
    P.replay(nc, sems)
    es.close()
    return nc


def _bf(a):
    return np.asarray(a, dtype=np.float32).astype(ml_dtypes.bfloat16)


def _gain_layout(g):
    return np.ascontiguousarray(np.asarray(g, np.float32).reshape(DC, 128).T)


def make_in_maps(inputs):
    x = np.asarray(inputs["x"], np.float32)
    shared = {
        "ffn1_w_gate": np.asarray(inputs["ffn1_w_gate"], np.float32)[0],
        "ffn1_w_up": np.asarray(inputs["ffn1_w_up"], np.float32)[0],
        "ffn1_w_down": np.asarray(inputs["ffn1_w_down"], np.float32)[0],
        "w_in": np.asarray(inputs["w_in"], np.float32)[0],
        "g_ffn1": _gain_layout(inputs["ffn1_norm"][0]),
        "g_mix": _gain_layout(inputs["mix_norm"][0]),
        "g_cross": _gain_layout(inputs["cross_norm"][0]),
        "g_mem": _gain_layout(inputs["mem_norm"][0]),
        "g_ffn2": _gain_layout(inputs["ffn2_norm"][0]),
        "g_final": _gain_layout(inputs["final_norm"]),
        "w_branch_attn": np.asarray(inputs["w_branch_attn"], np.float32)[0],
        "w_branch_delta": np.asarray(inputs["w_branch_delta"], np.float32)[0],
        "w_out": np.asarray(inputs["w_out"], np.float32)[0],
        "cross_wq": np.asarray(inputs["cross_wq"], np.float32)[0],
        "cross_wkv": np.asarray(inputs["cross_wkv"], np.float32)[0],
        "cross_wo": np.asarray(inputs["cross_wo"], np.float32)[0],
        "ffn2_w_gate": np.asarray(inputs["ffn2_w_gate"], np.float32)[0],
        "ffn2_w_up": np.asarray(inputs["ffn2_w_up"], np.float32)[0],
        "ffn2_w_down": np.asarray(inputs["ffn2_w_down"], np.float32)[0],
        "c_ident": np.eye(128, dtype=np.float32),
        "c_relb": np.ascontiguousarray(np.broadcast_to(np.asarray(inputs["rel_bias"], np.float32).reshape(1, 256), (128, 256))),
    }
    shared.update(gdn_consts(inputs))
    maps = []
    for c in range(8):
        b, s = c // 2, c % 2
        xin = np.zeros((NTOK, D), np.float32)
        if s == 1:
            xin[:] = x[b]
        else:
            xin[NOWN:] = x[b, :NOWN]
        m = dict(shared)
        m["xin"] = xin
        m["memb"] = np.ascontiguousarray(np.asarray(inputs["mem"], np.float32)[b])
        m.update(moba_consts(s))
        maps.append(m)
    return maps


def kernel(**inputs):
    nc = build()
    maps = make_in_maps(inputs)
    res = run_bass_kernel_spmd(nc, maps, core_ids=list(range(8)))
    outp = np.zeros((4, 4096, D), np.float32)
    for c in range(8):
        b, s = c // 2, c % 2
        outp[b, s * NOWN:(s + 1) * NOWN] = res.results[c]["out"]
    return outp
```

```python
import math
import numpy as np
import ml_dtypes
import concourse.bass as bass
import concourse.mybir as mybir
from concourse.bass_utils import run_bass_kernel_spmd

F32 = mybir.dt.float32
BF16 = mybir.dt.bfloat16
I32 = mybir.dt.int32
AF = mybir.ActivationFunctionType
ALU = mybir.AluOpType
AX = mybir.AxisListType

D = 2048
DC = 16
DFF = 5632
FC = 44
NTOK = 4096
NOWN = 2048
TT = 512
NTILE = NTOK // TT
EPS = 1e-6
BIG = 30000.0
NDS = 16


class Prog:
    ENG = ("pe", "dve", "act", "pool", "sp")

    def __init__(self):
        self.streams = {e: [] for e in self.ENG}
        self.cnt = {e: 0 for e in self.ENG}
        self.seen = {e: {} for e in self.ENG}
        self.state = {}
        self.dma_q = {}
        self.total = 0
        self.limit = None
        self.marks = []

    def _wait(self, eng, tok):
        s, v = tok
        if self.seen[eng].get(s, 0) >= v:
            return
        self.seen[eng][s] = v
        self.streams[eng].append(("wait", s, v))

    def _deps(self, reads, writes):
        deps = []
        for k in reads:
            st = self.state.get(k)
            if st is not None and st[0] is not None:
                deps.append(st[0])
        for k in writes:
            st = self.state.get(k)
            if st is not None:
                if st[0] is not None:
                    deps.append(st[0])
                deps.extend(st[1])
        return deps

    def _commit(self, tok, reads, writes):
        for k in reads:
            st = self.state.setdefault(k, [None, []])
            st[1] = [t for t in st[1] if t[0] != tok[0]] + [tok]
        for k in writes:
            self.state[k] = [tok, []]

    def _maxdeps(self, reads, writes):
        best = {}
        for s_, v in self._deps(reads, writes):
            if v > best.get(s_, 0):
                best[s_] = v
        return list(best.items())

    def op(self, eng, fn, reads=(), writes=()):
        self.total += 1
        if self.limit is not None and self.total > self.limit:
            return
        for tok in self._maxdeps(reads, writes):
            if tok[0] == eng and eng == "pe":
                continue
            self._wait(eng, tok)
        self.cnt[eng] += 1
        tok = (eng, self.cnt[eng])
        self.streams[eng].append(("op", fn, eng, 1))
        self._commit(tok, reads, writes)

    def dma(self, q, fn, reads=(), writes=()):
        self.total += 1
        if self.limit is not None and self.total > self.limit:
            return
        i = self.dma_q.get(q, 0)
        self.dma_q[q] = i + 1
        s = "dma_%s%d" % (q, i % NDS)
        v = 16 * (i // NDS + 1)
        if i >= NDS:
            self._wait(q, (s, v - 16))
        for tok in self._maxdeps(reads, writes):
            self._wait(q, tok)
        self.streams[q].append(("op", fn, s, 16))
        self._commit((s, v), reads, writes)

    def barrier(self):
        toks = [(e, self.cnt[e]) for e in self.ENG if self.cnt[e] > 0]
        for q, n in self.dma_q.items():
            for j in range(min(n, NDS)):
                last_i = ((n - 1 - j) // NDS) * NDS + j
                toks.append(("dma_%s%d" % (q, j), 16 * (last_i // NDS + 1)))
        for e in self.ENG:
            for t in toks:
                self._wait(e, t)
        self.state = {}

    def replay(self, nc, sems):
        engs = {}

        def run(name, eng):
            for it in self.streams[name]:
                if it[0] == "wait":
                    eng.wait_ge(sems[it[1]], it[2])
                else:
                    ins = it[1](eng)
                    ins.then_inc(sems[it[2]], it[3])

        with nc.Block() as block:
            @block.tensor
            def _(e):
                run("pe", e)

            @block.vector
            def _(e):
                run("dve", e)

            @block.scalar
            def _(e):
                run("act", e)

            @block.gpsimd
            def _(e):
                run("pool", e)

            @block.sync
            def _(e):
                run("sp", e)


def t5_thresholds():
    d = np.arange(0, 2048)
    nf = np.maximum(d, 1).astype(np.float32)
    large = 16 + (np.log(nf / np.float32(16)) / np.float32(math.log(128 / 16)) * np.float32(16)).astype(np.int32)
    large = np.minimum(large, 31)
    b = np.where(d < 16, d, large)
    return [int(np.argmax(b >= r)) for r in range(32)]


def moba_consts(s_role):
    dist = np.zeros((128, 4, 256), np.float32)
    for kc in range(4):
        dist[:, kc, :] = 256 + np.arange(256)[None, :] - 128 * kc - np.arange(128)[:, None]
    maskown = np.zeros((128, 8, 2, 16), np.float32)
    farmask = np.zeros((128, 8, 2, 16), np.float32)
    for qb in range(8):
        own = 8 + qb
        maskown[:, qb, :, own:] = -BIG
        if s_role == 0:
            maskown[:, qb, :, 0:8] = -BIG
        farmask[:, qb, :, 0:max(own - 1, 0)] = 1.0
    esel = np.zeros((16, 16, 128), np.float32)
    for n in range(16):
        esel[n, n, :] = 1.0
    return {"c_dist": dist.reshape(128, 1024), "c_maskown": maskown.reshape(128, 256),
            "c_farmask": farmask.reshape(128, 256), "c_esel": esel.reshape(16, 2048)}


def gdn_consts(inputs):
    i = np.arange(128)
    tri = (i[:, None] <= i[None, :]).astype(np.float32)
    masku = np.where(i[None, :] > i[:, None], -BIG, 0.0).astype(np.float32)
    strict = (i[:, None] > i[None, :]).astype(np.float32)
    cw = np.asarray(inputs["gdn_conv"], np.float32)[0]
    conv = np.ascontiguousarray(cw.T.reshape(24, 128, 4).transpose(1, 0, 2).reshape(128, 96))
    rep = lambda v, n: np.ascontiguousarray(np.broadcast_to(np.asarray(v, np.float32).reshape(1, n), (128, n)))
    return {"c_tri": tri, "c_masku": masku, "c_strict": strict, "c_onorm": rep(inputs["gdn_out_norm"][0], 128),
            "c_conv": conv, "c_alog": rep(inputs["gdn_a_log"][0], 8), "c_dtb": rep(inputs["gdn_dt_bias"][0], 8)}


IN_ATT_Q, IN_ATT_K, IN_ATT_V = 0, 1024, 2048
IN_GDN_Q, IN_GDN_K, IN_GDN_V = 3072, 4096, 5120
IN_Z, IN_BA, IN_GA, IN_GB = 6144, 7168, 7184, 9232


def build(dbg=False, tiles=None, phases="ABCDE", tiny_w=False, heads=None):
    heads = list(range(8)) if heads is None else heads
    tiles = list(range(NTILE)) if tiles is None else tiles
    from contextlib import ExitStack
    nc = bass.Bass("TRN2", target_bir_lowering=False)
    P = Prog()
    import os as _os
    if _os.environ.get("OPLIM"):
        P.limit = int(_os.environ["OPLIM"])
    es = ExitStack()

    def din(name, shape, dt=F32):
        import os
        if "noin" in os.environ.get("BIS", "") and name != "xin" and name not in os.environ.get("KEEP", "").split(","):
            class _D:
                def rearrange(self, *a, **k): return self
                def __getitem__(self, k): return self
            return _D()
        return nc.dram_tensor(name, list(shape), dt, kind="ExternalInput").ap()

    import os
    BIS0 = os.environ.get("BIS", "")

    def dscr(name, shape, dt):
        if "nodscr" in BIS0:
            return None
        return nc.dram_tensor(name, list(shape), dt).ap()

    def sb(name, shape, dt):
        if "nosb" in BIS0 and name != "xtok":
            return None
        return es.enter_context(nc.sbuf_tensor(name, list(shape), dt))

    xin = din("xin", [128, D] if tiny_w else [NTOK, D])
    if tiny_w:
        w1g, w1u, w1d, w_in = (din(n, [128, 128]) for n in ("ffn1_w_gate", "ffn1_w_up", "ffn1_w_down", "w_in"))
    else:
        w1g, w1u, w1d = din("ffn1_w_gate", [D, DFF]), din("ffn1_w_up", [D, DFF]), din("ffn1_w_down", [DFF, D])
        w_in = din("w_in", [D, 11280])
    g_ffn1, g_mix = din("g_ffn1", [128, DC]), din("g_mix", [128, DC])
    c_ident = din("c_ident", [128, 128])
    c_dist = din("c_dist", [128, 1024])
    c_maskown = din("c_maskown", [128, 256])
    c_farmask = din("c_farmask", [128, 256])
    c_relb = din("c_relb", [128, 256])
    c_esel = din("c_esel", [16, 2048])
    if tiny_w:
        wbra, wbrd, wout, wq, wkv, wo, w2g, w2u, w2d = (din(n, [128, 128]) for n in (
            "w_branch_attn", "w_branch_delta", "w_out", "cross_wq", "cross_wkv", "cross_wo", "ffn2_w_gate", "ffn2_w_up", "ffn2_w_down"))
    else:
        wbra, wbrd, wout = din("w_branch_attn", [1024, D]), din("w_branch_delta", [1024, D]), din("w_out", [D, D])
        wq, wkv, wo = din("cross_wq", [D, 512]), din("cross_wkv", [D, 1024]), din("cross_wo", [512, D])
        w2g, w2u, w2d = din("ffn2_w_gate", [D, DFF]), din("ffn2_w_up", [D, DFF]), din("ffn2_w_down", [DFF, D])
    g_cross, g_mem, g_ffn2, g_final = (din(n, [128, DC]) for n in ("g_cross", "g_mem", "g_ffn2", "g_final"))
    memb = din("memb", [256, D])
    c_tri, c_masku, c_strict, c_onorm = (din(n, [128, 128]) for n in ("c_tri", "c_masku", "c_strict", "c_onorm"))
    c_conv = din("c_conv", [128, 96])
    c_alog, c_dtb = din("c_alog", [128, 8]), din("c_dtb", [128, 8])
    del_d = dscr("del_d", [8, 128, NOWN], BF16)
    out = nc.dram_tensor("out", [NOWN, D], F32, kind="ExternalOutput").ap()
    att_d = dscr("att_d", [8, 128, NOWN], BF16)

    h1_d = dscr("h1_d", [128, DC, NOWN], F32)
    qT_d = dscr("qT_d", [8, 128, NOWN], BF16)
    kT_d = dscr("kT_d", [8, 128, NTOK], BF16)
    v_d = dscr("v_d", [NTOK, 1024], BF16)
    gq_d = dscr("gq_d", [8, 128, NTOK], BF16)
    gk_d = dscr("gk_d", [8, 128, NTOK], BF16)
    gv_d = dscr("gv_d", [8, 128, NTOK], BF16)
    z_d = dscr("z_d", [8, 128, NOWN], BF16)
    ba_d = dscr("ba_d", [NTOK, 16], F32)
    ga_d = dscr("ga_d", [16, 128, NOWN], BF16)
    gb_d = dscr("gb_d", [16, 128, NOWN], BF16)
    dbg_out = {}
    if dbg:
        dbg_out["d_h1"] = nc.dram_tensor("d_h1", [128, DC, NOWN], F32, kind="ExternalOutput").ap()
        dbg_out["d_kT"] = nc.dram_tensor("d_kT", [8, 128, NTOK], BF16, kind="ExternalOutput").ap()
        dbg_out["d_v"] = nc.dram_tensor("d_v", [NTOK, 1024], BF16, kind="ExternalOutput").ap()
        dbg_out["d_ba"] = nc.dram_tensor("d_ba", [NTOK, 16], F32, kind="ExternalOutput").ap()
        dbg_out["d_ga"] = nc.dram_tensor("d_ga", [16, 128, NOWN], BF16, kind="ExternalOutput").ap()

    ident = sb("ident", [128, 128], F32)
    ones_bf = sb("ones_bf", [128, 128], BF16)
    identb_t = sb("identb", [128, 128], BF16)
    identb = identb_t[:, :]
    gains = sb("gains", [128, 8, DC], F32)
    xtok = sb("xtok", [128, 2, D], F32)
    xT = sb("xT", [128, DC, TT], F32)
    uT = sb("uT", [128, DC, TT], BF16)
    aT = sb("aT", [128, FC, TT], BF16)
    wring = sb("wring", [128, 4, 4096], BF16)
    wba = sb("wba", [128, DC, 16], BF16)
    sq = sb("sq", [128, 2, TT], BF16)
    rstd = sb("rstd", [128, TT], F32)
    sg = sb("sg", [128, 2, TT], F32)
    ostage = sb("ostage", [128, 4, TT], BF16)
    bastage = sb("bastage", [128, 4, 16], F32)
    kms = sb("kms", [128, 8, 16], F32)
    misc = sb("misc", [128, 4096], F32)
    kvm = sb("kvm", [128, 2048], BF16)
    esel = sb("esel", [16, 2048], F32)
    eselb = sb("eselb", [16, 2048], BF16)
    import os
    nps = 4 if "ps4" in os.environ.get("BIS", "") else 8
    ps = [es.enter_context(nc.psum_tensor("ps%d" % i, [128, 512], F32)) for i in range(nps)]

    sems = {e: es.enter_context(nc.semaphore("s_" + e)) for e in Prog.ENG}
    for q in ("sp", "pool"):
        for j in range(NDS):
            sems["dma_%s%d" % (q, j)] = es.enter_context(nc.semaphore("s_dma_%s%d" % (q, j)))

    cp_rr = [0]

    def evac_copy(out_ap, in_ap, reads, writes):
        cp_rr[0] ^= 1
        if cp_rr[0]:
            P.op("dve", lambda e, o=out_ap, i=in_ap: e.tensor_copy(o, i), reads, writes)
        else:
            P.op("act", lambda e, o=out_ap, i=in_ap: e.copy(o, i), reads, writes)

    wr_i = [0]

    def wslot():
        s = wr_i[0] % 4
        wr_i[0] += 1
        return s

    def load_wcols(w2d, col0, ncols, nrc=DC):
        s = wslot()
        view = wring[:, s, 0:nrc * ncols].rearrange("p (c n) -> p c n", n=ncols)
        src = w2d.rearrange("(c p) n -> p c n", p=128)
        hs = max(nrc // 2, 1)
        for half in range(2):
            rows = range(half * hs, min(half * hs + hs, nrc))
            wkeys = tuple(sorted({("w", s, (r * ncols) // 2048) for r in rows}))
            P.dma("pool", lambda e, o=view[:, half * hs:half * hs + hs, :],
                  i=src[:, half * hs:half * hs + hs, col0:col0 + ncols]: e.dma_start(out=o, in_=i),
                  reads=(), writes=wkeys)
        return s, view

    def rms_to_uT(gidx, nt=TT, inplace=False):
        for c in range(DC):
            P.op("act", lambda e, o=sq[:, c % 2, 0:nt], i=xT[:, c, 0:nt]: e.activation(o, i, AF.Square),
                 reads=(("xT", c),), writes=(("sq", c % 2),))
            P.op("pe", lambda e, o=ps[6][:, 0:nt], r=sq[:, c % 2, 0:nt], st=(c == 0), sp=(c == DC - 1):
                 e.matmul(o, ones_bf[:, :], r, start=st, stop=sp),
                 reads=(("sq", c % 2), "ones"), writes=(("ps", 6),))
        P.op("dve", lambda e: e.tensor_scalar(rstd[:, 0:nt], ps[6][:, 0:nt], 1.0 / D, EPS, op0=ALU.mult, op1=ALU.add),
             reads=(("ps", 6),), writes=("rstd",))
        P.op("act", lambda e: e.activation(rstd[:, 0:nt], rstd[:, 0:nt], AF.Sqrt),
             reads=("rstd",), writes=("rstd",))
        P.op("dve", lambda e: e.reciprocal(rstd[:, 0:nt], rstd[:, 0:nt]),
             reads=("rstd",), writes=("rstd",))
        for c in range(DC):
            if inplace:
                P.op("dve", lambda e, o=xT[:, c, 0:nt], g=gains[:, gidx, c:c + 1]:
                     e.scalar_tensor_tensor(o, o, g, rstd[:, 0:nt], op0=ALU.mult, op1=ALU.mult),
                     reads=(("xT", c), "rstd", "gains"), writes=(("xT", c),))
                continue
            P.op("dve", lambda e, o=uT[:, c, 0:nt], i=xT[:, c, 0:nt], g=gains[:, gidx, c:c + 1]:
                 e.scalar_tensor_tensor(o, i, g, rstd[:, 0:nt], op0=ALU.mult, op1=ALU.mult),
                 reads=(("xT", c), "rstd", "gains"), writes=(("uT", c),))

    def ffn(wg, wu, wd):
        for fb in range(FC // 2):
            sgi, vg = load_wcols(wg, fb * 256, 256)
            sui, vu = load_wcols(wu, fb * 256, 256)
            for j in range(2):
                f = fb * 2 + j
                pg, pu = ps[f % 2], ps[2 + f % 2]
                for c in range(DC):
                    P.op("pe", lambda e, o=pg[:, :], l=vg[:, c, j * 128:(j + 1) * 128], r=uT[:, c, :],
                         st=(c == 0), sp=(c == DC - 1): e.matmul(o, l, r, start=st, stop=sp),
                         reads=(("w", sgi, c // 8), ("uT", c)), writes=(("ps", f % 2),))
                for c in range(DC):
                    P.op("pe", lambda e, o=pu[:, :], l=vu[:, c, j * 128:(j + 1) * 128], r=uT[:, c, :],
                         st=(c == 0), sp=(c == DC - 1): e.matmul(o, l, r, start=st, stop=sp),
                         reads=(("w", sui, c // 8), ("uT", c)), writes=(("ps", 2 + f % 2),))
                P.op("act", lambda e, o=sg[:, f % 2, :], i=pg[:, :]: e.activation(o, i, AF.Silu),
                     reads=(("ps", f % 2),), writes=(("sg", f % 2),))
                P.op("dve", lambda e, o=aT[:, f, :], a=sg[:, f % 2, :], b=pu[:, :]: e.tensor_tensor(o, b, a, op=ALU.mult),
                     reads=(("sg", f % 2), ("ps", 2 + f % 2)), writes=(("aT", f),))
        wdv = wd.rearrange("(fc p) d -> p fc d", p=128)
        for dg in range(4):
            for fb in range(FC // 4):
                s = wslot()
                view = wring[:, s, 0:2048].rearrange("p (f n) -> p f n", n=512)
                P.dma("pool", lambda e, o=view, i=wdv[:, fb * 4:fb * 4 + 4, dg * 512:(dg + 1) * 512]:
                      e.dma_start(out=o, in_=i), reads=(), writes=(("w", s, 0), ("w", s, 1)))
                for j in range(4):
                    f = fb * 4 + j
                    for q in range(4):
                        P.op("pe", lambda e, o=ps[4 + q][:, :], l=view[:, j, q * 128:(q + 1) * 128], r=aT[:, f, :],
                             st=(f == 0), sp=(f == FC - 1): e.matmul(o, l, r, start=st, stop=sp),
                             reads=(("w", s, 0), ("w", s, 1), ("aT", f)), writes=(("ps", 4 + q),))
            for q in range(4):
                c = dg * 4 + q
                P.op("dve", lambda e, o=xT[:, c, :], i=ps[4 + q][:, :]:
                     e.scalar_tensor_tensor(o, i, 0.5, o, op0=ALU.mult, op1=ALU.add),
                     reads=(("ps", 4 + q), ("xT", c)), writes=(("xT", c),))

    def load_xT(tok0, src=None, nsub=4):
        src = xin if src is None else src
        for s in range(nsub):
            P.dma("sp", lambda e, o=xtok[:, s % 2, :], i=src[tok0 + s * 128: tok0 + (s + 1) * 128, :]:
                  e.dma_start(out=o, in_=i), reads=(), writes=(("xtok", s % 2),))
            for cg in range(4):
                pb = ps[4 + cg % 2]
                for j in range(4):
                    c = cg * 4 + j
                    P.op("pe", lambda e, o=pb[:, j * 128:(j + 1) * 128], i=xtok[:, s % 2, c * 128:(c + 1) * 128]:
                         e.transpose(o, i, ident[:, :]),
                         reads=(("xtok", s % 2), "ident"), writes=(("ps", 4 + cg % 2),))
                evac_copy(xT[:, cg * 4:cg * 4 + 4, s * 128:(s + 1) * 128],
                          pb[:, :].rearrange("p (c t) -> p c t", t=128),
                          reads=(("ps", 4 + cg % 2),), writes=tuple(("xT", cg * 4 + j) for j in range(4)))

    import os
    BIS = os.environ.get("BIS", "")
    if "mini" in BIS:
        P.dma("pool", lambda e: e.dma_start(out=xtok[:, 0, :], in_=xin[0:128, :]), writes=("xtok",))
        P.dma("pool", lambda e: e.dma_start(out=out[0:128, :], in_=xtok[:, 0, :]), reads=("xtok",))
        P.barrier()
    if "nosetup" not in BIS:
      P.dma("sp", lambda e: e.dma_start(out=ident[:, :], in_=c_ident[:, :]), writes=("ident",))
      P.op("dve", lambda e: e.memset(ones_bf[:, :], 1.0), writes=("ones",))
      P.op("dve", lambda e: e.tensor_copy(identb, ident[:, :]), reads=("ident",), writes=("identb",))
      P.dma("sp", lambda e: e.dma_start(out=gains[:, 0, :], in_=g_ffn1[:, :]), writes=("gains",))
      P.dma("sp", lambda e: e.dma_start(out=gains[:, 1, :], in_=g_mix[:, :]), writes=("gains",))
      P.op("dve", lambda e: e.memset(kms[:, :, :], 0.0), writes=("kms",))
    if "wout" in BIS:
      P.dma("sp", lambda e: e.dma_start(out=out[0:128, :], in_=xtok[:, 0, :]), writes=())
      P.barrier()

    ost_i = [0]

    def inproj_fm(col0, dest, tok0, ntok_dest_off, func=None, kmean_h=None, t=None):
        s, view = load_wcols(w_in, col0, 256)
        for j in range(2):
            pi = ost_i[0] % 4
            ost_i[0] += 1
            pb = ps[pi]
            for c in range(DC):
                P.op("pe", lambda e, o=pb[:, :], l=view[:, c, j * 128:(j + 1) * 128], r=uT[:, c, :],
                     st=(c == 0), sp=(c == DC - 1): e.matmul(o, l, r, start=st, stop=sp),
                     reads=(("w", s, c // 8), ("uT", c)), writes=(("ps", pi),))
            if func is None:
                evac_copy(ostage[:, pi, :], pb[:, :], reads=(("ps", pi),), writes=(("ost", pi),))
            else:
                P.op("act", lambda e, o=ostage[:, pi, :], i=pb[:, :]: e.activation(o, i, func),
                     reads=(("ps", pi),), writes=(("ost", pi),))
            if kmean_h is not None:
                h = kmean_h + j
                P.op("dve", lambda e, o=kms[:, h, 2 * t:2 * t + 2], i=pb[:, :].rearrange("p (b k) -> p b k", k=256):
                     e.reduce_sum(o, i, axis=AX.X), reads=(("ps", pi),), writes=("kms",))
            dch = dest[0][dest[1] + j]
            P.dma("sp", lambda e, o=dch[:, ntok_dest_off:ntok_dest_off + TT], i=ostage[:, pi, :]:
                  e.dma_start(out=o, in_=i), reads=(("ost", pi),), writes=())

    def inproj_tm_v(tok0):
        for blk in range(4):
            s, view = load_wcols(w_in, IN_ATT_V + blk * 256, 256)
            for sub in range(4):
                pi = ost_i[0] % 4
                ost_i[0] += 1
                pb = ps[pi]
                for c in range(DC):
                    P.op("pe", lambda e, o=pb[:, 0:256], l=uT[:, c, sub * 128:(sub + 1) * 128], r=view[:, c, :],
                         st=(c == 0), sp=(c == DC - 1): e.matmul(o, l, r, start=st, stop=sp),
                         reads=(("w", s, c // 8), ("uT", c)), writes=(("ps", pi),))
                evac_copy(ostage[:, pi, 0:256], pb[:, 0:256], reads=(("ps", pi),), writes=(("ost", pi),))
                P.dma("sp", lambda e, o=v_d[tok0 + sub * 128: tok0 + (sub + 1) * 128, blk * 256:(blk + 1) * 256],
                      i=ostage[:, pi, 0:256]: e.dma_start(out=o, in_=i), reads=(("ost", pi),), writes=())

    def inproj_ba(tok0):
        for sub in range(4):
            pi = ost_i[0] % 4
            ost_i[0] += 1
            pb = ps[pi]
            for c in range(DC):
                P.op("pe", lambda e, o=pb[:, 0:16], l=uT[:, c, sub * 128:(sub + 1) * 128], r=wba[:, c, :],
                     st=(c == 0), sp=(c == DC - 1): e.matmul(o, l, r, start=st, stop=sp),
                     reads=("wba", ("uT", c)), writes=(("ps", pi),))
            P.op("dve", lambda e, o=bastage[:, sub, :], i=pb[:, 0:16]: e.tensor_copy(o, i),
                 reads=(("ps", pi),), writes=(("bast", sub),))
            P.dma("sp", lambda e, o=ba_d[tok0 + sub * 128: tok0 + (sub + 1) * 128, :], i=bastage[:, sub, :]:
                  e.dma_start(out=o, in_=i), reads=(("bast", sub),), writes=())

    if "A" in phases:
        P.dma("pool", lambda e: e.dma_start(out=wba[:, :, :],
              in_=w_in.rearrange("(c p) n -> p c n", p=128)[:, :, IN_BA:IN_BA + 16]), writes=("wba",))
        for t in tiles:
            tok0 = t * TT
            own = tok0 >= NTOK - NOWN
            otok = tok0 - (NTOK - NOWN)
            load_xT(tok0)
            rms_to_uT(0)
            ffn(w1g, w1u, w1d)
            rms_to_uT(1)
            if own:
                P.dma("sp", lambda e, o=h1_d[:, :, otok:otok + TT]: e.dma_start(out=o, in_=xT[:, :, :]),
                      reads=tuple(("xT", c) for c in range(DC)), writes=())
            for hb in range(4):
                if own:
                    inproj_fm(IN_ATT_Q + hb * 256, (qT_d, hb * 2), tok0, otok)
                inproj_fm(IN_ATT_K + hb * 256, (kT_d, hb * 2), tok0, tok0)
                inproj_fm(IN_GDN_Q + hb * 256, (gq_d, hb * 2), tok0, tok0)
                inproj_fm(IN_GDN_K + hb * 256, (gk_d, hb * 2), tok0, tok0)
                inproj_fm(IN_GDN_V + hb * 256, (gv_d, hb * 2), tok0, tok0)
                if own:
                    inproj_fm(IN_Z + hb * 256, (z_d, hb * 2), tok0, otok, func=AF.Silu)
            inproj_tm_v(tok0)
            inproj_ba(tok0)
            if own:
                for gbk in range(8):
                    inproj_fm(IN_GA + gbk * 256, (ga_d, gbk * 2), tok0, otok, func=AF.Sigmoid)
                    inproj_fm(IN_GB + gbk * 256, (gb_d, gbk * 2), tok0, otok, func=AF.Sigmoid)
        P.barrier()

    SCALE = 128.0 ** -0.5
    aTf = aT[:, :, :].rearrange("p c t -> p (c t)")
    xTf = xT[:, :, :].rearrange("p c t -> p (c t)")
    if "C" in phases:
        KT = aTf[:, 0:4096]
        Vt = aTf[:, 4096:8192].rearrange("p (c d) -> p c d", d=128)
        qT = aTf[:, 8192:10240]
        pT = aTf[:, 10240:12288].rearrange("p (r q) -> p r q", q=256)
        kmb = aTf[:, 12288:12304]
        attT = aTf[:, 12544:14592]
        biasT = xTf.rearrange("p (h k q) -> p h k q", h=8, k=4)
        mo = [0]

        def mtile(n):
            a = mo[0]
            mo[0] += n
            assert mo[0] <= 4096
            return misc[:, a:a + n]
        distt = mtile(1024).rearrange("p (k q) -> p k q", q=256)
        indt = mtile(512).rearrange("p (k q) -> p k q", q=256)
        tmpS = mtile(512).rearrange("p (k q) -> p k q", q=256)
        maskown = mtile(256).rearrange("p (b n) -> p b n", n=32)
        farmask = mtile(256).rearrange("p (b n) -> p b n", n=32)
        relb = mtile(256).rearrange("p (r h) -> p r h", h=8)
        dbc = mtile(256).rearrange("p (r h) -> p r h", h=8)
        sc = mtile(32)
        m8 = mtile(16).rearrange("p (t e) -> p t e", e=8)
        tsel = mtile(32)
        selb = mtile(32)
        selbT = mtile(256)
        rsum = mtile(256)
        ksum = mtile(16)
        P.dma("sp", lambda e: e.dma_start(out=distt, in_=c_dist.rearrange("p (k q) -> p k q", q=256)), writes=("dist",))
        P.dma("sp", lambda e: e.dma_start(out=maskown, in_=c_maskown.rearrange("p (b n) -> p b n", n=32)), writes=("maskown",))
        P.dma("sp", lambda e: e.dma_start(out=farmask, in_=c_farmask.rearrange("p (b n) -> p b n", n=32)), writes=("farmask",))
        P.dma("sp", lambda e: e.dma_start(out=relb, in_=c_relb.rearrange("p (r h) -> p r h", h=8)), writes=("relb",))
        P.dma("sp", lambda e: e.dma_start(out=esel[0:16, :], in_=c_esel[:, :]), writes=("esel",))
        P.op("dve", lambda e: e.tensor_copy(eselb[0:16, :], esel[0:16, :]), reads=("esel",), writes=("eselb",))
        selbTb = aTf[:, 14592:14848]
        P.op("dve", lambda e: e.tensor_copy(dbc[:, 0:1, :], relb[:, 0:1, :]), reads=("relb",), writes=("dbc",))
        P.op("dve", lambda e: e.tensor_sub(dbc[:, 1:32, :], relb[:, 1:32, :], relb[:, 0:31, :]), reads=("relb",), writes=("dbc",))
        thr = t5_thresholds()
        for kc in range(4):
            for h in range(8):
                P.op("dve", lambda e, o=biasT[:, h, kc, :], i=distt[:, kc, :]:
                     e.tensor_scalar(o, i, 0.0, -BIG, op0=ALU.is_lt, op1=ALU.mult),
                     reads=("dist",), writes=(("bias", h, kc),))
            for r in range(32):
                P.op("dve", lambda e, o=indt[:, r % 2, :], i=distt[:, kc, :], th=float(thr[r]):
                     e.tensor_single_scalar(o, i, th, op=ALU.is_ge), reads=("dist",), writes=(("ind", r % 2),))
                for h in range(8):
                    P.op("dve", lambda e, o=biasT[:, h, kc, :], i=indt[:, r % 2, :], sc_=dbc[:, r, h:h + 1]:
                         e.scalar_tensor_tensor(o, i, sc_, o, op0=ALU.mult, op1=ALU.add),
                         reads=(("ind", r % 2), "dbc", ("bias", h, kc)), writes=(("bias", h, kc),))
        for h in heads:
            P.dma("sp", lambda e, i=kT_d[h]: e.dma_start(out=KT, in_=i), writes=("KT",))
            for g4 in range(4):
                P.dma("sp", lambda e, o=Vt[:, g4 * 8:(g4 + 1) * 8, :],
                      i=v_d[g4 * 1024:(g4 + 1) * 1024, h * 128:(h + 1) * 128].rearrange("(c p) d -> p c d", p=128):
                      e.dma_start(out=o, in_=i), writes=(("Vt", g4),))
            P.dma("sp", lambda e, i=qT_d[h]: e.dma_start(out=qT, in_=i), writes=("qT",))
            P.op("dve", lambda e: e.reduce_sum(ksum, KT.rearrange("p (b k) -> p b k", k=256), axis=AX.X),
                 reads=("KT",), writes=("ksum",))
            P.op("act", lambda e: e.activation(kmb, ksum, AF.Copy, scale=1.0 / 256), reads=("ksum",), writes=("kmb",))
            for qb in range(8):
                own = 8 + qb
                for t in range(2):
                    P.op("pe", lambda e, o=ps[7][:, t * 16:(t + 1) * 16], l=qT[:, qb * 256 + t * 128: qb * 256 + (t + 1) * 128]:
                         e.matmul(o, l, kmb, start=True, stop=True), reads=("qT", "kmb"), writes=(("ps", 7),))
                P.op("dve", lambda e, m=maskown[:, qb, :]: e.tensor_tensor(sc, ps[7][:, 0:32], m, op=ALU.add),
                     reads=(("ps", 7), "maskown"), writes=("sc",))
                for t in range(2):
                    P.op("dve", lambda e, o=m8[:, t, :], i=sc[:, t * 16:(t + 1) * 16]: e.max(o, i), reads=("sc",), writes=(("m8", t),))
                for t in range(2):
                    P.op("dve", lambda e, o=tsel[:, t * 16:(t + 1) * 16], i=sc[:, t * 16:(t + 1) * 16], th=m8[:, t, 2:3]:
                         e.tensor_scalar(o, i, th, 1.0, op0=ALU.is_ge, op1=ALU.subtract),
                         reads=("sc", ("m8", t)), writes=("tsel",))
                P.op("dve", lambda e, m=maskown[:, qb, :]: e.scalar_tensor_tensor(selb, tsel, BIG, m, op0=ALU.mult, op1=ALU.add),
                     reads=("tsel", "maskown"), writes=("selb",))
                P.op("dve", lambda e, f=farmask[:, qb, :], t31=relb[:, 31, h:h + 1]:
                     e.scalar_tensor_tensor(selb, f, t31, selb, op0=ALU.mult, op1=ALU.add),
                     reads=("selb", "farmask", "relb"), writes=("selb",))
                for t in range(2):
                    P.op("pe", lambda e, o=ps[6][0:16, t * 128:(t + 1) * 128], i=selb[:, t * 16:(t + 1) * 16]:
                         e.transpose(o, i, ident[:, :]), reads=("selb", "ident"), writes=(("ps", 6),))
                P.op("act", lambda e: e.activation(selbTb[0:16, :], ps[6][0:16, 0:256], AF.Copy, scale=1.0 / SCALE),
                     reads=(("ps", 6),), writes=("selbT",))
                nch = 2 * (own + 1)
                po, pr = 2 + 2 * (qb % 2), 3 + 2 * (qb % 2)
                def qk_(ci):
                    n = ci // 2
                    pb = ps[ci % 2]
                    P.op("pe", lambda e, o=pb[:, 0:256], l=KT[:, ci * 128:(ci + 1) * 128], r=qT[:, qb * 256:(qb + 1) * 256],
                         sp_=(n == own): e.matmul(o, l, r, start=True, stop=sp_),
                         reads=("KT", "qT"), writes=(("ps", ci % 2),))
                    if n < own:
                        P.op("pe", lambda e, o=pb[:, 0:256], l=eselb[0:16, n * 128:(n + 1) * 128]:
                             e.matmul(o, l, selbTb[0:16, :], start=False, stop=True),
                             reads=("eselb", "selbT"), writes=(("ps", ci % 2),))
                def ex_(ci):
                    n = ci // 2
                    pb = ps[ci % 2]
                    sl = ci % 8
                    if n >= own - 1:
                        kc = (n - (own - 1)) * 2 + ci % 2
                        P.op("dve", lambda e, o=tmpS[:, ci % 2, :], i=pb[:, 0:256], b=biasT[:, h, kc, :]:
                             e.scalar_tensor_tensor(o, i, SCALE, b, op0=ALU.mult, op1=ALU.add),
                             reads=(("ps", ci % 2), ("bias", h, kc)), writes=(("tmpS", ci % 2),))
                        P.op("act", lambda e, o=pT[:, sl, :], i=tmpS[:, ci % 2, :]: e.activation(o, i, AF.Exp),
                             reads=(("tmpS", ci % 2),), writes=(("pT", sl),))
                    else:
                        P.op("act", lambda e, o=pT[:, sl, :], i=pb[:, 0:256]: e.activation(o, i, AF.Exp, scale=SCALE),
                             reads=(("ps", ci % 2),), writes=(("pT", sl),))
                def pv_(ci):
                    sl = ci % 8
                    P.op("pe", lambda e, o=ps[po][:, 0:256], l=Vt[:, ci, :], r=pT[:, sl, :], st=(ci == 0), sp_=(ci == nch - 1):
                         e.matmul(o, l, r, start=st, stop=sp_), reads=(("Vt", ci // 8), ("pT", sl)), writes=(("ps", po),))
                    P.op("pe", lambda e, o=ps[pr][:, 0:256], r=pT[:, sl, :], st=(ci == 0), sp_=(ci == nch - 1):
                         e.matmul(o, ones_bf[:, :], r, start=st, stop=sp_), reads=("ones", ("pT", sl)), writes=(("ps", pr),))
                qk_(0)
                for ci in range(nch):
                    if ci + 1 < nch:
                        qk_(ci + 1)
                    ex_(ci)
                    pv_(ci)
                P.op("dve", lambda e, i=ps[pr][:, 0:256]: e.reciprocal(rsum, i), reads=(("ps", pr),), writes=("rsum",))
                P.op("dve", lambda e, o=attT[:, qb * 256:(qb + 1) * 256], i=ps[po][:, 0:256]:
                     e.tensor_tensor(o, i, rsum, op=ALU.mult), reads=(("ps", po), "rsum"), writes=("attT",))
            P.dma("sp", lambda e, o=att_d[h]: e.dma_start(out=o, in_=attT), reads=("attT",), writes=())
        P.barrier()

    if "D" in phases:
        uTf = uT[:, :, :].rearrange("p c t -> p (c t)")
        xtf = xtok[:, :, :].rearrange("p c t -> p (c t)")
        rawq, rawk, rawv = aTf[:, 0:4096], aTf[:, 4096:8192], aTf[:, 8192:12288]
        zT = aTf[:, 12288:14336]
        delT = aTf[:, 14336:16384]
        vsb = aTf[:, 16384:20480]
        qn, kn = xTf[:, 0:4096], xTf[:, 4096:8192]
        cacc = xtf[:, 0:4096]
        w32 = [xtf[:, i * 128:(i + 1) * 128] for i in range(32)]
        Et, Mt, R, gbt, osb, t1s, Stt = w32[0], w32[1], w32[2], w32[3], w32[4], w32[5], w32[6]
        Mp, Np = [w32[7], w32[8]], [w32[9], w32[10]]
        attnf = w32[11]
        b16 = [uTf[:, i * 128:(i + 1) * 128] for i in range(16)]
        attnT, kdec, kbeg, vbeta, Rb, wTn, vnew, Sb, onb = b16[0:9]
        mo = [0]

        def mt2(n):
            a = mo[0]
            mo[0] += n
            assert mo[0] <= 4096
            return misc[:, a:a + n]
        bat = mt2(512).rearrange("p (c n) -> p c n", n=16)
        beta, gt, Gt, Glt, eG, eGlG, cdt, skb = (mt2(256) for _ in range(8))
        tri, negtri, masku, strict, onorm, ones_f = (mt2(128) for _ in range(6))
        convw = mt2(96)
        alog, dtb, negA = mt2(8), mt2(8), mt2(8)
        ss, rs = mt2(1), mt2(1)
        v3 = lambda t: t.rearrange("p (c h) -> p c h", h=8)

        def slot(b, k):
            return ps[b][:, k * 128:(k + 1) * 128]

        def slotb(b, k):
            return ps[b][:, :].bitcast(BF16)[:, k * 256:k * 256 + 128]
        for dst, src, key in ((tri, c_tri, "tri"), (masku, c_masku, "masku"), (strict, c_strict, "strict"),
                              (onorm, c_onorm, "onorm"), (convw, c_conv, "convw"), (alog, c_alog, "alog"), (dtb, c_dtb, "dtb")):
            P.dma("sp", lambda e, o=dst, i=src: e.dma_start(out=o, in_=i[:, :]), writes=(key,))
        P.dma("sp", lambda e: e.dma_start(out=bat, in_=ba_d.rearrange("(c p) n -> p c n", p=128)), writes=("bat",))
        P.op("dve", lambda e: e.memset(ones_f, 1.0), writes=("ones_f",))
        P.op("dve", lambda e: e.tensor_scalar_mul(negtri, tri, -1.0), reads=("tri",), writes=("negtri",))
        P.op("act", lambda e: e.activation(v3(beta), bat[:, :, 0:8], AF.Sigmoid), reads=("bat",), writes=("beta",))
        for h in range(8):
            P.op("dve", lambda e, o=v3(gt)[:, :, h], i=bat[:, :, 8 + h], sc_=dtb[:, h:h + 1]: e.tensor_scalar_add(o, i, sc_),
                 reads=("bat", "dtb"), writes=("gt",))
        P.op("act", lambda e: e.activation(gt, gt, AF.Exp), reads=("gt",), writes=("gt",))
        P.op("dve", lambda e: e.tensor_scalar_add(gt, gt, 1.0), reads=("gt",), writes=("gt",))
        P.op("act", lambda e: e.activation(gt, gt, AF.Ln), reads=("gt",), writes=("gt",))
        P.op("act", lambda e: e.activation(negA, alog, AF.Exp), reads=("alog",), writes=("negA",))
        P.op("dve", lambda e: e.tensor_scalar_mul(negA, negA, -1.0), reads=("negA",), writes=("negA",))
        for h in range(8):
            P.op("dve", lambda e, o=v3(gt)[:, :, h], sc_=negA[:, h:h + 1]: e.tensor_scalar_mul(o, o, sc_),
                 reads=("gt", "negA"), writes=("gt",))
        P.op("pe", lambda e: e.matmul(ps[5][:, 0:256], tri, gt, start=True, stop=True), reads=("tri", "gt"), writes=(("ps", 5),))
        P.op("pe", lambda e: e.matmul(ps[5][:, 256:512], ones_f, gt, start=True, stop=True), reads=("ones_f", "gt"), writes=(("ps", 5),))
        P.op("dve", lambda e: e.tensor_copy(Gt, ps[5][:, 0:256]), reads=(("ps", 5),), writes=("Gt",))
        P.op("dve", lambda e: e.tensor_copy(Glt, ps[5][:, 256:512]), reads=(("ps", 5),), writes=("Glt",))
        P.op("act", lambda e: e.activation(eG, Gt, AF.Exp), reads=("Gt",), writes=("eG",))
        P.op("act", lambda e: e.activation(cdt, Glt, AF.Exp), reads=("Glt",), writes=("cdt",))
        P.op("dve", lambda e: e.tensor_sub(eGlG, Glt, Gt), reads=("Glt", "Gt"), writes=("eGlG",))
        P.op("act", lambda e: e.activation(eGlG, eGlG, AF.Exp), reads=("eGlG",), writes=("eGlG",))
        P.op("dve", lambda e: e.tensor_mul(skb, beta, eG), reads=("beta", "eG"), writes=("skb",))
        P.marks.append(("D-setup-end", P.total))
        P.barrier()
        SHARED = {"tri", "negtri", "masku", "strict", "onorm", "ones_f", "ones", "ident", "identb", "convw", "gt", "beta",
                  "eG", "eGlG", "cdt", "skb", "Gt", "Glt", "rstd", "cacc", "raw"}
        wrf = wring[:, :, :].rearrange("p s n -> p (s n)").bitcast(F32)
        rawb = uTf[:, 0:4096]
        LT = []
        for l_ in range(2):
            qn_l, kn_l = (xTf[:, 0:4096], xTf[:, 4096:8192]) if l_ == 0 else (wrf[:, 0:4096], wrf[:, 4096:8192])
            wt = [xtf[:, (l_ * 12 + i) * 128:(l_ * 12 + i + 1) * 128] for i in range(12)]
            bt = [uTf[:, 4096 + (l_ * 9 + i) * 128: 4096 + (l_ * 9 + i + 1) * 128] for i in range(9)]
            LT.append(dict(zT=aTf[:, 8192 + l_ * 2048: 8192 + (l_ + 1) * 2048], delT=aTf[:, 12288 + l_ * 2048: 12288 + (l_ + 1) * 2048],
                           vsb=aTf[:, l_ * 4096:(l_ + 1) * 4096], qn=qn_l, kn=kn_l, wt=wt, bt=bt, ss=mt2(1), rs=mt2(1)))

        def lane_ops(l):
            def lk(k):
                if k in SHARED or (isinstance(k, tuple) and k[0] in ("ps", "sq")):
                    return k
                return ("L", l, k)

            def op(eng, fn, reads=(), writes=()):
                P.op(eng, fn, tuple(lk(k) for k in reads), tuple(lk(k) for k in writes))

            def dma(q, fn, reads=(), writes=()):
                P.dma(q, fn, tuple(lk(k) for k in reads), tuple(lk(k) for k in writes))
            return op, dma

        def unpack(l):
            T = LT[l]
            wt, bt = T["wt"], T["bt"]
            return (T["zT"], T["delT"], T["vsb"], T["qn"], T["kn"], wt[0], wt[1], wt[2], wt[3], wt[4], wt[5], wt[6],
                    [wt[7], wt[8]], [wt[9], wt[10]], wt[11], bt[0], bt[1], bt[2], bt[3], bt[4], bt[5], bt[6], bt[7], bt[8], T["ss"], T["rs"])

        def prologue(h, l):
            op, dma = lane_ops(l)
            (zT, delT, vsb, qn, kn, Et, Mt, R, gbt, osb, t1s, Stt, Mp, Np, attnf,
             attnT, kdec, kbeg, vbeta, Rb, wTn, vnew, Sb, onb, ss, rs) = unpack(l)
            dma("sp", lambda e, i=z_d[h]: e.dma_start(out=zT, in_=i), writes=("zT",))
            for part, (rsrc, key) in enumerate(((gq_d, "raw"), (gk_d, "raw"), (gv_d, "raw"))):
                raw = rawb
                dma("sp", lambda e, i=rsrc[h]: e.dma_start(out=rawb, in_=i), writes=("raw",))
                cw = lambda j: convw[:, (part * 8 + h) * 4 + j:(part * 8 + h) * 4 + j + 1]
                op("dve", lambda e, r=raw, w=cw(3): e.tensor_scalar_mul(cacc, r, w), reads=(key, "convw"), writes=("cacc",))
                for sh in (1, 2, 3):
                    op("dve", lambda e, r=raw, w=cw(3 - sh), sh=sh:
                         e.scalar_tensor_tensor(cacc[:, sh:4096], r[:, 0:4096 - sh], w, cacc[:, sh:4096], op0=ALU.mult, op1=ALU.add),
                         reads=(key, "convw", "cacc"), writes=("cacc",))
                if part == 2:
                    op("act", lambda e: e.activation(vsb, cacc, AF.Silu), reads=("cacc",), writes=("vsb",))
                    continue
                dstn, dkey = (qn, "qn") if part == 0 else (kn, "kn")
                op("act", lambda e, o=dstn: e.activation(o, cacc, AF.Silu), reads=("cacc",), writes=(dkey,))
                for blk in range(8):
                    bs = slice(blk * 512, (blk + 1) * 512)
                    op("act", lambda e, i=dstn[:, bs], o=sq[:, blk % 2, :]: e.activation(o, i, AF.Square),
                         reads=(dkey,), writes=(("sq", blk % 2),))
                    pb = 6 + blk % 2
                    op("pe", lambda e, o=ps[pb][:, :], r=sq[:, blk % 2, :]: e.matmul(o, ones_bf[:, :], r, start=True, stop=True),
                         reads=(("sq", blk % 2), "ones"), writes=(("ps", pb),))
                    op("dve", lambda e, i=ps[pb][:, :]: e.tensor_scalar_add(rstd[:, :], i, EPS), reads=(("ps", pb),), writes=("rstd",))
                    op("act", lambda e: e.activation(rstd[:, :], rstd[:, :], AF.Sqrt), reads=("rstd",), writes=("rstd",))
                    op("dve", lambda e: e.reciprocal(rstd[:, :], rstd[:, :]), reads=("rstd",), writes=("rstd",))
                    op("dve", lambda e, o=dstn[:, bs], scl=(SCALE if part == 0 else 1.0):
                         e.scalar_tensor_tensor(o, o, scl, rstd[:, :], op0=ALU.mult, op1=ALU.mult),
                         reads=(dkey, "rstd"), writes=(dkey,))

        def chunk_body(h, c, l):
            op, dma = lane_ops(l)
            (zT, delT, vsb, qn, kn, Et, Mt, R, gbt, osb, t1s, Stt, Mp, Np, attnf,
             attnT, kdec, kbeg, vbeta, Rb, wTn, vnew, Sb, onb, ss, rs) = unpack(l)
            own = c >= 16
            col = c * 8 + h
            cs = slice(c * 128, (c + 1) * 128)
            colap = lambda t: t[:, col:col + 1]
            op("act", lambda e, g_=colap(gt): e.activation(gbt, ones_f, AF.Copy, scale=g_), reads=("gt", "ones_f"), writes=("gb",))
            yield
            op("pe", lambda e: e.matmul(slot(0, 0), tri, gbt, start=True, stop=False), reads=("tri", "gb"), writes=(("ps", 0),))
            op("pe", lambda e: e.matmul(slot(0, 0), gbt, negtri, start=False, stop=False), reads=("negtri", "gb"), writes=(("ps", 0),))
            op("pe", lambda e: e.matmul(slot(0, 0), ident[:, :], masku, start=False, stop=True), reads=("ident", "masku"), writes=(("ps", 0),))
            op("act", lambda e: e.activation(Et, slot(0, 0), AF.Exp), reads=(("ps", 0),), writes=("E",))
            if c in (0, 16): P.marks.append(("c%d-E" % c, P.total))
            yield
            op("pe", lambda e, k_=kn[:, cs]: e.matmul(slot(1, 0), k_, k_, start=True, stop=True), reads=("kn",), writes=(("ps", 1),))
            op("dve", lambda e, b_=colap(beta): e.scalar_tensor_tensor(Mt, slot(1, 0), b_, Et, op0=ALU.mult, op1=ALU.mult),
                 reads=(("ps", 1), "beta", "E"), writes=("Mt",))
            op("dve", lambda e: e.tensor_mul(Mp[0], Mt, strict), reads=("Mt", "strict"), writes=(("Mp", 0),))
            yield
            op("pe", lambda e: e.transpose(slot(1, 1), Mp[0], ident[:, :]), reads=(("Mp", 0), "ident"), writes=(("ps", 1),))
            op("act", lambda e: e.copy(Np[0], slot(1, 1)), reads=(("ps", 1),), writes=(("Np", 0),))
            op("dve", lambda e: e.scalar_tensor_tensor(R, Np[0], -1.0, ident[:, :], op0=ALU.mult, op1=ALU.add), reads=(("Np", 0), "ident"), writes=("R",))
            if c in (0, 16): P.marks.append(("c%d-preDoubling" % c, P.total))
            for it in range(6):
                cur, nxt = it % 2, 1 - it % 2
                yield
                op("pe", lambda e, o=slot(2, cur), l=Np[cur], r=Mp[cur]: e.matmul(o, l, r, start=True, stop=True),
                     reads=(("Np", cur), ("Mp", cur)), writes=(("ps", 2),))
                op("act", lambda e, o=Mp[nxt], i=slot(2, cur): e.copy(o, i), reads=(("ps", 2),), writes=(("Mp", nxt),))
                if it < 5:
                    yield
                    op("pe", lambda e, o=slot(3, cur), l=Mp[cur], r=Np[cur]: e.matmul(o, l, r, start=True, stop=True),
                         reads=(("Np", cur), ("Mp", cur)), writes=(("ps", 3),))
                    op("dve", lambda e, o=Np[nxt], i=slot(3, cur): e.tensor_copy(o, i), reads=(("ps", 3),), writes=(("Np", nxt),))
                yield
                op("pe", lambda e, o=slot(4, cur), l=Mp[nxt]: e.matmul(o, l, R, start=True, stop=True),
                     reads=(("Mp", nxt), "R"), writes=(("ps", 4),))
                op("dve", lambda e, i=slot(4, cur): e.tensor_add(R, i, R), reads=(("ps", 4), "R"), writes=("R",))
            op("act", lambda e: e.copy(Rb, R), reads=("R",), writes=("Rb",))
            if own:
                yield
                op("pe", lambda e, q_=qn[:, cs], k_=kn[:, cs]: e.matmul(slot(5, 0), q_, k_, start=True, stop=True),
                     reads=("qn", "kn"), writes=(("ps", 5),))
                op("dve", lambda e: e.tensor_mul(attnf, slot(5, 0), Et), reads=(("ps", 5), "E"), writes=("attnf",))
                yield
                op("pe", lambda e: e.transpose(slot(5, 1), attnf, ident[:, :]), reads=("attnf", "ident"), writes=(("ps", 5),))
                op("act", lambda e: e.copy(attnT, slot(5, 1)), reads=(("ps", 5),), writes=("attnT",))
            if c in (0, 16): P.marks.append(("c%d-preKV" % c, P.total))
            yield
            op("pe", lambda e, k_=kn[:, cs]: e.transpose(slot(5, 2), k_, ident[:, :]), reads=("kn", "ident"), writes=(("ps", 5),))
            op("dve", lambda e, s_=colap(eGlG): e.tensor_scalar_mul(kdec, slot(5, 2), s_), reads=(("ps", 5), "eGlG"), writes=("kdec",))
            op("dve", lambda e, s_=colap(skb): e.tensor_scalar_mul(kbeg, slot(5, 2), s_), reads=(("ps", 5), "skb"), writes=("kbeg",))
            yield
            op("pe", lambda e, v_=vsb[:, cs]: e.transpose(slotb(5, 3), v_, identb), reads=("vsb", "identb"), writes=(("ps", 5),))
            op("dve", lambda e, s_=colap(beta): e.tensor_scalar_mul(vbeta, slotb(5, 3), s_), reads=(("ps", 5), "beta"), writes=("vbeta",))
            if c in (0, 16): P.marks.append(("c%d-preW" % c, P.total))
            yield
            op("pe", lambda e: e.matmul(slot(6, 0), kbeg, Rb, start=True, stop=True), reads=("kbeg", "Rb"), writes=(("ps", 6),))
            op("act", lambda e: e.activation(wTn, slot(6, 0), AF.Copy, scale=-1.0), reads=(("ps", 6),), writes=("wTn",))
            yield
            op("pe", lambda e: e.matmul(slot(6, 1), Rb, vbeta, start=True, stop=False), reads=("Rb", "vbeta"), writes=(("ps", 6),))
            op("pe", lambda e: e.matmul(slot(6, 1), wTn, Sb, start=False, stop=True), reads=("wTn", "Sb"), writes=(("ps", 6),))
            op("dve", lambda e: e.tensor_copy(vnew, slot(6, 1)), reads=(("ps", 6),), writes=("vnew",))
            if own:
                yield
                op("pe", lambda e, q_=qn[:, cs]: e.matmul(slot(6, 2), q_, Stt, start=True, stop=True), reads=("qn", "S"), writes=(("ps", 6),))
                op("pe", lambda e: e.matmul(slot(6, 3), attnT, vnew, start=True, stop=True), reads=("attnT", "vnew"), writes=(("ps", 6),))
                op("dve", lambda e, s_=colap(eG): e.tensor_scalar_mul(t1s, slot(6, 2), s_), reads=(("ps", 6), "eG"), writes=("t1",))
                op("dve", lambda e: e.tensor_add(osb, slot(6, 3), t1s), reads=("t1", ("ps", 6)), writes=("osb",))
            if c in (0, 16): P.marks.append(("c%d-preState" % c, P.total))
            yield
            op("pe", lambda e: e.matmul(slot(7, 0), kdec, vnew, start=True, stop=True), reads=("kdec", "vnew"), writes=(("ps", 7),))
            op("dve", lambda e, s_=colap(cdt): e.tensor_scalar_mul(Stt, Stt, s_), reads=("S", "cdt"), writes=("S",))
            op("dve", lambda e: e.tensor_add(Stt, slot(7, 0), Stt), reads=("S", ("ps", 7)), writes=("S",))
            op("act", lambda e: e.copy(Sb, Stt), reads=("S",), writes=("Sb",))
            if c in (0, 16): P.marks.append(("c%d-preOut" % c, P.total))
            if own:
                oc = slice((c - 16) * 128, (c - 15) * 128)
                op("dve", lambda e: e.memset(ss, 0.0), writes=("ss",))
                op("act", lambda e: e.activation(t1s, osb, AF.Square, accum_out=ss), reads=("osb", "ss"), writes=("t1", "ss"))
                op("dve", lambda e: e.tensor_scalar(rs, ss, 1.0 / 128, EPS, op0=ALU.mult, op1=ALU.add), reads=("ss",), writes=("rs",))
                op("act", lambda e: e.activation(rs, rs, AF.Sqrt), reads=("rs",), writes=("rs",))
                op("dve", lambda e: e.reciprocal(rs, rs), reads=("rs",), writes=("rs",))
                op("dve", lambda e: e.scalar_tensor_tensor(onb, osb, rs, onorm, op0=ALU.mult, op1=ALU.mult),
                     reads=("osb", "rs", "onorm"), writes=("onb",))
                yield
                op("pe", lambda e: e.transpose(slotb(7, 2), onb, identb), reads=("onb", "identb"), writes=(("ps", 7),))
                op("dve", lambda e, o=delT[:, oc], z_=zT[:, oc]: e.tensor_mul(o, slotb(7, 2), z_), reads=(("ps", 7), "zT"), writes=("delT",))

        for h0 in range(0, len(heads), 2):
            hp = heads[h0:h0 + 2]
            for l, h in enumerate(hp):
                prologue(h, l)
            P.barrier()
            for l, h in enumerate(hp):
                op, dma = lane_ops(l)
                T = LT[l]
                op("dve", lambda e, t=T["wt"][6]: e.memset(t, 0.0), writes=("S",))
                op("dve", lambda e, t=T["bt"][7]: e.memset(t, 0.0), writes=("Sb",))
            for c in range(32):
                gens = [chunk_body(h, c, l) for l, h in enumerate(hp)]
                while gens:
                    for g_ in list(gens):
                        try:
                            next(g_)
                        except StopIteration:
                            gens.remove(g_)
            for l, h in enumerate(hp):
                op, dma = lane_ops(l)
                dma("sp", lambda e, o=del_d[h], i=LT[l]["delT"]: e.dma_start(out=o, in_=i), reads=("delT",), writes=())
            P.barrier()

    if "E" in phases:
        KmT = kvm[:, 0:1024].rearrange("p (h m) -> p h m", m=256)
        Vm = kvm[:, 1024:2048].rearrange("p (s n) -> p s n", n=512)
        attA, delA = aT[:, 0:8, :], aT[:, 8:16, :]
        qxT, oxT = aT[:, 16:20, :], aT[:, 20:24, :]
        rr = misc[:, 0:512]
        for gi, gsrc in ((2, g_cross), (3, g_mem), (4, g_ffn2), (5, g_final)):
            P.dma("sp", lambda e, o=gains[:, gi, :], i=gsrc: e.dma_start(out=o, in_=i[:, :]), writes=("gains",))
        load_xT(0, src=memb, nsub=2)
        rms_to_uT(3, nt=256)
        for hb in range(2):
            s_, view = load_wcols(wkv, hb * 256, 256)
            for j in range(2):
                hd = hb * 2 + j
                for c in range(DC):
                    P.op("pe", lambda e, o=ps[j][:, 0:256], l=view[:, c, j * 128:(j + 1) * 128], r=uT[:, c, 0:256],
                         st=(c == 0), sp_=(c == DC - 1): e.matmul(o, l, r, start=st, stop=sp_),
                         reads=(("w", s_, c // 8), ("uT", c)), writes=(("ps", j),))
                evac_copy(KmT[:, hd, :], ps[j][:, 0:256], reads=(("ps", j),), writes=("kvm",))
        for blk in range(2):
            s_, view = load_wcols(wkv, 512 + blk * 256, 256)
            for sub in range(2):
                for c in range(DC):
                    P.op("pe", lambda e, o=ps[2 + sub][:, 0:256], l=uT[:, c, sub * 128:(sub + 1) * 128], r=view[:, c, :],
                         st=(c == 0), sp_=(c == DC - 1): e.matmul(o, l, r, start=st, stop=sp_),
                         reads=(("w", s_, c // 8), ("uT", c)), writes=(("ps", 2 + sub),))
                evac_copy(Vm[:, sub, blk * 256:(blk + 1) * 256], ps[2 + sub][:, 0:256], reads=(("ps", 2 + sub),), writes=("kvm",))
        for t in range(NOWN // TT):
            if t + (NTOK - NOWN) // TT not in tiles:
                continue
            ts_ = slice(t * TT, (t + 1) * TT)
            P.dma("sp", lambda e, i=att_d[:, :, ts_].rearrange("h p t -> p h t"): e.dma_start(out=attA, in_=i),
                  writes=tuple(("aT", f) for f in range(8)))
            P.dma("sp", lambda e, i=del_d[:, :, ts_].rearrange("h p t -> p h t"): e.dma_start(out=delA, in_=i),
                  writes=tuple(("aT", f) for f in range(8, 16)))
            P.dma("sp", lambda e, i=h1_d[:, :, ts_]: e.dma_start(out=xT[:, :, :], in_=i), writes=tuple(("xT", c) for c in range(DC)))
            for pair in range(8):
                sa, va = load_wcols(wbra, pair * 256, 256, nrc=8)
                sd, vd = load_wcols(wbrd, pair * 256, 256, nrc=8)
                for j in range(2):
                    c = pair * 2 + j
                    P.dma("sp", lambda e, o=ostage[:, 0 + j, :], i=ga_d[c][:, ts_]: e.dma_start(out=o, in_=i), writes=(("ost", 0 + j),))
                    P.dma("sp", lambda e, o=ostage[:, 2 + j, :], i=gb_d[c][:, ts_]: e.dma_start(out=o, in_=i), writes=(("ost", 2 + j),))
                    for hh in range(8):
                        P.op("pe", lambda e, o=ps[j][:, :], l=va[:, hh, j * 128:(j + 1) * 128], r=attA[:, hh, :],
                             st=(hh == 0), sp_=(hh == 7): e.matmul(o, l, r, start=st, stop=sp_),
                             reads=(("w", sa, 0), ("aT", hh)), writes=(("ps", j),))
                    for hh in range(8):
                        P.op("pe", lambda e, o=ps[2 + j][:, :], l=vd[:, hh, j * 128:(j + 1) * 128], r=delA[:, hh, :],
                             st=(hh == 0), sp_=(hh == 7): e.matmul(o, l, r, start=st, stop=sp_),
                             reads=(("w", sd, 0), ("aT", 8 + hh)), writes=(("ps", 2 + j),))
                    P.op("dve", lambda e, o=sg[:, 0, :], i=ps[j][:, :], g_=ostage[:, 0 + j, :]: e.tensor_tensor(o, i, g_, op=ALU.mult),
                         reads=(("ps", j), ("ost", 0 + j)), writes=(("sg", 0),))
                    P.op("dve", lambda e, o=sg[:, 1, :], i=ps[2 + j][:, :], g_=ostage[:, 2 + j, :]: e.tensor_tensor(o, i, g_, op=ALU.mult),
                         reads=(("ps", 2 + j), ("ost", 2 + j)), writes=(("sg", 1),))
                    P.op("pool", lambda e, o=uT[:, c, :]: e.tensor_tensor(o, sg[:, 0, :], sg[:, 1, :], op=ALU.add),
                         reads=(("sg", 0), ("sg", 1)), writes=(("uT", c),))

            def proj_add(w2d, nrc, rhs_of, rkeys):
                for pair in range(8):
                    s_, view = load_wcols(w2d, pair * 256, 256, nrc=nrc)
                    for j in range(2):
                        c = pair * 2 + j
                        pb = 4 + c % 2
                        for k in range(nrc):
                            P.op("pe", lambda e, o=ps[pb][:, :], l=view[:, k, j * 128:(j + 1) * 128], r=rhs_of(k),
                                 st=(k == 0), sp_=(k == nrc - 1): e.matmul(o, l, r, start=st, stop=sp_),
                                 reads=(("w", s_, (k * 256) // 2048), rkeys(k)), writes=(("ps", pb),))
                        P.op("dve", lambda e, o=xT[:, c, :], i=ps[pb][:, :]: e.tensor_add(o, i, o),
                             reads=(("ps", pb), ("xT", c)), writes=(("xT", c),))
            proj_add(wout, DC, lambda k: uT[:, k, :], lambda k: ("uT", k))
            rms_to_uT(2)
            for pair in range(2):
                s_, view = load_wcols(wq, pair * 256, 256)
                for j in range(2):
                    hd = pair * 2 + j
                    for c in range(DC):
                        P.op("pe", lambda e, o=ps[j][:, :], l=view[:, c, j * 128:(j + 1) * 128], r=uT[:, c, :],
                             st=(c == 0), sp_=(c == DC - 1): e.matmul(o, l, r, start=st, stop=sp_),
                             reads=(("w", s_, c // 8), ("uT", c)), writes=(("ps", j),))
                    evac_copy(qxT[:, hd, :], ps[j][:, :], reads=(("ps", j),), writes=(("aT", 16 + hd),))
            for hd in range(4):
                for mc in range(2):
                    P.op("pe", lambda e, o=ps[mc][:, :], l=KmT[:, hd, mc * 128:(mc + 1) * 128], r=qxT[:, hd, :]:
                         e.matmul(o, l, r, start=True, stop=True), reads=("kvm", ("aT", 16 + hd)), writes=(("ps", mc),))
                    P.op("act", lambda e, o=aT[:, 24 + mc, :], i=ps[mc][:, :]: e.activation(o, i, AF.Exp, scale=SCALE),
                         reads=(("ps", mc),), writes=(("aT", 24 + mc),))
                    P.op("pe", lambda e, l=Vm[:, mc, hd * 128:(hd + 1) * 128], r=aT[:, 24 + mc, :], st=(mc == 0), sp_=(mc == 1):
                         e.matmul(ps[2][:, :], l, r, start=st, stop=sp_), reads=("kvm", ("aT", 24 + mc)), writes=(("ps", 2),))
                    P.op("pe", lambda e, r=aT[:, 24 + mc, :], st=(mc == 0), sp_=(mc == 1):
                         e.matmul(ps[3][:, :], ones_bf[:, :], r, start=st, stop=sp_), reads=("ones", ("aT", 24 + mc)), writes=(("ps", 3),))
                P.op("dve", lambda e: e.reciprocal(rr, ps[3][:, :]), reads=(("ps", 3),), writes=("rr",))
                P.op("dve", lambda e, o=oxT[:, hd, :]: e.tensor_tensor(o, ps[2][:, :], rr, op=ALU.mult),
                     reads=(("ps", 2), "rr"), writes=(("aT", 20 + hd),))
            proj_add(wo, 4, lambda k: oxT[:, k, :], lambda k: ("aT", 20 + k))
            rms_to_uT(4)
            ffn(w2g, w2u, w2d)
            rms_to_uT(5, inplace=True)
            for sub in range(4):
                for cg in range(4):
                    pb = ps[4 + cg % 2]
                    for j in range(4):
                        c = cg * 4 + j
                        P.op("pe", lambda e, o=pb[:, j * 128:(j + 1) * 128], i=xT[:, c, sub * 128:(sub + 1) * 128]:
                             e.transpose(o, i, ident[:, :]), reads=(("xT", c), "ident"), writes=(("ps", 4 + cg % 2),))
                    evac_copy(xtok[:, sub % 2, cg * 512:(cg + 1) * 512], pb[:, :], reads=(("ps", 4 + cg % 2),), writes=(("xtok", sub % 2),))
                P.dma("sp", lambda e, o=out[t * TT + sub * 128: t * TT + (sub + 1) * 128, :], i=xtok[:, sub % 2, :]:
                      e.dma_start(out=o, in_=i), reads=(("xtok", sub % 2),), writes=())
        P.barrier()

    if dbg:
        for nm, src in (("d_h1", h1_d), ("d_kT", kT_d), ("d_v", v_d), ("d_ba", ba_d), ("d_ga", ga_d)):
            P.dma("sp", lambda e, o=dbg_out[nm], i=src: e.dma_start(out=o, in_=i), writes=())
        P.barrier()

    if _os.environ.get("OPMARKS"):
        print("MARKS", P.marks, "total", P.total)
    P.replay(nc, sems)
    es.close()
    return nc


def _bf(a):
    return np.asarray(a, dtype=np.float32).astype(ml_dtypes.bfloat16)


def _gain_layout(g):
    return np.ascontiguousarray(np.asarray(g, np.float32).reshape(DC, 128).T)


def make_in_maps(inputs):
    x = np.asarray(inputs["x"], np.float32)
    shared = {
        "ffn1_w_gate": np.asarray(inputs["ffn1_w_gate"], np.float32)[0],
        "ffn1_w_up": np.asarray(inputs["ffn1_w_up"], np.float32)[0],
        "ffn1_w_down": np.asarray(inputs["ffn1_w_down"], np.float32)[0],
        "w_in": np.asarray(inputs["w_in"], np.float32)[0],
        "g_ffn1": _gain_layout(inputs["ffn1_norm"][0]),
        "g_mix": _gain_layout(inputs["mix_norm"][0]),
        "g_cross": _gain_layout(inputs["cross_norm"][0]),
        "g_mem": _gain_layout(inputs["mem_norm"][0]),
        "g_ffn2": _gain_layout(inputs["ffn2_norm"][0]),
        "g_final": _gain_layout(inputs["final_norm"]),
        "w_branch_attn": np.asarray(inputs["w_branch_attn"], np.float32)[0],
        "w_branch_delta": np.asarray(inputs["w_branch_delta"], np.float32)[0],
        "w_out": np.asarray(inputs["w_out"], np.float32)[0],
        "cross_wq": np.asarray(inputs["cross_wq"], np.float32)[0],
        "cross_wkv": np.asarray(inputs["cross_wkv"], np.float32)[0],
        "cross_wo": np.asarray(inputs["cross_wo"], np.float32)[0],
        "ffn2_w_gate": np.asarray(inputs["ffn2_w_gate"], np.float32)[0],
        "ffn2_w_up": np.asarray(inputs["ffn2_w_up"], np.float32)[0],
        "ffn2_w_down": np.asarray(inputs["ffn2_w_down"], np.float32)[0],
        "c_ident": np.eye(128, dtype=np.float32),
        "c_relb": np.ascontiguousarray(np.broadcast_to(np.asarray(inputs["rel_bias"], np.float32).reshape(1, 256), (128, 256))),
    }
    shared.update(gdn_consts(inputs))
    maps = []
    for c in range(8):
        b, s = c // 2, c % 2
        xin = np.zeros((NTOK, D), np.float32)
        if s == 1:
            xin[:] = x[b]
        else:
            xin[NOWN:] = x[b, :NOWN]
        m = dict(shared)
        m["xin"] = xin
        m["memb"] = np.ascontiguousarray(np.asarray(inputs["mem"], np.float32)[b])
        m.update(moba_consts(s))
        maps.append(m)
    return maps


def kernel(**inputs):
    nc = build()
    maps = make_in_maps(inputs)
    res = run_bass_kernel_spmd(nc, maps, core_ids=list(range(8)))
    outp = np.zeros((4, 4096, D), np.float32)
    for c in range(8):
        b, s = c // 2, c % 2
        outp[b, s * NOWN:(s + 1) * NOWN] = res.results[c]["out"]
    return outp
```
